# Optimizing a Trainium2 kernel written in Bass

```python
import math
import jax
import jax.numpy as jnp
from jax import lax
import numpy as np

D_MODEL = 1024
BATCH = 8
SEQ = 2048
DEPTH = 1
DEC_BATCH = 32
DEC_SEQ = 32
PAST_LEN = 4096

CHUNK = 64
N_META = 16
D_CONV = 512
CONV_WIDTH = 31
N_HEADS = 8
N_KV_HEADS = 2
HEAD_DIM = 64
D_ATTN = N_HEADS * HEAD_DIM
D_KV = N_KV_HEADS * HEAD_DIM
WINDOW = 128
WIN_CHUNKS = WINDOW // CHUNK
N_BUCKETS = 32
MAX_DISTANCE = 256
D_FF = 2816
FFN_CONV_WIDTH = 3
D_IN = 2 * D_CONV + D_ATTN + 2 * D_KV
EPS = 1e-6
NEG_INF = -1e30

kernel_name = 'hymba_conformer_swa_sink_stream_step'


def rmsnorm(x, g):
    xf = x.astype(jnp.float32)
    r = lax.rsqrt(jnp.mean(xf * xf, axis=-1, keepdims=True) + EPS)
    return (xf * r).astype(x.dtype) * g


def layernorm(x, g, b):
    xf = x.astype(jnp.float32)
    mu = jnp.mean(xf, axis=-1, keepdims=True)
    var = jnp.mean(jnp.square(xf - mu), axis=-1, keepdims=True)
    return ((xf - mu) * lax.rsqrt(var + EPS)).astype(x.dtype) * g + b


def depthwise_causal_conv(x_ext, w, b):
    c = x_ext.shape[-1]
    y = lax.conv_general_dilated(x_ext, w[:, None, :], window_strides=(1,), padding='VALID',
                                 dimension_numbers=('NWC', 'WIO', 'NWC'), feature_group_count=c)
    return y + b


def rel_bucket(rel):
    nb = N_BUCKETS // 2
    max_exact = nb // 2
    ret = jnp.where(rel > 0, nb, 0)
    n = jnp.abs(rel)
    nf = jnp.maximum(n, 1).astype(jnp.float32)
    large = max_exact + (jnp.log(nf / max_exact) / math.log(MAX_DISTANCE / max_exact)
                         * (nb - max_exact)).astype(jnp.int32)
    large = jnp.minimum(large, nb - 1)
    return ret + jnp.where(n < max_exact, n, large)


def rel_bias(q_pos, k_pos, table):
    bkt = rel_bucket(k_pos[..., None, :] - q_pos[..., :, None])
    return jnp.moveaxis(table[bkt], -1, -3)


def sink_attention(q, k, v, bias, sinks):
    b_, n_, nq = q.shape[:3]
    g = N_HEADS // N_KV_HEADS
    qg = q.reshape(b_, n_, nq, N_KV_HEADS, g, HEAD_DIM)
    s = jnp.einsum('bnqhgd,bnkhd->bnhgqk', qg, k, preferred_element_type=jnp.float32) * (HEAD_DIM ** -0.5)
    s = s + bias.reshape(n_, N_KV_HEADS, g, nq, -1).astype(jnp.float32)
    sk = sinks.astype(jnp.float32).reshape(1, 1, N_KV_HEADS, g, 1, 1)
    m = jnp.maximum(jnp.max(s, axis=-1, keepdims=True), sk)
    p = jnp.exp(s - m)
    denom = jnp.sum(p, axis=-1, keepdims=True) + jnp.exp(sk - m)
    o = jnp.einsum('bnhgqk,bnkhd->bnqhgd', (p / denom).astype(v.dtype), v)
    return o.reshape(b_, n_, nq, D_ATTN)


def project(h, w_in):
    z = h @ w_in
    a_in, q, k, v = jnp.split(z, [2 * D_CONV, 2 * D_CONV + D_ATTN, 2 * D_CONV + D_ATTN + D_KV], axis=-1)
    glu = a_in[..., :D_CONV] * jax.nn.sigmoid(a_in[..., D_CONV:])
    sh = z.shape[:-1]
    return (glu, q.reshape(*sh, N_HEADS, HEAD_DIM), k.reshape(*sh, N_KV_HEADS, HEAD_DIM),
            v.reshape(*sh, N_KV_HEADS, HEAD_DIM))


def conv_module(glu_ext, dw_w, dw_b, ln_g, ln_b):
    y = depthwise_causal_conv(glu_ext, dw_w, dw_b)
    return jax.nn.silu(layernorm(y, ln_g, ln_b))


def conv_ffn(x, gate_past, g, w_up, dw_w, dw_b, w_down):
    h = rmsnorm(x, g)
    u = h @ w_up
    gate, up = u[..., :D_FF], u[..., D_FF:]
    gate_ext = jnp.concatenate([gate_past, gate], axis=1)
    y = jax.nn.silu(depthwise_causal_conv(gate_ext, dw_w, dw_b)) * up
    return x + y @ w_down, gate_ext[:, -(FFN_CONV_WIDTH - 1):]


def prompt_attention(q, k, v, table, sinks):
    bsz, t = q.shape[:2]
    nc = (t - N_META) // CHUNK
    nband = (WIN_CHUNKS + 1) * CHUNK

    def band(x):
        xc = x[:, N_META:].reshape(bsz, nc, CHUNK, N_KV_HEADS, HEAD_DIM)
        xpad = jnp.pad(xc, ((0, 0), (WIN_CHUNKS, 0), (0, 0), (0, 0), (0, 0)))
        xb = jnp.concatenate([xpad[:, j:j + nc] for j in range(WIN_CHUNKS + 1)], axis=2)
        xm = jnp.broadcast_to(x[:, None, :N_META], (bsz, nc, N_META, N_KV_HEADS, HEAD_DIM))
        return jnp.concatenate([xm, xb], axis=2)

    kb, vb = band(k), band(v)
    meta_pos = jnp.arange(N_META, dtype=jnp.int32)
    cidx = jnp.arange(nc, dtype=jnp.int32)[:, None]
    q_pos = N_META + cidx * CHUNK + jnp.arange(CHUNK, dtype=jnp.int32)[None]
    band_pos = N_META + (cidx - WIN_CHUNKS) * CHUNK + jnp.arange(nband, dtype=jnp.int32)[None]
    k_pos = jnp.concatenate([jnp.broadcast_to(meta_pos[None], (nc, N_META)), band_pos], axis=1)
    valid = jnp.concatenate([jnp.ones((nc, N_META), dtype=bool), band_pos >= N_META], axis=1)
    bias = jnp.where(valid[:, None, None, :], rel_bias(q_pos, k_pos, table), NEG_INF)
    qf = q[:, N_META:].reshape(bsz, nc, CHUNK, N_HEADS, HEAD_DIM)
    o_frames = sink_attention(qf, kb, vb, bias, sinks).reshape(bsz, nc * CHUNK, D_ATTN)
    o_meta = sink_attention(q[:, None, :N_META], k[:, None, :N_META], v[:, None, :N_META],
                            rel_bias(meta_pos[None], meta_pos[None], table), sinks)[:, 0]
    return jnp.concatenate([o_meta, o_frames], axis=1)


def sample_attention(q, k, v, k_meta, v_meta, k_win, v_win, table, sinks):
    n = q.shape[1]
    w = k_win.shape[1]
    k_all = jnp.concatenate([k_meta, k_win, k], axis=1)
    v_all = jnp.concatenate([v_meta, v_win, v], axis=1)
    q_pos = N_META + PAST_LEN + jnp.arange(n, dtype=jnp.int32)
    k_pos = jnp.concatenate([jnp.arange(N_META, dtype=jnp.int32),
                             N_META + PAST_LEN - w + jnp.arange(w, dtype=jnp.int32), q_pos])
    bias = rel_bias(q_pos[None], k_pos[None], table)
    o = sink_attention(q[:, None], k_all[:, None], v_all[:, None], bias, sinks)[:, 0]
    return o, k_all[:, -w:], v_all[:, -w:]


def setup_inputs(seed: int = 0) -> dict:
    key = jax.random.key(seed)
    ks = jax.random.split(key, 24)

    def nrm(k, shape, s):
        return jax.random.normal(k, shape, jnp.float32) * s

    win = min(WINDOW, PAST_LEN)
    return {
        'x_prompt': nrm(ks[0], (BATCH, SEQ, D_MODEL), 1.0),
        'x_sample': nrm(ks[1], (DEC_BATCH, DEC_SEQ, D_MODEL), 1.0),
        'cache_k_meta': nrm(ks[2], (DEPTH, DEC_BATCH, N_META, N_KV_HEADS, HEAD_DIM), 1.0),
        'cache_v_meta': nrm(ks[3], (DEPTH, DEC_BATCH, N_META, N_KV_HEADS, HEAD_DIM), 1.0),
        'cache_k_win': nrm(ks[4], (DEPTH, DEC_BATCH, win, N_KV_HEADS, HEAD_DIM), 1.0),
        'cache_v_win': nrm(ks[5], (DEPTH, DEC_BATCH, win, N_KV_HEADS, HEAD_DIM), 1.0),
        'state_conv_a': nrm(ks[6], (DEPTH, DEC_BATCH, CONV_WIDTH - 1, D_CONV), 0.5),
        'state_conv_ffn': nrm(ks[7], (DEPTH, DEC_BATCH, FFN_CONV_WIDTH - 1, D_FF), 0.5),
        'meta_tokens': nrm(ks[8], (N_META, D_MODEL), 1.0),
        'rel_bias_table': nrm(ks[9], (N_BUCKETS, N_HEADS), 0.5),
        'norm_mix': 1.0 + nrm(ks[10], (DEPTH, D_MODEL), 0.05),
        'w_in': nrm(ks[11], (DEPTH, D_MODEL, D_IN), D_MODEL ** -0.5),
        'conv_dw_w': nrm(ks[12], (DEPTH, CONV_WIDTH, D_CONV), CONV_WIDTH ** -0.5),
        'conv_dw_b': nrm(ks[13], (DEPTH, D_CONV), 0.01),
        'conv_ln_g': 1.0 + nrm(ks[14], (DEPTH, D_CONV), 0.05),
        'conv_ln_b': nrm(ks[15], (DEPTH, D_CONV), 0.01),
        'attn_sinks': nrm(ks[16], (DEPTH, N_HEADS), 1.0),
        'w_out': nrm(ks[17], (DEPTH, D_CONV + D_ATTN, D_MODEL), (D_CONV + D_ATTN) ** -0.5),
        'norm_ffn': 1.0 + nrm(ks[18], (DEPTH, D_MODEL), 0.05),
        'w_up': nrm(ks[19], (DEPTH, D_MODEL, 2 * D_FF), D_MODEL ** -0.5),
        'ffn_dw_w': nrm(ks[20], (DEPTH, FFN_CONV_WIDTH, D_FF), FFN_CONV_WIDTH ** -0.5),
        'ffn_dw_b': nrm(ks[21], (DEPTH, D_FF), 0.01),
        'w_down': nrm(ks[22], (DEPTH, D_FF, D_MODEL), D_FF ** -0.5),
        'norm_final': 1.0 + nrm(ks[23], (D_MODEL,), 0.05),
    }


def reference(x_prompt, x_sample, cache_k_meta, cache_v_meta, cache_k_win, cache_v_win,
              state_conv_a, state_conv_ffn, meta_tokens, rel_bias_table, norm_mix, w_in,
              conv_dw_w, conv_dw_b, conv_ln_g, conv_ln_b, attn_sinks, w_out, norm_ffn,
              w_up, ffn_dw_w, ffn_dw_b, w_down, norm_final):
    bsz = x_prompt.shape[0]
    xp = jnp.concatenate([jnp.broadcast_to(meta_tokens[None].astype(x_prompt.dtype), (bsz, N_META, D_MODEL)),
                          x_prompt], axis=1)
    xs = x_sample
    km_p, vm_p, kw_p, vw_p, ca_p, cf_p = [], [], [], [], [], []
    kw_s, vw_s, ca_s, cf_s = [], [], [], []
    for l in range(DEPTH):
        glu, q, k, v = project(rmsnorm(xp, norm_mix[l]), w_in[l])
        glu_ext = jnp.concatenate([jnp.zeros((bsz, CONV_WIDTH - 1, D_CONV), glu.dtype), glu], axis=1)
        a_out = conv_module(glu_ext, conv_dw_w[l], conv_dw_b[l], conv_ln_g[l], conv_ln_b[l])
        attn = prompt_attention(q, k, v, rel_bias_table, attn_sinks[l])
        xp = xp + jnp.concatenate([a_out, attn], axis=-1) @ w_out[l]
        xp, ffn_state = conv_ffn(xp, jnp.zeros((bsz, FFN_CONV_WIDTH - 1, D_FF), xp.dtype), norm_ffn[l],
                                 w_up[l], ffn_dw_w[l], ffn_dw_b[l], w_down[l])
        win = cache_k_win.shape[2]
        km_p.append(k[:, :N_META])
        vm_p.append(v[:, :N_META])
        kw_p.append(k[:, -win:])
        vw_p.append(v[:, -win:])
        ca_p.append(glu_ext[:, -(CONV_WIDTH - 1):])
        cf_p.append(ffn_state)
        glu, q, k, v = project(rmsnorm(xs, norm_mix[l]), w_in[l])
        glu_ext = jnp.concatenate([state_conv_a[l], glu], axis=1)
        a_out = conv_module(glu_ext, conv_dw_w[l], conv_dw_b[l], conv_ln_g[l], conv_ln_b[l])
        attn, k_new_win, v_new_win = sample_attention(q, k, v, cache_k_meta[l], cache_v_meta[l],
                                                      cache_k_win[l], cache_v_win[l], rel_bias_table, attn_sinks[l])
        xs = xs + jnp.concatenate([a_out, attn], axis=-1) @ w_out[l]
        xs, ffn_state = conv_ffn(xs, state_conv_ffn[l], norm_ffn[l], w_up[l], ffn_dw_w[l], ffn_dw_b[l], w_down[l])
        kw_s.append(k_new_win)
        vw_s.append(v_new_win)
        ca_s.append(glu_ext[:, -(CONV_WIDTH - 1):])
        cf_s.append(ffn_state)
    y_prompt = rmsnorm(xp, norm_final)[:, N_META:]
    y_sample = rmsnorm(xs, norm_final)
    return (y_prompt, y_sample,
            jnp.stack(km_p), jnp.stack(vm_p), jnp.stack(kw_p), jnp.stack(vw_p), jnp.stack(ca_p), jnp.stack(cf_p),
            jnp.stack(kw_s), jnp.stack(vw_s), jnp.stack(ca_s), jnp.stack(cf_s))
```

```python
import math
from contextlib import ExitStack

import numpy as np
import concourse.bass as bass
import concourse.mybir as mybir
from concourse.bass_utils import run_bass_kernel_spmd

F32 = mybir.dt.float32
BF16 = mybir.dt.bfloat16
AF = mybir.ActivationFunctionType
ALU = mybir.AluOpType

D = 1024
SEQ = 2048
NMETA = 16
CH = 64
DCONV = 512
CW = 31
DFF = 2816
NJ = DFF // 128
EPS = 1e-6
NT = 256
NSB = 4
SL = 32
NROWS = NMETA + SEQ + NSB * SL
E_OFF = 271
E_LEN = 335


class Sched:
    def __init__(self, nc, sems):
        self.nc = nc
        self.free_sems = list(sems)
        self.prog = {k: [] for k in ('pe', 'act', 'dve', 'pool', 'sp')}
        self.esem = {k: self.free_sems.pop() for k in ('pe', 'act', 'dve', 'pool')}
        self.count = {k: 0 for k in self.esem}
        self.seen = {k: {} for k in self.prog}
        self.lastw = {}
        self.readers = {}
        self.slotsem = {}
        self.slotcnt = {}
        self.semid = {}
        self.cur_tile = -1
        self.wtile = {}

    def _deps(self, reads, writes, eng=None):
        deps = []
        mysem = self.esem.get(eng)
        for r in reads:
            if r in self.lastw:
                deps.append(self.lastw[r])
            if r[:2] in ('ps', 'pt'):
                deps.extend(t for t in self.readers.get(r, []) if t[0] is not mysem)
        for w in writes:
            if w in self.lastw:
                deps.append(self.lastw[w])
            deps.extend(self.readers.get(w, []))
        return deps

    def _emit_waits(self, eng, deps):
        need = {}
        for (sem, val) in deps:
            k = id(sem)
            self.semid[k] = sem
            if self.seen[eng].get(k, 0) >= val:
                continue
            need[k] = max(need.get(k, 0), val)
        for k, val in need.items():
            self.seen[eng][k] = val
            sem = self.semid[k]
            self.prog[eng].append(lambda e, sem=sem, val=val: e.wait_ge(sem, val))

    def _record(self, reads, writes, tok):
        for r in reads:
            assert self.wtile.get(r, -1) <= self.cur_tile or self.cur_tile < 0, (r, self.wtile.get(r), self.cur_tile)
        for w in writes:
            self.wtile[w] = self.cur_tile
        for w in writes:
            self.lastw[w] = tok
            self.readers[w] = []
        for r in reads:
            if r not in writes:
                self.readers.setdefault(r, []).append(tok)

    def op(self, eng, fn, reads=(), writes=()):
        self._emit_waits(eng, self._deps(reads, writes, eng))
        sem = self.esem[eng]
        self.count[eng] += 1
        val = self.count[eng]
        self.prog[eng].append(lambda e, fn=fn, sem=sem: fn(e).then_inc(sem, 1))
        self._record(reads, writes, (sem, val))

    def dma(self, q, slot, fn_list, reads=(), writes=()):
        if slot not in self.slotsem:
            self.slotsem[slot] = self.free_sems.pop()
            self.slotcnt[slot] = 0
        sem = self.slotsem[slot]
        self._emit_waits(q, self._deps(reads, writes))
        for fn in fn_list:
            self.slotcnt[slot] += 16
            self.prog[q].append(lambda e, fn=fn, sem=sem: fn(e).then_inc(sem, 16))
        self._record(reads, writes, (sem, self.slotcnt[slot]))

    def finish(self, eng='sp'):
        deps = [(s, self.slotcnt[k]) for k, s in self.slotsem.items()]
        deps += [(self.esem[k], self.count[k]) for k in self.esem if self.count[k] > 0]
        self._emit_waits(eng, deps)

    def emit(self, block):
        progs = self.prog
        self.prog = {k: [] for k in progs}

        @block.tensor
        def _(e):
            for f in progs['pe']:
                f(e)

        @block.scalar
        def _(e):
            for f in progs['act']:
                f(e)

        @block.vector
        def _(e):
            for f in progs['dve']:
                f(e)

        @block.gpsimd
        def _(e):
            for f in progs['pool']:
                f(e)

        @block.sync
        def _(e):
            for f in progs['sp']:
                f(e)


def build_program(stop=None):
    nc = bass.Bass("TRN2", target_bir_lowering=False)
    import os as _os
    sub = int(_os.environ.get("KSUB", "99"))
    lvl = {None: 9, 'setup0': 0, 'setup': 1, 'meta': 2, 'f1': 3, 'frames': 4, 'p1': 5, 'w2': 6, 't2': 7}[stop]

    def din(name, shape):
        return nc.dram_tensor(name, list(shape), F32, kind="ExternalInput").ap()

    def dout(name, shape):
        return nc.dram_tensor(name, list(shape), F32, kind="ExternalOutput").ap()

    xp = din("xp", [SEQ, D]); xs = din("xs", [NSB * SL, D]); meta = din("meta", [NMETA, D])
    ckm = din("ckm", [NSB, NMETA, 128]); cvm = din("cvm", [NSB, NMETA, 128])
    ckw = din("ckw", [NSB, 128, 128]); cvw = din("cvw", [NSB, 128, 128])
    sca = din("sca", [NSB, CW - 1, DCONV]); scf = din("scf", [NSB, 2, DFF])
    win = din("win", [D, 1792]); wout = din("wout", [D, D]); wup = din("wup", [D, 2 * DFF]); wdn = din("wdn", [DFF, D])
    gmix = din("gmix", [D]); gffn = din("gffn", [D]); gfin = din("gfin", [D])
    cpar = din("cpar", [CW + 3, DCONV])
    fpar = din("fpar", [4, DFF])
    sinks = din("sinks", [8]); tab = din("tab", [32, 8]); eoh = din("eoh", [32, E_LEN])
    yp = dout("yp", [SEQ, D]); ys = dout("ys", [NSB * SL, D])
    kmp = dout("kmp", [NMETA, 128]); vmp = dout("vmp", [NMETA, 128])
    kwp = dout("kwp", [128, 128]); vwp = dout("vwp", [128, 128])
    cap = dout("cap", [CW - 1, DCONV]); cfp = dout("cfp", [2, DFF])
    kws = dout("kws", [NSB, 128, 128]); vws = dout("vws", [NSB, 128, 128])
    cas = dout("cas", [NSB, CW - 1, DCONV]); cfs = dout("cfs", [NSB, 2, DFF])
    x2 = nc.dram_tensor("x2", [NROWS, D], F32).ap()

    with ExitStack() as outer:
        ec = outer.enter_context
        sems = [ec(nc.semaphore(f"s{i}")) for i in range(60)]
        S = Sched(nc, sems)

        def sb(name, shape, dt=F32, ctx=None):
            return (ctx or ec)(nc.sbuf_tensor(name, list(shape), dt))

        NBANK = 8
        psb = [ec(nc.psum_tensor(f"ps{i}", [128, 512], F32)) for i in range(NBANK)]
        rr = {'b': 0}

        def bank():
            i = rr['b']; rr['b'] = (i + 1) % NBANK
            return psb[i], f"ps{i}"

        def tbank():
            ps, pk = bank()
            return ps[:].bitcast(BF16), pk

        identf = sb("identf", [128, 128]); ident = sb("ident", [128, 128], BF16)
        ones_bf = sb("ones_bf", [128, 128], BF16)
        gmix_b = sb("gmix_b", [128, D])
        cparT = sb("cparT", [128, 4, CW + 3])
        fparT = sb("fparT", [128, NJ, 4])
        epsc = sb("epsc", [128, 1]); mhalf = sb("mhalf", [128, 1])
        S.op('pool', lambda e: e.memset(mhalf[:], -0.5), writes=['mhalf'])
        S.op('pool', lambda e: e.memset(identf[:], 0.0), writes=['identf'])
        S.op('pool', lambda e: e.affine_select(out=identf[:], in_=identf[:], pattern=[[-1, 128]], compare_op=ALU.not_equal,
                                               fill=1.0, base=0, channel_multiplier=1), reads=['identf'], writes=['identf'])
        S.op('pool', lambda e: e.tensor_copy(ident[:], identf[:]), reads=['identf'], writes=['ident'])
        S.op('pool', lambda e: e.memset(ones_bf[:], 1.0), writes=['ones_bf'])
        S.dma('sp', 'c_g', [lambda e: e.dma_start(out=gmix_b[:], in_=gmix.partition_broadcast(128))], writes=['gb'])

        def act_copy(out, in_, reads, writes, scale=None):
            if scale is None:
                S.op('act', lambda e: e.activation(out=out, in_=in_, func=AF.Copy), reads=reads, writes=writes)
            else:
                S.op('act', lambda e: e.activation(out=out, in_=in_, func=AF.Copy, scale=scale), reads=reads, writes=writes)

        with ExitStack() as p1:
            pc = p1.enter_context

            def sb1(name, shape, dt=F32):
                return sb(name, shape, dt, ctx=pc)

            Win = sb1("Win", [128, 8, 1792], BF16); Wout = sb1("Wout", [128, 8, D], BF16)
            diag = sb1("diag", [128, 4 * CW, 128], BF16)
            XLQ = _os.environ.get("KXLQ", "pool")
            NXB = 3
            xt = [sb1(f"xt{i}", [128, 2, D]) for i in range(NXB)]
            hb = [sb1(f"hb{i}", [128, D], BF16) for i in range(2)]
            HTC = 128 + NT
            hT = [sb1(f"hT{i}", [128, 8, HTC], BF16) for i in range(2)]
            GLC = 30 + NT
            glb = [sb1(f"glb{i}", [128, 4, GLC], BF16) for i in range(2)]
            sA = sb1("sA", [128, 2, 256]); sBt = sb1("sBt", [128, 2, 256])
            glf = sb1("glf", [128, 4, NSB * SL])
            junk1 = glf[:].rearrange("p j c -> p (j c)")
            sig = [sb1(f"sig{i}", [128, NT]) for i in range(2)]
            cv = sb1("cv", [128, 4, NT]); cvb = sb1("cvb", [128, 4, NT], BF16); sq = sb1("sq", [128, 4, NT], BF16)
            lnm = sb1("lnm", [128, NT]); lnq = sb1("lnq", [128, NT]); lnv = sb1("lnv", [128, NT]); lnr = sb1("lnr", [128, NT])
            tn = [sb1(f"tn{i}", [128, NT]) for i in range(2)]
            qTk = [sb1(f"qT{i}", [128, 4, NT], BF16) for i in range(2)]
            KTl = sb1("KTl", [128, SEQ], BF16)
            KB = sb1("KB", [128, SEQ // CH, 80], BF16)
            KTm = sb1("KTm", [128, NMETA], BF16)
            VA = [sb1(f"VA{i}", [128, 4, 2, 128], BF16) for i in range(2)]
            VB = [sb1(f"VB{i}", [128, 4, 2, 128], BF16) for i in range(2)]
            VM = sb1("VM", [128, 2, 128], BF16)
            catT = sb1("catT", [128, 8, NT], BF16)
            pA = [sb1(f"pA{i}", [128, 2, 256], BF16) for i in range(2)]
            pB = [sb1(f"pB{i}", [128, 2, 256], BF16) for i in range(2)]
            rec = sb1("rec", [128, 2, 256])
            kvst = [sb1("kvst0", [128, 256])] * 2
            ost = sb1("ost", [32, DCONV])
            Eb = ost[:, E_LEN + 1:DCONV].bitcast(BF16)[:, 0:E_LEN]
            rs_ss = sb1("rs_ss", [128, 4]); rs_sd = sb1("rs_sd", [128, 4]); rs_r = sb1("rs_r", [128, 4])
            bA = sb1("bA", [128, 2, 4, 64]); bA1 = sb1("bA1", [64, 2, 4, 64])
            bB = [sb1(f"bB{c}", [80, 2, 4, 64]) for c in range(4)]
            bM = sb1("bM", [16, 2, 4, 16])
            bSA = sb1("bSA", [128, 2, 4, SL]); bSB = sb1("bSB", [48, 2, 4, SL])
            tabf = sb1("tabf", [32, 8]); tabd = sb1("tabd", [32, 8]); tab3 = sb1("tab3", [32, 3, 8], BF16)
            skf = sb1("skf", [1, 8]); ske = sb1("ske", [1, 8]); skd = sb1("skd", [1, 8]); sk3 = sb1("sk3", [1, 3, 8], BF16)
            sk3r = sb1("sk3r", [1, 3, 8, 64], BF16)
            msk = sb1("msk", [1, 2, 128], BF16)
            cparS = rec[0:CW + 3].rearrange("p k n -> p (k n)")
            ckw_b = sb1("ckw_b", [128, NSB, 128], BF16); ckm_b = sb1("ckm_b", [16, NSB, 128], BF16)
            ckw_s = cv[:, 0:2, :].rearrange("p a (b f) -> p (a b) f", b=2)
            cvw_s = cv[:, 2:4, :].rearrange("p a (b f) -> p (a b) f", b=2)
            ckm_s = sA[0:16].rearrange("p a (b f) -> p (a b) f", b=2)
            cvm_s = sBt[:].rearrange("p a (b f) -> p (a b) f", b=2)
            KTsA = sb1("KTsA", [128, NSB, 128], BF16); KBs = sb1("KBs", [128, NSB, 48], BF16)
            VAs = sb1("VAs", [128, NSB, 2, 128], BF16); VBs = sb1("VBs", [128, NSB, 2, 128], BF16)
            block1 = pc(nc.Block())

            S.dma('pool', 'w_in', [lambda e, kc=kc: e.dma_start(out=Win[:, kc, :], in_=win[kc * 128:(kc + 1) * 128, :]) for kc in range(8)], writes=['Win'])
            S.op('dve', lambda e: e.tensor_scalar(out=Win[:, :, 0:DCONV], in0=Win[:, :, 0:DCONV], scalar1=0.5, scalar2=None, op0=ALU.mult), reads=['Win'], writes=['Win'])
            S.dma('pool', 'w_out', [lambda e, kc=kc: e.dma_start(out=Wout[:, kc, :], in_=wout[kc * 128:(kc + 1) * 128, :]) for kc in range(8)], writes=['Wout'])
            S.dma('sp', 'c_par2', [lambda e: e.dma_start(out=cparS, in_=cpar)], writes=['rec0', 'rec1'])
            S.dma('sp', 'c_par', [
                                  lambda e: e.dma_start(out=ost[0:32, 0:E_LEN], in_=eoh), lambda e: e.dma_start(out=tabf[:], in_=tab),
                                  lambda e: e.dma_start(out=skf[:], in_=sinks.rearrange("(o h) -> o h", o=1))], writes=['par', 'ost'])
            S.dma('sp', 'c_cache', [lambda e: e.dma_start(out=ckw_s, in_=ckw.rearrange("b t f -> t b f")),
                                    lambda e: e.dma_start(out=ckm_s, in_=ckm.rearrange("b t f -> t b f")),
                                    lambda e: e.dma_start(out=cvw_s, in_=cvw.rearrange("b t f -> t b f")),
                                    lambda e: e.dma_start(out=cvm_s[32:48, :, :], in_=cvm.rearrange("b t f -> t b f"))], writes=['cv', 'sA', 'sBt'])
            S.dma('sp', 'o_roll', [lambda e: e.dma_start(out=kws[:, 0:96, :], in_=ckw[:, 32:128, :]),
                                   lambda e: e.dma_start(out=vws[:, 0:96, :], in_=cvw[:, 32:128, :])])

            for j in range(4):
                ps, pk = bank()
                S.op('pe', lambda e, ps=ps, j=j: e.transpose(ps[:, 0:CW + 3], cparS[:, j * 128:(j + 1) * 128], identf[0:CW + 3, 0:CW + 3]),
                     reads=['rec0', 'rec1', 'identf'], writes=[pk])
                act_copy(cparT[:, j, :], ps[:, 0:CW + 3], [pk], ['cparT'])
            for j in range(4):
                S.op('dve', lambda e, j=j: e.tensor_tensor(out=diag[:, j * CW:(j + 1) * CW, :],
                                                           in0=identf[:].unsqueeze(1).broadcast_to([128, CW, 128]),
                                                           in1=cparT[:, j, 0:CW].unsqueeze(2).broadcast_to([128, CW, 128]), op=ALU.mult),
                     reads=['identf', 'cparT'], writes=['diag'])
            def split3(dst3, src, tmp, rkey, wkey):
                S.op('dve', lambda e: e.tensor_copy(tmp, src), reads=[rkey], writes=[wkey + 't'])
                for i in range(3):
                    S.op('dve', lambda e, i=i: e.tensor_copy(dst3[:, i, :], tmp), reads=[wkey + 't'], writes=[wkey])
                    if i < 2:
                        S.op('dve', lambda e, i=i: e.tensor_tensor(out=tmp, in0=tmp, in1=dst3[:, i, :], op=ALU.subtract), reads=[wkey + 't', wkey], writes=[wkey + 't'])
            S.op('act', lambda e: e.activation(out=ske[:], in_=skf[:], func=AF.Exp), reads=['par'], writes=['ske'])
            split3(sk3, ske[:], skd[:], 'ske', 'sk3')
            S.op('dve', lambda e: e.tensor_copy(Eb, ost[0:32, 0:E_LEN]), reads=['par', 'ost'], writes=['Eb', 'ost'])
            split3(tab3, tabf[:], tabd[:], 'par', 'tab3')
            S.op('dve', lambda e: e.tensor_copy(sk3r[:], sk3[:].unsqueeze(3).broadcast_to([1, 3, 8, 64])), reads=['sk3'], writes=['sk3r'])
            S.op('pool', lambda e: e.memset(msk[:], 0.0), writes=['msk'])
            S.op('pool', lambda e: e.memset(msk[:, 0, 64:128], 1.0), reads=['msk'], writes=['msk'])
            S.op('pool', lambda e: e.memset(msk[:, 1, 0:64], 1.0), reads=['msk'], writes=['msk'])

            NSPLIT = 2

            def gen_bias(dst, parts, nq):
                ps, pk = bank()
                psv = ps[:, 0:nq * 8].rearrange("p (q h) -> p q h", h=8)

                def f(e):
                    last = None
                    for (r0, M, s0) in parts:
                        for i in range(nq):
                            for t3 in range(NSPLIT):
                                last = e.matmul(psv[r0:r0 + M, i, :], lhsT=Eb[:, s0 - i:s0 - i + M], rhs=tab3[:, t3, :], start=(t3 == 0), stop=(t3 == NSPLIT - 1))
                    return last
                S.op('pe', f, reads=['Eb', 'tab3', 'ost'], writes=[pk])
                rows = max(r0 + M for (r0, M, s0) in parts)
                S.op('dve', lambda e: e.tensor_copy(dst[0:rows].rearrange("p k h q -> p (k h) q"), psv[0:rows].rearrange("p q h -> p h q")),
                     reads=[pk], writes=['bias'])

            gen_bias(bM, [(0, 16, E_OFF)], 16)
            bias_jobs = [lambda: gen_bias(bB[0], [(0, 64, E_OFF), (64, 16, E_OFF - 16)], 64),
                         lambda: gen_bias(bA1, [(0, 64, E_OFF - 64)], 64),
                         lambda: gen_bias(bB[1], [(0, 64, E_OFF), (64, 16, E_OFF - 16 - 64)], 64),
                         lambda: gen_bias(bA, [(0, 128, E_OFF - 128)], 64),
                         lambda: gen_bias(bB[2], [(0, 64, E_OFF), (64, 16, E_OFF - 16 - 128)], 64),
                         lambda: gen_bias(bB[3], [(0, 64, E_OFF), (64, 16, E_OFF - 16 - 192)], 64),
                         lambda: gen_bias(bSA, [(0, 128, E_OFF - 128)], SL),
                         lambda: gen_bias(bSB, [(0, 32, E_OFF), (32, 16, SL)], SL)]

            for t_ in (VA[0], VA[1], VB[0], VB[1], VM, VAs, VBs):
                S.op('pool', lambda e, t_=t_: e.memset(t_[:], 1.0), writes=['Vinit'])

            S.op('dve', lambda e: e.tensor_copy(ckw_b[:], ckw_s), reads=['cv'], writes=['ckw_b'])
            S.op('dve', lambda e: e.tensor_copy(ckm_b[:], ckm_s), reads=['sA'], writes=['ckm_b'])
            for s in range(NSB):
                pt, tk = tbank()
                S.op('pe', lambda e, pt=pt, s=s: e.transpose(pt[:, 0:128], ckw_b[:, s, :], ident[:]), reads=['ckw_b', 'ident'], writes=[tk])
                act_copy(KTsA[:, s, :], pt[:, 0:128], [tk], ['KTsA'])
                pt, tk = tbank()
                S.op('pe', lambda e, pt=pt, s=s: e.transpose(pt[:, 0:16], ckm_b[:, s, :], ident[0:16, 0:16]), reads=['ckm_b', 'ident'], writes=[tk])
                act_copy(KBs[:, s, 32:48], pt[:, 0:16], [tk], ['KBs'])
            S.op('dve', lambda e: e.tensor_copy(VAs[:, :, 0, 0:64], cvw_s[:, :, 0:64]), reads=['cv', 'Vinit'], writes=['VAs'])
            S.op('dve', lambda e: e.tensor_copy(VAs[:, :, 1, 64:128], cvw_s[:, :, 64:128]), reads=['cv', 'VAs'], writes=['VAs'])
            S.op('dve', lambda e: e.tensor_copy(VBs[32:48, :, 0, 0:64], cvm_s[32:48, :, 0:64]), reads=['sBt', 'Vinit'], writes=['VBs'])
            S.op('dve', lambda e: e.tensor_copy(VBs[32:48, :, 1, 64:128], cvm_s[32:48, :, 64:128]), reads=['sBt', 'VBs'], writes=['VBs'])

            def head_a(ti, src_fn, n, gb, keyp):
                xbuf = xt[ti % NXB]
                nb = (n + 127) // 128
                S.dma(XLQ, f'x{keyp}{ti % NXB}', src_fn(xbuf), writes=[f'xt{keyp}{ti % NXB}'])
                xk = f'xt{keyp}{ti % NXB}'
                for b in range(nb):
                    r = min(128, n - b * 128)
                    S.op('act', lambda e, b=b, r=r: e.activation(out=junk1[0:r, :], in_=xbuf[0:r, b, 0:512], func=AF.Square,
                                                                 accum_out=rs_ss[0:r, 2 * b:2 * b + 1]), reads=[xk], writes=['rs_ss', 'glf'])
                    S.op('act', lambda e, b=b, r=r: e.activation(out=junk1[0:r, :], in_=xbuf[0:r, b, 512:1024], func=AF.Square,
                                                                 accum_out=rs_ss[0:r, 2 * b + 1:2 * b + 2]), reads=[xk], writes=['rs_ss', 'glf'])
                    S.op('dve', lambda e, b=b, r=r: e.tensor_tensor(out=rs_sd[0:r, b:b + 1], in0=rs_ss[0:r, 2 * b:2 * b + 1],
                                                                    in1=rs_ss[0:r, 2 * b + 1:2 * b + 2], op=ALU.add), reads=['rs_ss'], writes=['rs_sd'])
                    S.op('dve', lambda e, b=b, r=r: e.tensor_scalar(out=rs_sd[0:r, 2 + b:3 + b], in0=rs_sd[0:r, b:b + 1], scalar1=1.0 / D, scalar2=EPS,
                                                                    op0=ALU.mult, op1=ALU.add), reads=['rs_sd'], writes=['rs_sd2'])
                    S.op('pool', lambda e, b=b, r=r: e.tensor_tensor(out=rs_r[0:r, b:b + 1], in0=rs_sd[0:r, 2 + b:3 + b], in1=mhalf[0:r, :], op=ALU.pow),
                         reads=['rs_sd2', 'mhalf'], writes=['rs_r'])
                    hbb = hb[b % 2]
                    S.op('dve', lambda e, b=b, r=r, hbb=hbb: e.scalar_tensor_tensor(out=hbb[0:r, :], in0=xbuf[0:r, b, :], scalar=rs_r[0:r, b:b + 1],
                                                                                    in1=gb[0:r, :], op0=ALU.mult, op1=ALU.mult),
                         reads=[xk, 'rs_r', 'gb'], writes=[f'hb{b % 2}'])

            def load_norm_transpose(ti, src_fn, n, xbuf, hTb, gb, keyp):
                nb = (n + 127) // 128
                for b in range(nb):
                    r = min(128, n - b * 128)
                    hbb = hb[b % 2]
                    if b > 0:
                        yield
                    pt, tk = tbank()

                    def tf(e, pt=pt, hbb=hbb, r=r):
                        last = None
                        for kc in range(8):
                            last = e.transpose(pt[:, kc * 128:kc * 128 + r], hbb[0:r, kc * 128:(kc + 1) * 128], ident[0:r, 0:r])
                        return last
                    S.op('pe', tf, reads=[f'hb{b % 2}', 'ident'], writes=[tk])
                    act_copy(hTb[:, :, 128 + b * 128:128 + b * 128 + r], pt[:].rearrange("p (k t) -> p k t", k=8)[:, :, 0:r], [tk], [f'hT{keyp}{ti % 2}'])

            S.op('pool', lambda e: e.memset(epsc[:], EPS), writes=['epsc'])

            def attn_A(qcols, nq, blocks, cat_cols, pidx):
                N = 4 * nq
                stiles = [sA, sBt]
                ptiles = [pA[pidx], pB[pidx]]
                pkeys = [f'pA{pidx}', f'pB{pidx}']
                skeys = ['sA', 'sBt']
                for bi, (KTap, M, VEap, bias_ap, rkeys) in enumerate(blocks):
                    ps, pk = bank()

                    def sf(e, ps=ps, KTap=KTap, M=M):
                        last = None
                        for kv in range(2):
                            last = e.matmul(ps[0:M, kv * 256:kv * 256 + N], lhsT=KTap,
                                            rhs=qTk[kv][:, :, qcols:qcols + nq], start=True, stop=True)
                        return last
                    S.op('pe', sf, reads=['qT'] + rkeys, writes=[pk])
                    st = stiles[bi]
                    S.op('dve', lambda e, ps=ps, st=st, M=M, bias_ap=bias_ap: e.tensor_tensor(
                        out=st[0:M, :, 0:N], in0=ps[0:M, :].rearrange("p (k n) -> p k n", k=2)[:, :, 0:N],
                        in1=bias_ap.rearrange("p k h q -> p k (h q)"), op=ALU.add), reads=[pk, 'bias'], writes=[skeys[bi]])
                    S.op('act', lambda e, st=st, M=M, pt_=ptiles[bi]: e.activation(out=pt_[0:M, :, 0:N], in_=st[0:M, :, 0:N], func=AF.Exp),
                         reads=[skeys[bi]], writes=[pkeys[bi]])

            def attn_B(qcols, nq, blocks, cat_cols, pidx):
                N = 4 * nq
                ptiles = [pA[pidx], pB[pidx]]
                pkeys = [f'pA{pidx}', f'pB{pidx}']
                po, pok = bank()

                def of(e, po=po):
                    last = None
                    for kv in range(2):
                        for bi, (KTap, M, VEap, bias_ap, rkeys) in enumerate(blocks):
                            e.matmul(po[:, kv * 256:kv * 256 + N], lhsT=VEap[0:M, kv, :], rhs=ptiles[bi][0:M, kv, 0:N], start=(bi == 0), stop=False)
                        for t3 in range(3):
                            last = e.matmul(po[:, kv * 256:kv * 256 + N], lhsT=msk[:, kv, :], rhs=sk3r[:, t3, kv * 4:kv * 4 + 4, 0:nq], start=False, stop=(t3 == 2))
                    return last
                vr = []
                for b_ in blocks:
                    vr += b_[4]
                S.op('pe', of, reads=[pkeys[i] for i in range(len(blocks))] + vr + ['msk', 'sk3r'], writes=[pok])
                S.op('dve', lambda e, po=po: e.reciprocal(rec[64:128, 0, 0:N], po[64:128, 0:N]), reads=[pok], writes=['rec0'])
                S.op('dve', lambda e, po=po: e.reciprocal(rec[0:64, 1, 0:N], po[0:64, 256:256 + N]), reads=[pok], writes=['rec1'])
                S.op('dve', lambda e, po=po: e.tensor_tensor(out=catT[0:64, 4:8, cat_cols:cat_cols + nq],
                                                             in0=po[0:64, 0:N].rearrange("p (h q) -> p h q", h=4),
                                                             in1=rec[64:128, 0, 0:N].rearrange("p (h q) -> p h q", h=4), op=ALU.mult),
                     reads=[pok, 'rec0'], writes=['catT'])
                S.op('dve', lambda e, po=po: e.tensor_tensor(out=catT[64:128, 4:8, cat_cols:cat_cols + nq],
                                                             in0=po[64:128, 256:256 + N].rearrange("p (h q) -> p h q", h=4),
                                                             in1=rec[0:64, 1, 0:N].rearrange("p (h q) -> p h q", h=4), op=ALU.mult),
                     reads=[pok, 'rec1'], writes=['catT'])

            def kv_window(hTb, hk, c0, M, extra_reads=()):
                ps, pk = bank()

                def f(e, ps=ps):
                    last = None
                    for kc in range(8):
                        last = e.matmul(ps[0:M, 0:256], lhsT=hTb[:, kc, c0:c0 + M], rhs=Win[:, kc, 1536:1792], start=(kc == 0), stop=(kc == 7))
                    return last
                S.op('pe', f, reads=[hk, 'Win'] + list(extra_reads), writes=[pk])
                return ps, pk

            def v_to(ps, pk, M, dst, dkey, r0=0):
                S.op('act', lambda e: e.activation(out=dst[r0:r0 + M, 0, 0:64], in_=ps[r0:r0 + M, 128:192], func=AF.Copy), reads=[pk, 'Vinit'], writes=[dkey])
                S.op('act', lambda e: e.activation(out=dst[r0:r0 + M, 1, 64:128], in_=ps[r0:r0 + M, 192:256], func=AF.Copy), reads=[pk, dkey], writes=[dkey])

            stc = {'n': 0}

            def kv_out(ps, pk, r0, M, kdst, vdst):
                i = 0
                st = kvst[i]
                S.op('dve', lambda e: e.tensor_copy(st[r0:r0 + M, :], ps[r0:r0 + M, 0:256]), reads=[pk], writes=[f'kvst{i}'])
                if int(_os.environ.get("KDMA", "2")) >= 1:
                    S.dma('sp', f'o_kv{i}', [lambda e: e.dma_start(out=kdst, in_=st[r0:r0 + M, 0:128])], reads=[f'kvst{i}'])
                if int(_os.environ.get("KDMA", "2")) >= 2:
                    S.dma('sp', f'o_kv{i}', [lambda e: e.dma_start(out=vdst, in_=st[r0:r0 + M, 128:256])], reads=[f'kvst{i}'])

            def mixer_tile(ti, kind, n, src_fn, x2rows, tchunk0=None, nseg=1, pre=None, post_glu=None):
                xbuf = xt[ti % NXB]; hTb = hT[ti % 2]; hTp = hT[(ti + 1) % 2]
                gl = glb[ti % 2]; gln = glb[(ti + 1) % 2]
                glk = f'glb{ti % 2}'; glnk = f'glb{(ti + 1) % 2}'
                nb = (n + 127) // 128
                seglen = n // nseg
                if kind == 'frames' and ti > 0:
                    pass
                if pre is not None:
                    pre()
                xk, hk = f'xta{ti % NXB}', f'hTa{ti % 2}'
                if sub <= 1:
                    return
                if kind == 'frames':
                    S.op('pool', lambda e: e.tensor_copy(hTp[:, :, 0:128], hTb[:, :, NT:NT + 128]), reads=[hk], writes=[f'hTa{(ti + 1) % 2}t'])
                hreads = [hk, f'hTa{ti % 2}t'] if (kind == 'frames' and ti > 1) else [hk]

                def segview(ap3, off):
                    return ap3

                for j in range(4):
                    pa, pak = bank(); pb, pbk = bank()

                    def f(e, pa=pa, pb=pb, j=j):
                        last = None
                        for (ps, fc) in ((pa, j), (pb, 4 + j)):
                            for kc in range(8):
                                last = e.matmul(ps[:, 0:n], lhsT=Win[:, kc, fc * 128:(fc + 1) * 128], rhs=hTb[:, kc, 128:128 + n], start=(kc == 0), stop=(kc == 7))
                        return last
                    S.op('pe', f, reads=[hk, 'Win'], writes=[pak, pbk])
                    sg = sig[j % 2]
                    S.op('act', lambda e, pb=pb, sg=sg: e.activation(out=sg[:, 0:n], in_=pb[:, 0:n], func=AF.Tanh, scale=0.5), reads=[pbk], writes=[f'sig{j % 2}'])
                    if nseg == 1:
                        gdst = gl[:, j, 30:30 + n]; gin0 = pa[:, 0:n]; gin1 = sg[:, 0:n]
                    else:
                        gdst = gl[:, j, 0:nseg * (30 + seglen)].rearrange("p (s c) -> p s c", s=nseg)[:, :, 30:30 + seglen]
                        gin0 = pa[:, 0:n].rearrange("p (s c) -> p s c", s=nseg); gin1 = sg[:, 0:n].rearrange("p (s c) -> p s c", s=nseg)
                    S.op('dve', lambda e, gdst=gdst, gin0=gin0, gin1=gin1: e.scalar_tensor_tensor(out=gdst, in0=gin1, scalar=1.0, in1=gin0, op0=ALU.add, op1=ALU.mult),
                         reads=[pak, f'sig{j % 2}'], writes=[glk])
                    if post_glu is not None:
                        c0_, w_ = (n - 30, 30) if nseg == 1 else (0, n)
                        S.op('dve', lambda e, pa=pa, sg=sg, j=j, c0_=c0_, w_=w_: e.scalar_tensor_tensor(out=glf[:, j, 0:w_], in0=sg[:, c0_:c0_ + w_], scalar=1.0, in1=pa[:, c0_:c0_ + w_], op0=ALU.add, op1=ALU.mult),
                             reads=[pak, f'sig{j % 2}'], writes=['glf'])
                    yield
                if sub <= 2:
                    return
                if post_glu is not None:
                    post_glu()
                yield
                def proj_fm(fc):
                    ps, pk = bank()

                    def f(e, ps=ps, fc=fc):
                        last = None
                        for kc in range(8):
                            last = e.matmul(ps[:, 0:n], lhsT=Win[:, kc, fc * 128:(fc + 1) * 128], rhs=hTb[:, kc, 128:128 + n], start=(kc == 0), stop=(kc == 7))
                        return last
                    S.op('pe', f, reads=[hk, 'Win'], writes=[pk])
                    return ps, pk
                for fc in (12,):
                    ps, pk = proj_fm(fc)
                    if kind == 'meta':
                        act_copy(KTm[:, :], ps[:, 0:n], [pk], ['KTm'])
                        S.op('pool', lambda e: e.tensor_copy(KB[:, :, 64:80], KTm[:].unsqueeze(1).broadcast_to([128, SEQ // CH, 16])), reads=['KTm'], writes=['KBm'])
                    elif kind == 'frames':
                        f0 = tchunk0 * CH
                        act_copy(KTl[:, f0:f0 + n], ps[:, 0:n], [pk], [f'KTl{ti}'])
                        S.op('dve', lambda e, ps=ps: e.tensor_copy(KB[:, tchunk0:tchunk0 + n // CH, 0:64], ps[:, 0:n].rearrange("p (c t) -> p c t", t=CH)),
                             reads=[pk], writes=[f'KBo{ti}'])
                    else:
                        S.op('dve', lambda e, ps=ps: e.tensor_copy(KBs[:, :, 0:SL], ps[:, 0:n].rearrange("p (s t) -> p s t", t=SL)), reads=[pk], writes=['KBs'])
                    yield
                if sub <= 3:
                    return
                yield
                if kind == 'meta':
                    kvar = int(_os.environ.get("KVAR", "9"))
                    ps, pk = kv_window(hTb, hk, 128, NMETA)
                    if kvar >= 1:
                        v_to(ps, pk, NMETA, VM, 'VM')
                    if kvar >= 2:
                        kv_out(ps, pk, 0, NMETA, kmp, vmp)
                    if kvar >= 3:
                        ps, pk = kv_window(hTb, hk, 64, 80)
                    if kvar >= 4:
                        for i in range(2):
                            for lc in range(4):
                                v_to(ps, pk, NMETA, VB[i][:, lc], f'VB{i}', r0=64)
                elif kind == 'frames':
                    tp = ti % 2
                    for lc in range(4):
                        c = tchunk0 + lc
                        ps, pk = kv_window(hTb, hk, 128 + lc * CH, CH)
                        v_to(ps, pk, CH, VB[tp][:, lc], f'VB{tp}')
                        if c == 1:
                            ps, pk = kv_window(hTb, hk, 128, CH)
                            v_to(ps, pk, CH, VA[tp][:, lc], f'VA{tp}')
                        elif c >= 2:
                            ps, pk = kv_window(hTb, hk, lc * CH, 128, extra_reads=[f'hTa{ti % 2}t'])
                            v_to(ps, pk, 128, VA[tp][:, lc], f'VA{tp}')
                        yield
                    if tchunk0 + 4 == SEQ // CH:
                        ps, pk = kv_window(hTb, hk, NT, 128)
                        kv_out(ps, pk, 0, 128, kwp, vwp)
                else:
                    for s in range(NSB):
                        ps, pk = kv_window(hTb, hk, 128 + s * SL, SL)
                        v_to(ps, pk, SL, VBs[:, s], 'VBs')
                        kv_out(ps, pk, 0, SL, kws[s, 96:128, :], vws[s, 96:128, :])
                        yield
                if sub <= 4:
                    return
                yield
                seg_stride = 30 + seglen
                for j in range(4):
                    ps, pk = bank()

                    def f(e, ps=ps, j=j):
                        last = None
                        for k in range(CW):
                            if nseg == 1:
                                rhs = gl[:, j, k:k + n]
                            else:
                                rhs = gl[:, j, 0:nseg * seg_stride].rearrange("p (s c) -> p s c", s=nseg)[:, :, k:k + seglen]
                            last = e.matmul(ps[:, 0:n], lhsT=diag[:, j * CW + k, :], rhs=rhs, start=(k == 0), stop=(k == CW - 1))
                        return last
                    S.op('pe', f, reads=[glk, 'diag'], writes=[pk])
                    S.op('act', lambda e, ps=ps, j=j: e.activation(out=cv[:, j, 0:n], in_=ps[:, 0:n], func=AF.Identity, bias=cparT[:, j, CW:CW + 1]),
                         reads=[pk, 'cparT'], writes=['cv'])
                    yield
                if kind in ('meta', 'frames'):
                    S.op('pool', lambda e: e.tensor_copy(gln[:, :, 0:30], gl[:, :, n:n + 30]), reads=[glk], writes=[glnk])
                if sub <= 5:
                    return
                yield
                S.op('dve', lambda e: e.tensor_copy(cvb[:, :, 0:n], cv[:, :, 0:n]), reads=['cv'], writes=['cvb'])
                S.op('dve', lambda e: e.tensor_tensor(out=sq[:, :, 0:n], in0=cv[:, :, 0:n], in1=cv[:, :, 0:n], op=ALU.mult), reads=['cv'], writes=['sq'])
                yield
                p1_, p1k = bank(); p2_, p2k = bank()

                def f(e):
                    last = None
                    for (ps, src) in ((p1_, cvb), (p2_, sq)):
                        for j in range(4):
                            last = e.matmul(ps[:, 0:n], lhsT=ones_bf[:], rhs=src[:, j, 0:n], start=(j == 0), stop=(j == 3))
                    return last
                S.op('pe', f, reads=['cvb', 'sq', 'ones_bf'], writes=[p1k, p2k])
                act_copy(lnm[:, 0:n], p1_[:, 0:n], [p1k], ['lnm'], scale=1.0 / DCONV)
                S.op('dve', lambda e: e.tensor_tensor(out=lnq[:, 0:n], in0=lnm[:, 0:n], in1=lnm[:, 0:n], op=ALU.mult), reads=['lnm'], writes=['lnq'])
                S.op('dve', lambda e: e.scalar_tensor_tensor(out=lnv[:, 0:n], in0=p2_[:, 0:n], scalar=1.0 / DCONV, in1=lnq[:, 0:n], op0=ALU.mult, op1=ALU.subtract),
                     reads=[p2k, 'lnq'], writes=['lnv'])
                S.op('act', lambda e: e.activation(out=lnq[:, 0:n], in_=lnv[:, 0:n], func=AF.Sqrt, bias=epsc[:, :]), reads=['lnv', 'epsc'], writes=['lnq'])
                S.op('dve', lambda e: e.reciprocal(lnr[:, 0:n], lnq[:, 0:n]), reads=['lnq'], writes=['lnr'])
                yield
                for j in range(4):
                    t_ = tn[j % 2]; tk_ = f'tn{j % 2}'
                    S.op('dve', lambda e, t_=t_, j=j: e.tensor_tensor(out=t_[:, 0:n], in0=cv[:, j, 0:n], in1=lnm[:, 0:n], op=ALU.subtract), reads=['cv', 'lnm'], writes=[tk_])
                    S.op('dve', lambda e, t_=t_: e.tensor_tensor(out=t_[:, 0:n], in0=t_[:, 0:n], in1=lnr[:, 0:n], op=ALU.mult), reads=[tk_, 'lnr'], writes=[tk_])
                    S.op('act', lambda e, t_=t_, j=j: e.activation(out=catT[:, j, 0:n], in_=t_[:, 0:n], func=AF.Silu, scale=cparT[:, j, CW + 1:CW + 2],
                                                                   bias=cparT[:, j, CW + 2:CW + 3]), reads=[tk_, 'cparT'], writes=['catT'])
                if sub <= 6:
                    return
                yield
                for fc in range(8, 12):
                    ps, pk = proj_fm(fc)
                    act_copy(qTk[0][0:64, fc - 8, 0:n], ps[0:64, 0:n], [pk], ['qT'], scale=0.125)
                    act_copy(qTk[1][64:128, fc - 8, 0:n], ps[64:128, 0:n], [pk], ['qT'], scale=0.125)
                    yield
                units = []
                if kind == 'meta':
                    units.append((0, NMETA, [(KTm[:, :], NMETA, VM, bM[:], ['KTm', 'VM'])], 0, 0))
                elif kind == 'frames':
                    tp = ti % 2
                    for lc in range(4):
                        c = tchunk0 + lc
                        blocks = []
                        if c == 1:
                            blocks.append((KTl[:, 0:CH], CH, VA[tp][:, lc], bA1[:], [f'KTl{ti}', f'VA{tp}']))
                        elif c >= 2:
                            blocks.append((KTl[:, (c - 2) * CH:c * CH], 128, VA[tp][:, lc], bA[:], [f'KTl{ti}', f'KTl{ti - 1}', f'VA{tp}']))
                        blocks.append((KB[:, c, :], 80, VB[tp][:, lc], bB[min(c, 3)][:], [f'KBo{ti}', 'KBm', f'VB{tp}']))
                        units.append((lc * CH, CH, blocks, lc * CH, lc % 2))
                else:
                    for s in range(NSB):
                        blocks = [(KTsA[:, s, :], 128, VAs[:, s], bSA[:], ['KTsA', 'VAs']),
                                  (KBs[:, s, :], 48, VBs[:, s], bSB[:], ['KBs', 'VBs'])]
                        units.append((s * SL, SL, blocks, s * SL, s % 2))
                pendB = None
                for u in units:
                    attn_A(*u)
                    yield
                    if pendB is not None:
                        attn_B(*pendB)
                        yield
                    pendB = u
                attn_B(*pendB)
                if sub <= 7:
                    return
                yield
                for b in range(nb):
                    r = min(128, n - b * 128)
                    pa, pak = bank(); pb, pbk = bank()

                    def f(e, pa=pa, pb=pb, b=b, r=r):
                        last = None
                        for (ps, half) in ((pa, 0), (pb, 1)):
                            for kc in range(8):
                                last = e.matmul(ps[0:r, :], lhsT=catT[:, kc, b * 128:b * 128 + r], rhs=Wout[:, kc, half * 512:(half + 1) * 512], start=(kc == 0), stop=(kc == 7))
                        return last
                    S.op('pe', f, reads=['catT', 'Wout'], writes=[pak, pbk])
                    S.op('dve', lambda e, pa=pa, b=b, r=r: e.tensor_tensor(out=xbuf[0:r, b, 0:512], in0=pa[0:r, :], in1=xbuf[0:r, b, 0:512], op=ALU.add), reads=[pak, xk], writes=[xk])
                    S.op('dve', lambda e, pb=pb, b=b, r=r: e.tensor_tensor(out=xbuf[0:r, b, 512:1024], in0=pb[0:r, :], in1=xbuf[0:r, b, 512:1024], op=ALU.add), reads=[pbk, xk], writes=[xk])
                    if b + 1 < nb:
                        yield
                S.dma('sp', f'o_x2{ti % NXB}', [lambda e, b=b: e.dma_start(out=x2[x2rows + b * 128:x2rows + b * 128 + min(128, n - b * 128), :],
                                                                       in_=xbuf[0:min(128, n - b * 128), b, :]) for b in range(nb)], reads=[xk], writes=['x2d'])

            def glu_out(n, off, dst):
                ps, pk = bank()

                def f(e, ps=ps):
                    last = None
                    for j in range(4):
                        last = e.transpose(ps[0:30, j * 128:(j + 1) * 128], glf[:, j, off:off + 30], identf[:])
                    return last
                S.op('pe', f, reads=['glf', 'identf'], writes=[pk])
                S.op('dve', lambda e, ps=ps: e.tensor_copy(ost[0:30, :], ps[0:30, :]), reads=[pk], writes=['ost'])
                S.dma('sp', 'o_glu', [lambda e: e.dma_start(out=dst, in_=ost[0:30, :])], reads=['ost'])

            for i in range(2):
                S.op('pool', lambda e, i=i: e.memset(qTk[i][:], 0.0), writes=['qT'])
            for i in range(2):
                S.op('pool', lambda e, i=i: e.memset(hT[i][:], 0.0), writes=[f'hTa{i}', f'hTa{i}t'])
            S.op('pool', lambda e: e.memset(glb[0][:], 0.0), writes=['glb0'])
            gens = []
            heads = []
            head_t = []
            def meta_with_bias():
                k = 0
                for _ in mixer_tile(0, 'meta', NMETA, lambda xb: [lambda e: e.dma_start(out=xb[0:NMETA, 0, :], in_=meta)], 0):
                    yield
                    k += 1
                    if bias_jobs and k >= 3:
                        bias_jobs.pop(0)()
                while bias_jobs:
                    bias_jobs.pop(0)()
            if lvl >= 2:
                gens.append(meta_with_bias)
                heads.append(lambda: head_a(0, lambda xb: [lambda e: e.dma_start(out=xb[0:NMETA, 0, :], in_=meta)], NMETA, gmix_b, 'a'))
                head_t.append(lambda: load_norm_transpose(0, None, NMETA, None, hT[0], gmix_b, 'a'))
            for t in range(SEQ // NT if lvl >= 4 else (1 if lvl >= 3 else 0)):
                gens.append(lambda t=t: mixer_tile(t + 1, 'frames', NT, lambda xb, t=t: [lambda e: e.dma_start(out=xb[:], in_=xp[t * NT:(t + 1) * NT, :].rearrange("(b p) d -> p b d", p=128))],
                                                   NMETA + t * NT, tchunk0=t * (NT // CH),
                                                   post_glu=(lambda: glu_out(NT, 0, cap)) if t == SEQ // NT - 1 else None))
                heads.append(lambda t=t: head_a(t + 1, lambda xb, t=t: [lambda e: e.dma_start(out=xb[:], in_=xp[t * NT:(t + 1) * NT, :].rearrange("(b p) d -> p b d", p=128))], NT, gmix_b, 'a'))
                head_t.append(lambda t=t: load_norm_transpose(t + 1, None, NT, None, hT[(t + 1) % 2], gmix_b, 'a'))
            ti = SEQ // NT + 1
            gls = glb[ti % 2]

            def sample_pre():
              for s in range(NSB):
                sca_s = ost[0:CW - 1, :]
                S.dma('sp', 'c_sca', [lambda e, s=s: e.dma_start(out=sca_s, in_=sca[s])], writes=['ost'])
                ps, pk = bank()

                def f(e, ps=ps, s=s, sca_s=sca_s):
                    last = None
                    for j in range(4):
                        last = e.transpose(ps[:, j * 32:j * 32 + 30], sca_s[:, j * 128:(j + 1) * 128], identf[0:30, 0:30])
                    return last
                S.op('pe', f, reads=['ost', 'identf'], writes=[pk])
                S.op('dve', lambda e, ps=ps, s=s: e.tensor_copy(gls[:, :, s * 62:s * 62 + 30], ps[:, 0:128].rearrange("p (j c) -> p j c", j=4)[:, :, 0:30]),
                     reads=[pk], writes=[f'glb{ti % 2}'])

            def sample_post_glu():
                for s in range(NSB):
                    glu_out(NSB * SL, s * SL + 2, cas[s])
            if lvl >= 5:
                gens.append(lambda: mixer_tile(ti, 'sample', NSB * SL, lambda xb: [lambda e: e.dma_start(out=xb[:, 0, :], in_=xs)], NMETA + SEQ, nseg=NSB,
                                               pre=sample_pre, post_glu=sample_post_glu))
                heads.append(lambda: head_a(ti, lambda xb: [lambda e: e.dma_start(out=xb[:, 0, :], in_=xs)], NSB * SL, gmix_b, 'a'))
                head_t.append(lambda: load_norm_transpose(ti, None, NSB * SL, None, hT[ti % 2], gmix_b, 'a'))
            PSKEW = int(_os.environ.get("KSKEW", "16"))
            active = []
            tick = 0
            nxt = 0
            last_start = -10 ** 9
            HLEAD = int(_os.environ.get("KHLEAD", "14"))
            heads_emitted = 0
            started = {}
            while active or nxt < len(gens):
                if nxt < len(gens) and len(active) < 2 and tick - last_start >= PSKEW and heads_emitted > nxt:
                    active.append((nxt, gens[nxt]())); started[nxt] = tick; nxt += 1
                    last_start = tick
                if heads_emitted < len(heads):
                    h = heads_emitted
                    if h == 0 or (h - 1 in started and tick - started[h - 1] >= HLEAD):
                        S.cur_tile = h
                        heads[h]()
                        heads_emitted += 1
                        for _ in head_t[h]():
                            pass
                for (tid, g) in (list(active)[::-1] if _os.environ.get("KORD", "0") == "1" else list(active)):
                    S.cur_tile = tid
                    try:
                        next(g)
                    except StopIteration:
                        active.remove((tid, g))
                tick += 1
            S.cur_tile = -1
            S.finish('sp')
            S.emit(block1)

        with ExitStack() as p2:
            pc = p2.enter_context

            def sb2(name, shape, dt=F32):
                return sb(name, shape, dt, ctx=pc)

            Wup = sb2("Wup", [128, 8, 2 * DFF], BF16); Wdn = sb2("Wdn", [128, NJ, D], BF16)
            NXQ = 3
            xq = [sb2(f"xq{i}", [128, 2, D]) for i in range(NXQ)]
            hb2 = [sb2(f"hc{i}", [128, D], BF16) for i in range(2)]
            h2T = [sb2(f"h2T{i}", [128, 8, NT], BF16) for i in range(2)]
            GSC = 2 + NT
            gs = [sb2(f"gs{i}", [128, GSC]) for i in range(2)]
            cacc = [sb2(f"cacc{i}", [128, NT]) for i in range(2)]
            sl = [sb2(f"sl{i}", [128, NT]) for i in range(2)]
            yT = sb2("yT", [128, NJ, NT], BF16)
            ghist = sb2("ghist", [128, NJ, 2])
            scfT = sb2("scfT", [128, NSB, NJ, 2])
            gst = sb2("gst", [128, 1 + NSB, 2, NJ])
            gso = sb2("gso", [64, 128])
            r_ss = sb2("r_ss", [128, 4]); r_sd = sb2("r_sd", [128, 4]); r_r = sb2("r_r", [128, 4])
            f_ss = sb2("f_ss", [128, 4]); f_sd = sb2("f_sd", [128, 4]); f_r = sb2("f_r", [128, 4])
            eps2 = sb2("eps2", [128, 1]); junk = sb2("junk", [128, 512], BF16)
            gffn_b = sb2("gffn_b", [128, D]); gfin_b = sb2("gfin_b", [128, D])
            fparS = yT[0:4].rearrange("p j t -> p (j t)").bitcast(F32)
            assert tuple(fparS.shape) == (4, DFF), fparS.shape
            block2 = pc(nc.Block())
            S.dma('sp', 'c_g2', [lambda e: e.dma_start(out=gffn_b[:], in_=gffn.partition_broadcast(128)),
                                 lambda e: e.dma_start(out=gfin_b[:], in_=gfin.partition_broadcast(128)),
                                 lambda e: e.dma_start(out=fparS, in_=fpar)], writes=['gb2', 'yT'])
            for j in range(NJ):
                ps, pk = bank()
                S.op('pe', lambda e, ps=ps, j=j: e.transpose(ps[:, 0:4], fparS[:, j * 128:(j + 1) * 128], identf[0:4, 0:4]),
                     reads=['gb2', 'yT', 'identf'], writes=[pk])
                act_copy(fparT[:, j, :], ps[:, 0:4], [pk], ['fparT'])
            S.op('pool', lambda e: e.memset(eps2[:], EPS), writes=['eps2'])
            S.op('pool', lambda e: e.memset(ghist[:], 0.0), writes=[f'ghist{j}' for j in range(NJ)])

            CB = 512
            ncb = (DFF + CB - 1) // CB
            wq = []
            for cbk in range(ncb if lvl >= 6 else 0):
                c0 = cbk * CB; c1 = min(DFF, c0 + CB)
                wq.append(lambda cbk=cbk, c0=c0, c1=c1: S.dma('pool', f'wg{cbk}', [lambda e, kc=kc: e.dma_start(out=Wup[:, kc, c0:c1], in_=wup[kc * 128:(kc + 1) * 128, c0:c1]) for kc in range(8)], writes=[f'Wg{cbk}']))
                wq.append(lambda cbk=cbk, c0=c0, c1=c1: S.dma('pool', f'wu{cbk}', [lambda e, kc=kc: e.dma_start(out=Wup[:, kc, DFF + c0:DFF + c1], in_=wup[kc * 128:(kc + 1) * 128, DFF + c0:DFF + c1]) for kc in range(8)], writes=[f'Wu{cbk}']))
            for j0 in range(0, NJ if lvl >= 6 else 0, 4):
                j1 = min(NJ, j0 + 4)
                wq.append(lambda j0=j0, j1=j1: S.dma('pool', f'wd{j0}', [lambda e, j=j: e.dma_start(out=Wdn[:, j, :], in_=wdn[j * 128:(j + 1) * 128, :]) for j in range(j0, j1)], writes=[f'Wd{j0 // 4}']))
            for _ in range(min(4, len(wq))):
                wq.pop(0)()
            scfS = gso[0:NJ, :]
            for s in range(NSB):
                for t in range(2):
                    S.dma('sp', 'c_scf', [lambda e, s=s, t=t: e.dma_start(out=scfS, in_=scf[s, t, :].rearrange("(j p) -> j p", p=128))], writes=['gso'])
                    ps, pk = bank()
                    S.op('pe', lambda e, ps=ps: e.transpose(ps[:, 0:NJ], scfS, identf[0:NJ, 0:NJ]), reads=['gso', 'identf'], writes=[pk])
                    act_copy(scfT[:, s, :, t], ps[:, 0:NJ], [pk], ['scfT'])

            def out_rows(r0, r):
                res = []
                a, bnd = r0, r0 + r
                for (lo, hi, dst, base) in ((NMETA, NMETA + SEQ, yp, NMETA), (NMETA + SEQ, NROWS, ys, NMETA + SEQ)):
                    s_, e_ = max(a, lo), min(bnd, hi)
                    if s_ < e_:
                        res.append((s_ - r0, e_ - s_, dst[s_ - base:e_ - base, :]))
                return res

            usb = [sb2(f"usb{i}", [128, NT], BF16) for i in range(2)]

            def ffn_head(ti, r0, n, groups):
                xbuf = xq[ti % NXQ]; xk = f'xq{ti % NXQ}'
                nb = (n + 127) // 128
                hTb = h2T[ti % 2]; hk = f'h2T{ti % 2}'
                S.dma('sp', f'xq{ti % NXQ}', [lambda e, b=b: e.dma_start(out=xbuf[0:min(128, n - b * 128), b, :], in_=x2[r0 + b * 128:r0 + b * 128 + min(128, n - b * 128), :]) for b in range(nb)],
                      reads=['x2d'], writes=[xk])
                for b in range(nb):
                    r = min(128, n - b * 128)
                    for hf in range(2):
                        S.op('act', lambda e, b=b, r=r, hf=hf: e.activation(out=junk[0:r, :], in_=xbuf[0:r, b, hf * 512:(hf + 1) * 512], func=AF.Square,
                                                                            accum_out=r_ss[0:r, 2 * b + hf:2 * b + hf + 1]), reads=[xk], writes=['r_ss'])
                    S.op('dve', lambda e, b=b, r=r: e.tensor_tensor(out=r_sd[0:r, b:b + 1], in0=r_ss[0:r, 2 * b:2 * b + 1], in1=r_ss[0:r, 2 * b + 1:2 * b + 2], op=ALU.add), reads=['r_ss'], writes=['r_sd'])
                    S.op('dve', lambda e, b=b, r=r: e.tensor_scalar(out=r_sd[0:r, 2 + b:3 + b], in0=r_sd[0:r, b:b + 1], scalar1=1.0 / D, scalar2=EPS, op0=ALU.mult, op1=ALU.add), reads=['r_sd'], writes=['r_sd2'])
                    S.op('pool', lambda e, b=b, r=r: e.tensor_tensor(out=r_r[0:r, b:b + 1], in0=r_sd[0:r, 2 + b:3 + b], in1=mhalf[0:r, :], op=ALU.pow), reads=['r_sd2', 'mhalf'], writes=['r_r'])
                    hbb = hb2[b % 2]
                    S.op('dve', lambda e, b=b, r=r, hbb=hbb: e.scalar_tensor_tensor(out=hbb[0:r, :], in0=xbuf[0:r, b, :], scalar=r_r[0:r, b:b + 1], in1=gffn_b[0:r, :], op0=ALU.mult, op1=ALU.mult),
                         reads=[xk, 'r_r', 'gb2'], writes=[f'hc{b % 2}'])

            def ffn_head_b(ti, r0, n, groups):
                nb = (n + 127) // 128
                hTb = h2T[ti % 2]; hk = f'h2T{ti % 2}'
                for b in range(nb):
                    r = min(128, n - b * 128)
                    hbb = hb2[b % 2]
                    pt, tk = tbank()

                    def tf(e, pt=pt, hbb=hbb, r=r):
                        last = None
                        for kc in range(8):
                            last = e.transpose(pt[:, kc * 128:kc * 128 + r], hbb[0:r, kc * 128:(kc + 1) * 128], ident[0:r, 0:r])
                        return last
                    S.op('pe', tf, reads=[f'hc{b % 2}', 'ident'], writes=[tk])
                    act_copy(hTb[:, :, b * 128:b * 128 + r], pt[:].rearrange("p (k t) -> p k t", k=8)[:, :, 0:r], [tk], [hk])

            def ffn_chunks(ti, r0, n, groups, j0, j1):
                hTb = h2T[ti % 2]; hk = f'h2T{ti % 2}'
                lay = []
                off = 0
                for (col0, nseg, seglen, hist) in groups:
                    lay.append(off); off += nseg * (2 + seglen)
                pend = []
                for j in range(j0, j1):
                    if wq:
                        wq.pop(0)()
                    pg, pgk = bank(); pu, puk = bank()
                    cbk = (j * 128) // CB

                    def f(e, pg=pg, pu=pu, j=j):
                        last = None
                        for (ps, c0) in ((pg, j * 128), (pu, DFF + j * 128)):
                            for kc in range(8):
                                last = e.matmul(ps[:, 0:n], lhsT=Wup[:, kc, c0:c0 + 128], rhs=hTb[:, kc, 0:n], start=(kc == 0), stop=(kc == 7))
                        return last
                    S.op('pe', f, reads=[hk, f'Wg{cbk}', f'Wu{cbk}'], writes=[pgk, puk])
                    g_ = gs[j % 2]; gk = f'gs{j % 2}'; ca = cacc[j % 2]; cak = f'cacc{j % 2}'; s_ = sl[j % 2]; sk = f'sl{j % 2}'
                    ub = usb[j % 2]; ubk = f'usb{j % 2}'
                    for gi, (col0, nseg, seglen, hist) in enumerate(groups):
                        base = lay[gi]
                        gv = g_[:, base:base + nseg * (2 + seglen)].rearrange("p (s c) -> p s c", s=nseg)
                        if hist == 'carry':
                            act_copy(gv[:, 0, 0:2], ghist[:, j, :], [f'ghist{j}'], [gk + 'h'])
                        else:
                            act_copy(gv[:, :, 0:2], scfT[:, :, j, :], ['scfT'], [gk + 'h'])
                        act_copy(gv[:, :, 2:2 + seglen], pg[:, col0:col0 + nseg * seglen].rearrange("p (s c) -> p s c", s=nseg), [pgk], [gk])
                        cav = ca[:, col0:col0 + nseg * seglen].rearrange("p (s c) -> p s c", s=nseg)
                        S.op('dve', lambda e, gv=gv, cav=cav, j=j, seglen=seglen: e.tensor_scalar(out=cav, in0=gv[:, :, 0:seglen], scalar1=fparT[:, j, 0:1], scalar2=fparT[:, j, 3:4], op0=ALU.mult, op1=ALU.add),
                             reads=[gk, gk + 'h', 'fparT'], writes=[cak])
                        for k in (1, 2):
                            S.op('dve', lambda e, gv=gv, cav=cav, j=j, k=k, seglen=seglen: e.scalar_tensor_tensor(out=cav, in0=gv[:, :, k:k + seglen], scalar=fparT[:, j, k:k + 1], in1=cav, op0=ALU.mult, op1=ALU.add),
                                 reads=[gk, gk + 'h', 'fparT', cak], writes=[cak])
                        if hist == 'carry' and ti < NTILES2 - 1:
                            S.op('pool', lambda e, gv=gv, j=j, seglen=seglen: e.tensor_copy(ghist[:, j, :], gv[:, 0, seglen:seglen + 2]), reads=[gk], writes=[f'ghist{j}'])
                    if ti == NTILES2 - 1:
                        gv0 = g_[:, lay[0]:lay[0] + 2 + groups[0][2]]
                        S.op('pool', lambda e, gv0=gv0, j=j: e.tensor_copy(gst[:, 0, :, j], gv0[:, groups[0][2]:groups[0][2] + 2]), reads=[gk], writes=['gst'])
                        gv1 = g_[:, lay[1]:lay[1] + NSB * (2 + SL)].rearrange("p (s c) -> p s c", s=NSB)
                        S.op('pool', lambda e, gv1=gv1, j=j: e.tensor_copy(gst[:, 1:1 + NSB, :, j], gv1[:, :, SL:SL + 2]), reads=[gk], writes=['gst'])
                    act_copy(ub[:, 0:n], pu[:, 0:n], [puk], [ubk])

                    def back(ca=ca, s_=s_, ub=ub, j=j, cak=cak, sk=sk, ubk=ubk):
                        S.op('act', lambda e: e.activation(out=s_[:, 0:n], in_=ca[:, 0:n], func=AF.Silu), reads=[cak], writes=[sk])
                        ydst = yTe[ti % 2][:, j, 0:n] if j < JE else yT[:, j, 0:n]
                        S.op(YENG, lambda e: e.tensor_tensor(out=ydst, in0=ub[:, 0:n], in1=s_[:, 0:n], op=ALU.mult), reads=[ubk, sk],
                             writes=[f'yTe{ti % 2}' if j < JE else 'yT'])
                    if pend:
                        pend.pop()()
                    pend.append(back)
                if pend:
                    pend.pop()()

            JE = int(_os.environ.get("KJE", "4"))
            yTe = [sb2(f"yTe{i}", [128, max(JE, 1), NT], BF16) for i in range(2)]
            wdb = {}
            YENG = _os.environ.get("KYENG", "pool")

            def ffn_wdown(ti, r0, n, groups, part):
                nb = (n + 127) // 128
                wdb[ti] = [(bank(), bank()) for b in range(nb)]
                ja, jb = 0, NJ
                for b in range(nb):
                    r = min(128, n - b * 128)
                    (pa, pak), (pb, pbk) = wdb[ti][b]

                    def f(e, pa=pa, pb=pb, b=b, r=r):
                        last = None
                        for (ps, half) in ((pa, 0), (pb, 1)):
                            for j in range(ja, jb):
                                ysrc = yTe[ti % 2] if j < JE else yT
                                last = e.matmul(ps[0:r, :], lhsT=ysrc[:, j, b * 128:b * 128 + r], rhs=Wdn[:, j, half * 512:(half + 1) * 512], start=(j == 0), stop=(j == NJ - 1))
                        return last
                    S.op('pe', f, reads=['yT', f'yTe{ti % 2}'] + [f'Wd{i}' for i in range((NJ + 3) // 4)], writes=[pak, pbk])

            def ffn_tail(ti, r0, n, groups):
                xbuf = xq[ti % NXQ]; xk = f'xq{ti % NXQ}'
                nb = (n + 127) // 128
                for b in range(nb):
                    r = min(128, n - b * 128)
                    (pa, pak), (pb, pbk) = wdb[ti][b]
                    S.op('dve', lambda e, pa=pa, b=b, r=r: e.tensor_tensor(out=xbuf[0:r, b, 0:512], in0=pa[0:r, :], in1=xbuf[0:r, b, 0:512], op=ALU.add), reads=[pak, xk], writes=[xk])
                    S.op('dve', lambda e, pb=pb, b=b, r=r: e.tensor_tensor(out=xbuf[0:r, b, 512:1024], in0=pb[0:r, :], in1=xbuf[0:r, b, 512:1024], op=ALU.add), reads=[pbk, xk], writes=[xk])
                    for hf in range(2):
                        S.op('act', lambda e, b=b, r=r, hf=hf: e.activation(out=junk[0:r, :], in_=xbuf[0:r, b, hf * 512:(hf + 1) * 512], func=AF.Square,
                                                                            accum_out=f_ss[0:r, 2 * b + hf:2 * b + hf + 1]), reads=[xk], writes=['f_ss'])
                    S.op('dve', lambda e, b=b, r=r: e.tensor_tensor(out=f_sd[0:r, b:b + 1], in0=f_ss[0:r, 2 * b:2 * b + 1], in1=f_ss[0:r, 2 * b + 1:2 * b + 2], op=ALU.add), reads=['f_ss'], writes=['f_sd'])
                    S.op('dve', lambda e, b=b, r=r: e.tensor_scalar(out=f_sd[0:r, 2 + b:3 + b], in0=f_sd[0:r, b:b + 1], scalar1=1.0 / D, scalar2=EPS, op0=ALU.mult, op1=ALU.add), reads=['f_sd'], writes=['f_sd2'])
                    S.op('pool', lambda e, b=b, r=r: e.tensor_tensor(out=f_r[0:r, b:b + 1], in0=f_sd[0:r, 2 + b:3 + b], in1=mhalf[0:r, :], op=ALU.pow), reads=['f_sd2', 'mhalf'], writes=['f_r'])
                    S.op('dve', lambda e, b=b, r=r: e.scalar_tensor_tensor(out=xbuf[0:r, b, :], in0=xbuf[0:r, b, :], scalar=f_r[0:r, b:b + 1], in1=gfin_b[0:r, :], op0=ALU.mult, op1=ALU.mult),
                         reads=[xk, 'f_r', 'gb2'], writes=[xk])
                    outs = out_rows(r0 + b * 128, r)
                    if outs:
                        S.dma('sp', f'o_y{ti % NXQ}', [lambda e, p0=p0, cnt=cnt, dst=dst, b=b: e.dma_start(out=dst, in_=xbuf[p0:p0 + cnt, b, :]) for (p0, cnt, dst) in outs], reads=[xk])

            NFULL = (NMETA + SEQ) // NT
            NTILES2 = NFULL + 1
            tail = NMETA + SEQ - NFULL * NT
            tiles2 = [(t, t * NT, NT, [(0, 1, NT, 'carry')]) for t in range(NFULL if lvl >= 9 else (1 if lvl >= 7 else 0))]
            if lvl >= 9:
                tiles2.append((NFULL, NFULL * NT, tail + NSB * SL, [(0, 1, tail, 'carry'), (tail, NSB, SL, 'state')]))
            JH = int(_os.environ.get("KJH", "4"))
            JHB = int(_os.environ.get("KJHB", "4"))
            if tiles2:
                ffn_head(*tiles2[0])
                ffn_head_b(*tiles2[0])
            for i_, tl in enumerate(tiles2):
                nx = tiles2[i_ + 1] if i_ + 1 < len(tiles2) else None
                ffn_chunks(*tl, JE if i_ > 0 else 0, JH)
                if nx:
                    ffn_head(*nx)
                ffn_chunks(*tl, JH, JH + JHB)
                if nx:
                    ffn_head_b(*nx)
                ffn_chunks(*tl, JH + JHB, NJ)
                if nx:
                    ffn_chunks(*nx, 0, JE)
                ffn_wdown(*tl, 0)
                ffn_tail(*tl)
            for g in range(1 + NSB if lvl >= 9 else 0):
                ps, pk = bank()
                S.op('pe', lambda e, ps=ps, g=g: e.transpose(ps[0:2 * NJ, 0:128], gst[:, g, :, :].rearrange("p t j -> p (t j)"), identf[:]), reads=['gst', 'identf'], writes=[pk])
                S.op('dve', lambda e, ps=ps: e.tensor_copy(gso[0:2 * NJ, :], ps[0:2 * NJ, 0:128]), reads=[pk], writes=['gso'])
                dst = cfp if g == 0 else cfs[g - 1]
                S.dma('sp', 'o_gst', [lambda e, t=t, dst=dst: e.dma_start(out=dst[t, :].rearrange("(j p) -> j p", p=128), in_=gso[t * NJ:(t + 1) * NJ, :]) for t in range(2)], reads=['gso'])
            S.finish('sp')
            S.emit(block2)
    return nc


def _rel_bucket_np(rel):
    nb = 16
    max_exact = 8
    ret = np.where(rel > 0, nb, 0)
    n = np.abs(rel)
    nf = np.maximum(n, 1).astype(np.float32)
    large = max_exact + (np.log(nf / np.float32(max_exact)) / np.float32(math.log(256 / max_exact)) * np.float32(nb - max_exact)).astype(np.int32)
    large = np.minimum(large, nb - 1)
    return ret + np.where(n < max_exact, n, large)


_CACHE = {}


def kernel(x_prompt, x_sample, cache_k_meta, cache_v_meta, cache_k_win, cache_v_win, state_conv_a, state_conv_ffn,
           meta_tokens, rel_bias_table, norm_mix, w_in, conv_dw_w, conv_dw_b, conv_ln_g, conv_ln_b, attn_sinks, w_out,
           norm_ffn, w_up, ffn_dw_w, ffn_dw_b, w_down, norm_final):
    f = lambda a: np.ascontiguousarray(np.asarray(a, dtype=np.float32))
    x_prompt = f(x_prompt); x_sample = f(x_sample)
    w_in0 = f(w_in)[0]; w_out0 = f(w_out)[0]
    qcols = []
    for j in range(4):
        qcols += list(range(1024 + 64 * j, 1024 + 64 * j + 64)) + list(range(1024 + 64 * (j + 4), 1024 + 64 * (j + 4) + 64))
    cols = list(range(1024)) + qcols + list(range(1536, 1792))
    win_p = np.ascontiguousarray(w_in0[:, cols])
    rows = list(range(512))
    for j in range(4):
        rows += list(range(512 + 64 * j, 512 + 64 * j + 64)) + list(range(512 + 64 * (j + 4), 512 + 64 * (j + 4) + 64))
    wout_p = np.ascontiguousarray(w_out0[rows, :])
    cpar = np.ascontiguousarray(np.concatenate([f(conv_dw_w)[0], f(conv_dw_b), f(conv_ln_g), f(conv_ln_b)], axis=0))
    fpar = np.ascontiguousarray(np.concatenate([f(ffn_dw_w)[0], f(ffn_dw_b)], axis=0))
    rel = np.arange(E_LEN, dtype=np.int32) - E_OFF
    bkt = _rel_bucket_np(rel)
    eoh = (bkt[None, :] == np.arange(32)[:, None]).astype(np.float32)
    shared = {
        "meta": f(meta_tokens), "win": win_p, "wout": wout_p, "wup": f(w_up)[0], "wdn": f(w_down)[0],
        "gmix": f(norm_mix)[0], "gffn": f(norm_ffn)[0], "gfin": f(norm_final), "cpar": cpar, "fpar": fpar,
        "sinks": f(attn_sinks)[0], "tab": f(rel_bias_table), "eoh": eoh,
    }
    ckm = f(cache_k_meta)[0].reshape(32, NMETA, 128); cvm = f(cache_v_meta)[0].reshape(32, NMETA, 128)
    ckw = f(cache_k_win)[0].reshape(32, 128, 128); cvw = f(cache_v_win)[0].reshape(32, 128, 128)
    sca = f(state_conv_a)[0]; scf = f(state_conv_ffn)[0]
    in_maps = []
    for c in range(8):
        m = dict(shared)
        sl_ = slice(NSB * c, NSB * (c + 1))
        m.update({"xp": x_prompt[c], "xs": np.ascontiguousarray(x_sample[sl_].reshape(NSB * SL, D)),
                  "ckm": ckm[sl_], "cvm": cvm[sl_], "ckw": ckw[sl_], "cvw": cvw[sl_], "sca": sca[sl_], "scf": scf[sl_]})
        in_maps.append({k: np.ascontiguousarray(v) for k, v in m.items()})
    if _CACHE.get("maps_only"):
        return in_maps
    if "nc" not in _CACHE:
        _CACHE["nc"] = build_program()
    res = run_bass_kernel_spmd(_CACHE["nc"], in_maps, core_ids=list(range(8)))
    R = res.results
    cat = lambda k: np.stack([np.asarray(R[c][k], dtype=np.float32) for c in range(8)], axis=0)
    y_prompt = cat("yp")
    y_sample = cat("ys").reshape(32, SL, D)
    k_meta_p = cat("kmp").reshape(1, 8, NMETA, 2, 64); v_meta_p = cat("vmp").reshape(1, 8, NMETA, 2, 64)
    k_win_p = cat("kwp").reshape(1, 8, 128, 2, 64); v_win_p = cat("vwp").reshape(1, 8, 128, 2, 64)
    conv_a_p = cat("cap").reshape(1, 8, CW - 1, DCONV); conv_ffn_p = cat("cfp").reshape(1, 8, 2, DFF)
    k_win_s = cat("kws").reshape(1, 32, 128, 2, 64); v_win_s = cat("vws").reshape(1, 32, 128, 2, 64)
    conv_a_s = cat("cas").reshape(1, 32, CW - 1, DCONV); conv_ffn_s = cat("cfs").reshape(1, 32, 2, DFF)
    return (y_prompt, y_sample, k_meta_p, v_meta_p, k_win_p, v_win_p, conv_a_p, conv_ffn_p,
            k_win_s, v_win_s, conv_a_s, conv_ffn_s)
```

```python
import math
from contextlib import ExitStack

import numpy as np
import concourse.bass as bass
import concourse.mybir as mybir
from concourse.bass_utils import run_bass_kernel_spmd

F32 = mybir.dt.float32
BF16 = mybir.dt.bfloat16
AF = mybir.ActivationFunctionType
ALU = mybir.AluOpType

D = 1024
SEQ = 2048
NMETA = 16
CH = 64
DCONV = 512
CW = 31
DFF = 2816
NJ = DFF // 128
EPS = 1e-6
NT = 256
NSB = 4
SL = 32
NROWS = NMETA + SEQ + NSB * SL
E_OFF = 271
E_LEN = 335


class Sched:
    def __init__(self, nc, sems):
        self.nc = nc
        self.free_sems = list(sems)
        self.prog = {k: [] for k in ('pe', 'act', 'dve', 'pool', 'sp')}
        self.esem = {k: self.free_sems.pop() for k in ('pe', 'act', 'dve', 'pool')}
        self.count = {k: 0 for k in self.esem}
        self.seen = {k: {} for k in self.prog}
        self.lastw = {}
        self.readers = {}
        self.slotsem = {}
        self.slotcnt = {}
        self.semid = {}
        self.cur_tile = -1
        self.wtile = {}

    def _deps(self, reads, writes, eng=None):
        deps = []
        mysem = self.esem.get(eng)
        for r in reads:
            if r in self.lastw:
                deps.append(self.lastw[r])
            if r[:2] in ('ps', 'pt'):
                deps.extend(t for t in self.readers.get(r, []) if t[0] is not mysem)
        for w in writes:
            if w in self.lastw:
                deps.append(self.lastw[w])
            deps.extend(self.readers.get(w, []))
        return deps

    def _emit_waits(self, eng, deps):
        need = {}
        for (sem, val) in deps:
            k = id(sem)
            self.semid[k] = sem
            if self.seen[eng].get(k, 0) >= val:
                continue
            need[k] = max(need.get(k, 0), val)
        for k, val in need.items():
            self.seen[eng][k] = val
            sem = self.semid[k]
            self.prog[eng].append(lambda e, sem=sem, val=val: e.wait_ge(sem, val))

    def _record(self, reads, writes, tok):
        for r in reads:
            assert self.wtile.get(r, -1) <= self.cur_tile or self.cur_tile < 0, (r, self.wtile.get(r), self.cur_tile)
        for w in writes:
            self.wtile[w] = self.cur_tile
        for w in writes:
            self.lastw[w] = tok
            self.readers[w] = []
        for r in reads:
            if r not in writes:
                self.readers.setdefault(r, []).append(tok)

    def op(self, eng, fn, reads=(), writes=()):
        self._emit_waits(eng, self._deps(reads, writes, eng))
        sem = self.esem[eng]
        self.count[eng] += 1
        val = self.count[eng]
        self.prog[eng].append(lambda e, fn=fn, sem=sem: fn(e).then_inc(sem, 1))
        self._record(reads, writes, (sem, val))

    def dma(self, q, slot, fn_list, reads=(), writes=()):
        if slot not in self.slotsem:
            self.slotsem[slot] = self.free_sems.pop()
            self.slotcnt[slot] = 0
        sem = self.slotsem[slot]
        self._emit_waits(q, self._deps(reads, writes))
        for fn in fn_list:
            self.slotcnt[slot] += 16
            self.prog[q].append(lambda e, fn=fn, sem=sem: fn(e).then_inc(sem, 16))
        self._record(reads, writes, (sem, self.slotcnt[slot]))

    def finish(self, eng='sp'):
        deps = [(s, self.slotcnt[k]) for k, s in self.slotsem.items()]
        deps += [(self.esem[k], self.count[k]) for k in self.esem if self.count[k] > 0]
        self._emit_waits(eng, deps)

    def emit(self, block):
        progs = self.prog
        self.prog = {k: [] for k in progs}

        @block.tensor
        def _(e):
            for f in progs['pe']:
                f(e)

        @block.scalar
        def _(e):
            for f in progs['act']:
                f(e)

        @block.vector
        def _(e):
            for f in progs['dve']:
                f(e)

        @block.gpsimd
        def _(e):
            for f in progs['pool']:
                f(e)

        @block.sync
        def _(e):
            for f in progs['sp']:
                f(e)


def build_program(stop=None):
    nc = bass.Bass("TRN2", target_bir_lowering=False)
    import os as _os
    sub = int(_os.environ.get("KSUB", "99"))
    lvl = {None: 9, 'setup0': 0, 'setup': 1, 'meta': 2, 'f1': 3, 'frames': 4, 'p1': 5, 'w2': 6, 't2': 7}[stop]

    def din(name, shape):
        return nc.dram_tensor(name, list(shape), F32, kind="ExternalInput").ap()

    def dout(name, shape):
        return nc.dram_tensor(name, list(shape), F32, kind="ExternalOutput").ap()

    xp = din("xp", [SEQ, D]); xs = din("xs", [NSB * SL, D]); meta = din("meta", [NMETA, D])
    ckm = din("ckm", [NSB, NMETA, 128]); cvm = din("cvm", [NSB, NMETA, 128])
    ckw = din("ckw", [NSB, 128, 128]); cvw = din("cvw", [NSB, 128, 128])
    sca = din("sca", [NSB, CW - 1, DCONV]); scf = din("scf", [NSB, 2, DFF])
    win = din("win", [D, 1792]); wout = din("wout", [D, D]); wup = din("wup", [D, 2 * DFF]); wdn = din("wdn", [DFF, D])
    gmix = din("gmix", [D]); gffn = din("gffn", [D]); gfin = din("gfin", [D])
    cpar = din("cpar", [CW + 3, DCONV])
    fpar = din("fpar", [4, DFF])
    sinks = din("sinks", [8]); tab = din("tab", [32, 8]); eoh = din("eoh", [32, E_LEN])
    yp = dout("yp", [SEQ, D]); ys = dout("ys", [NSB * SL, D])
    kmp = dout("kmp", [NMETA, 128]); vmp = dout("vmp", [NMETA, 128])
    kwp = dout("kwp", [128, 128]); vwp = dout("vwp", [128, 128])
    cap = dout("cap", [CW - 1, DCONV]); cfp = dout("cfp", [2, DFF])
    kws = dout("kws", [NSB, 128, 128]); vws = dout("vws", [NSB, 128, 128])
    cas = dout("cas", [NSB, CW - 1, DCONV]); cfs = dout("cfs", [NSB, 2, DFF])
    x2 = nc.dram_tensor("x2", [NROWS, D], F32).ap()

    with ExitStack() as outer:
        ec = outer.enter_context
        sems = [ec(nc.semaphore(f"s{i}")) for i in range(60)]
        S = Sched(nc, sems)

        def sb(name, shape, dt=F32, ctx=None):
            return (ctx or ec)(nc.sbuf_tensor(name, list(shape), dt))

        NBANK = 8
        psb = [ec(nc.psum_tensor(f"ps{i}", [128, 512], F32)) for i in range(NBANK)]
        rr = {'b': 0}

        def bank():
            i = rr['b']; rr['b'] = (i + 1) % NBANK
            return psb[i], f"ps{i}"

        def tbank():
            ps, pk = bank()
            return ps[:].bitcast(BF16), pk

        identf = sb("identf", [128, 128]); ident = sb("ident", [128, 128], BF16)
        ones_bf = sb("ones_bf", [128, 128], BF16)
        gmix_b = sb("gmix_b", [128, D])
        cparT = sb("cparT", [128, 4, CW + 3])
        fparT = sb("fparT", [128, NJ, 4])
        epsc = sb("epsc", [128, 1]); mhalf = sb("mhalf", [128, 1])
        S.op('pool', lambda e: e.memset(mhalf[:], -0.5), writes=['mhalf'])
        S.op('pool', lambda e: e.memset(identf[:], 0.0), writes=['identf'])
        S.op('pool', lambda e: e.affine_select(out=identf[:], in_=identf[:], pattern=[[-1, 128]], compare_op=ALU.not_equal,
                                               fill=1.0, base=0, channel_multiplier=1), reads=['identf'], writes=['identf'])
        S.op('pool', lambda e: e.tensor_copy(ident[:], identf[:]), reads=['identf'], writes=['ident'])
        S.op('pool', lambda e: e.memset(ones_bf[:], 1.0), writes=['ones_bf'])
        S.dma('sp', 'c_g', [lambda e: e.dma_start(out=gmix_b[:], in_=gmix.partition_broadcast(128))], writes=['gb'])

        def act_copy(out, in_, reads, writes, scale=None):
            if scale is None:
                S.op('act', lambda e: e.activation(out=out, in_=in_, func=AF.Copy), reads=reads, writes=writes)
            else:
                S.op('act', lambda e: e.activation(out=out, in_=in_, func=AF.Copy, scale=scale), reads=reads, writes=writes)

        with ExitStack() as p1:
            pc = p1.enter_context

            def sb1(name, shape, dt=F32):
                return sb(name, shape, dt, ctx=pc)

            Win = sb1("Win", [128, 8, 1792], BF16); Wout = sb1("Wout", [128, 8, D], BF16)
            diag = sb1("diag", [128, 4 * CW, 128], BF16)
            XLQ = _os.environ.get("KXLQ", "pool")
            NXB = 3
            xt = [sb1(f"xt{i}", [128, 2, D]) for i in range(NXB)]
            hb = [sb1(f"hb{i}", [128, D], BF16) for i in range(2)]
            HTC = 128 + NT
            hT = [sb1(f"hT{i}", [128, 8, HTC], BF16) for i in range(2)]
            GLC = 30 + NT
            glb = [sb1(f"glb{i}", [128, 4, GLC], BF16) for i in range(2)]
            sA = sb1("sA", [128, 2, 256]); sBt = sb1("sBt", [128, 2, 256])
            glf = sb1("glf", [128, 4, NSB * SL])
            junk1 = glf[:].rearrange("p j c -> p (j c)")
            sig = [sb1(f"sig{i}", [128, NT]) for i in range(2)]
            cv = sb1("cv", [128, 4, NT]); cvb = sb1("cvb", [128, 4, NT], BF16); sq = sb1("sq", [128, 4, NT], BF16)
            lnm = sb1("lnm", [128, NT]); lnq = sb1("lnq", [128, NT]); lnv = sb1("lnv", [128, NT]); lnr = sb1("lnr", [128, NT])
            tn = [sb1(f"tn{i}", [128, NT]) for i in range(2)]
            qTk = [sb1(f"qT{i}", [128, 4, NT], BF16) for i in range(2)]
            KTl = sb1("KTl", [128, SEQ], BF16)
            KB = sb1("KB", [128, SEQ // CH, 80], BF16)
            KTm = sb1("KTm", [128, NMETA], BF16)
            VA = [sb1(f"VA{i}", [128, 4, 2, 128], BF16) for i in range(2)]
            VB = [sb1(f"VB{i}", [128, 4, 2, 128], BF16) for i in range(2)]
            VM = sb1("VM", [128, 2, 128], BF16)
            catT = sb1("catT", [128, 8, NT], BF16)
            pA = [sb1(f"pA{i}", [128, 2, 256], BF16) for i in range(2)]
            pB = [sb1(f"pB{i}", [128, 2, 256], BF16) for i in range(2)]
            rec = sb1("rec", [128, 2, 256])
            kvst = [sb1("kvst0", [128, 256])] * 2
            ost = sb1("ost", [32, DCONV])
            Eb = ost[:, E_LEN + 1:DCONV].bitcast(BF16)[:, 0:E_LEN]
            rs_ss = sb1("rs_ss", [128, 4]); rs_sd = sb1("rs_sd", [128, 4]); rs_r = sb1("rs_r", [128, 4])
            bA = sb1("bA", [128, 2, 4, 64]); bA1 = sb1("bA1", [64, 2, 4, 64])
            bB = [sb1(f"bB{c}", [80, 2, 4, 64]) for c in range(4)]
            bM = sb1("bM", [16, 2, 4, 16])
            bSA = sb1("bSA", [128, 2, 4, SL]); bSB = sb1("bSB", [48, 2, 4, SL])
            tabf = sb1("tabf", [32, 8]); tabd = sb1("tabd", [32, 8]); tab3 = sb1("tab3", [32, 3, 8], BF16)
            skf = sb1("skf", [1, 8]); ske = sb1("ske", [1, 8]); skd = sb1("skd", [1, 8]); sk3 = sb1("sk3", [1, 3, 8], BF16)
            sk3r = sb1("sk3r", [1, 3, 8, 64], BF16)
            msk = sb1("msk", [1, 2, 128], BF16)
            cparS = rec[0:CW + 3].rearrange("p k n -> p (k n)")
            ckw_b = sb1("ckw_b", [128, NSB, 128], BF16); ckm_b = sb1("ckm_b", [16, NSB, 128], BF16)
            ckw_s = cv[:, 0:2, :].rearrange("p a (b f) -> p (a b) f", b=2)
            cvw_s = cv[:, 2:4, :].rearrange("p a (b f) -> p (a b) f", b=2)
            ckm_s = sA[0:16].rearrange("p a (b f) -> p (a b) f", b=2)
            cvm_s = sBt[:].rearrange("p a (b f) -> p (a b) f", b=2)
            KTsA = sb1("KTsA", [128, NSB, 128], BF16); KBs = sb1("KBs", [128, NSB, 48], BF16)
            VAs = sb1("VAs", [128, NSB, 2, 128], BF16); VBs = sb1("VBs", [128, NSB, 2, 128], BF16)
            block1 = pc(nc.Block())

            for (wkey, c0_, c1_) in (('WinG', 0, 2 * DCONV), ('WinK', 1536, 1792), ('WinQ', 2 * DCONV, 1536)):
                S.dma('pool', 'w_in' + wkey, [lambda e, kc=kc, c0_=c0_, c1_=c1_: e.dma_start(out=Win[:, kc, c0_:c1_], in_=win[kc * 128:(kc + 1) * 128, c0_:c1_]) for kc in range(8)], writes=[wkey])
            S.op('dve', lambda e: e.tensor_scalar(out=Win[:, :, 0:DCONV], in0=Win[:, :, 0:DCONV], scalar1=0.5, scalar2=None, op0=ALU.mult), reads=['WinG'], writes=['WinG'])
            S.dma('pool', 'w_out', [lambda e, kc=kc: e.dma_start(out=Wout[:, kc, :], in_=wout[kc * 128:(kc + 1) * 128, :]) for kc in range(8)], writes=['Wout'])
            S.dma('sp', 'c_par2', [lambda e: e.dma_start(out=cparS, in_=cpar)], writes=['rec0', 'rec1'])
            S.dma('sp', 'c_par', [
                                  lambda e: e.dma_start(out=ost[0:32, 0:E_LEN], in_=eoh), lambda e: e.dma_start(out=tabf[:], in_=tab),
                                  lambda e: e.dma_start(out=skf[:], in_=sinks.rearrange("(o h) -> o h", o=1))], writes=['par', 'ost'])
            S.dma('sp', 'c_cache', [lambda e: e.dma_start(out=ckw_s, in_=ckw.rearrange("b t f -> t b f")),
                                    lambda e: e.dma_start(out=ckm_s, in_=ckm.rearrange("b t f -> t b f")),
                                    lambda e: e.dma_start(out=cvw_s, in_=cvw.rearrange("b t f -> t b f")),
                                    lambda e: e.dma_start(out=cvm_s[32:48, :, :], in_=cvm.rearrange("b t f -> t b f"))], writes=['cv', 'sA', 'sBt'])
            S.dma('sp', 'o_roll', [lambda e: e.dma_start(out=kws[:, 0:96, :], in_=ckw[:, 32:128, :]),
                                   lambda e: e.dma_start(out=vws[:, 0:96, :], in_=cvw[:, 32:128, :])])

            for j in range(4):
                ps, pk = bank()
                S.op('pe', lambda e, ps=ps, j=j: e.transpose(ps[:, 0:CW + 3], cparS[:, j * 128:(j + 1) * 128], identf[0:CW + 3, 0:CW + 3]),
                     reads=['rec0', 'rec1', 'identf'], writes=[pk])
                act_copy(cparT[:, j, :], ps[:, 0:CW + 3], [pk], ['cparT'])
            for j in range(4):
                S.op('dve', lambda e, j=j: e.tensor_tensor(out=diag[:, j * CW:(j + 1) * CW, :],
                                                           in0=identf[:].unsqueeze(1).broadcast_to([128, CW, 128]),
                                                           in1=cparT[:, j, 0:CW].unsqueeze(2).broadcast_to([128, CW, 128]), op=ALU.mult),
                     reads=['identf', 'cparT'], writes=['diag'])
            def split3(dst3, src, tmp, rkey, wkey):
                S.op('dve', lambda e: e.tensor_copy(tmp, src), reads=[rkey], writes=[wkey + 't'])
                for i in range(3):
                    S.op('dve', lambda e, i=i: e.tensor_copy(dst3[:, i, :], tmp), reads=[wkey + 't'], writes=[wkey])
                    if i < 2:
                        S.op('dve', lambda e, i=i: e.tensor_tensor(out=tmp, in0=tmp, in1=dst3[:, i, :], op=ALU.subtract), reads=[wkey + 't', wkey], writes=[wkey + 't'])
            S.op('act', lambda e: e.activation(out=ske[:], in_=skf[:], func=AF.Exp), reads=['par'], writes=['ske'])
            split3(sk3, ske[:], skd[:], 'ske', 'sk3')
            S.op('dve', lambda e: e.tensor_copy(Eb, ost[0:32, 0:E_LEN]), reads=['par', 'ost'], writes=['Eb', 'ost'])
            split3(tab3, tabf[:], tabd[:], 'par', 'tab3')
            S.op('dve', lambda e: e.tensor_copy(sk3r[:], sk3[:].unsqueeze(3).broadcast_to([1, 3, 8, 64])), reads=['sk3'], writes=['sk3r'])
            S.op('pool', lambda e: e.memset(msk[:], 0.0), writes=['msk'])
            S.op('pool', lambda e: e.memset(msk[:, 0, 64:128], 1.0), reads=['msk'], writes=['msk'])
            S.op('pool', lambda e: e.memset(msk[:, 1, 0:64], 1.0), reads=['msk'], writes=['msk'])

            NSPLIT = 2

            def gen_bias(dst, parts, nq):
                ps, pk = bank()
                psv = ps[:, 0:nq * 8].rearrange("p (q h) -> p q h", h=8)

                def f(e):
                    last = None
                    for (r0, M, s0) in parts:
                        for i in range(nq):
                            for t3 in range(NSPLIT):
                                last = e.matmul(psv[r0:r0 + M, i, :], lhsT=Eb[:, s0 - i:s0 - i + M], rhs=tab3[:, t3, :], start=(t3 == 0), stop=(t3 == NSPLIT - 1))
                    return last
                S.op('pe', f, reads=['Eb', 'tab3', 'ost'], writes=[pk])
                rows = max(r0 + M for (r0, M, s0) in parts)
                S.op('dve', lambda e: e.tensor_copy(dst[0:rows].rearrange("p k h q -> p (k h) q"), psv[0:rows].rearrange("p q h -> p h q")),
                     reads=[pk], writes=['bias'])

            gen_bias(bM, [(0, 16, E_OFF)], 16)
            bias_jobs = [lambda: gen_bias(bB[0], [(0, 64, E_OFF), (64, 16, E_OFF - 16)], 64),
                         lambda: gen_bias(bA1, [(0, 64, E_OFF - 64)], 64),
                         lambda: gen_bias(bB[1], [(0, 64, E_OFF), (64, 16, E_OFF - 16 - 64)], 64),
                         lambda: gen_bias(bA, [(0, 128, E_OFF - 128)], 64),
                         lambda: gen_bias(bB[2], [(0, 64, E_OFF), (64, 16, E_OFF - 16 - 128)], 64),
                         lambda: gen_bias(bB[3], [(0, 64, E_OFF), (64, 16, E_OFF - 16 - 192)], 64),
                         lambda: gen_bias(bSA, [(0, 128, E_OFF - 128)], SL),
                         lambda: gen_bias(bSB, [(0, 32, E_OFF), (32, 16, SL)], SL)]

            for t_ in (VA[0], VA[1], VB[0], VB[1], VM, VAs, VBs):
                S.op('pool', lambda e, t_=t_: e.memset(t_[:], 1.0), writes=['Vinit'])

            S.op('dve', lambda e: e.tensor_copy(ckw_b[:], ckw_s), reads=['cv'], writes=['ckw_b'])
            S.op('dve', lambda e: e.tensor_copy(ckm_b[:], ckm_s), reads=['sA'], writes=['ckm_b'])
            for s in range(NSB):
                pt, tk = tbank()
                S.op('pe', lambda e, pt=pt, s=s: e.transpose(pt[:, 0:128], ckw_b[:, s, :], ident[:]), reads=['ckw_b', 'ident'], writes=[tk])
                act_copy(KTsA[:, s, :], pt[:, 0:128], [tk], ['KTsA'])
                pt, tk = tbank()
                S.op('pe', lambda e, pt=pt, s=s: e.transpose(pt[:, 0:16], ckm_b[:, s, :], ident[0:16, 0:16]), reads=['ckm_b', 'ident'], writes=[tk])
                act_copy(KBs[:, s, 32:48], pt[:, 0:16], [tk], ['KBs'])
            S.op('dve', lambda e: e.tensor_copy(VAs[:, :, 0, 0:64], cvw_s[:, :, 0:64]), reads=['cv', 'Vinit'], writes=['VAs'])
            S.op('dve', lambda e: e.tensor_copy(VAs[:, :, 1, 64:128], cvw_s[:, :, 64:128]), reads=['cv', 'VAs'], writes=['VAs'])
            S.op('dve', lambda e: e.tensor_copy(VBs[32:48, :, 0, 0:64], cvm_s[32:48, :, 0:64]), reads=['sBt', 'Vinit'], writes=['VBs'])
            S.op('dve', lambda e: e.tensor_copy(VBs[32:48, :, 1, 64:128], cvm_s[32:48, :, 64:128]), reads=['sBt', 'VBs'], writes=['VBs'])

            def head_a(ti, src_fn, n, gb, keyp):
                xbuf = xt[ti % NXB]
                nb = (n + 127) // 128
                S.dma(XLQ, f'x{keyp}{ti % NXB}', src_fn(xbuf), writes=[f'xt{keyp}{ti % NXB}'])
                xk = f'xt{keyp}{ti % NXB}'
                for b in range(nb):
                    r = min(128, n - b * 128)
                    S.op('act', lambda e, b=b, r=r: e.activation(out=junk1[0:r, :], in_=xbuf[0:r, b, 0:512], func=AF.Square,
                                                                 accum_out=rs_ss[0:r, 2 * b:2 * b + 1]), reads=[xk], writes=['rs_ss', 'glf'])
                    S.op('act', lambda e, b=b, r=r: e.activation(out=junk1[0:r, :], in_=xbuf[0:r, b, 512:1024], func=AF.Square,
                                                                 accum_out=rs_ss[0:r, 2 * b + 1:2 * b + 2]), reads=[xk], writes=['rs_ss', 'glf'])
                    S.op('dve', lambda e, b=b, r=r: e.tensor_tensor(out=rs_sd[0:r, b:b + 1], in0=rs_ss[0:r, 2 * b:2 * b + 1],
                                                                    in1=rs_ss[0:r, 2 * b + 1:2 * b + 2], op=ALU.add), reads=['rs_ss'], writes=['rs_sd'])
                    S.op('dve', lambda e, b=b, r=r: e.tensor_scalar(out=rs_sd[0:r, 2 + b:3 + b], in0=rs_sd[0:r, b:b + 1], scalar1=1.0 / D, scalar2=EPS,
                                                                    op0=ALU.mult, op1=ALU.add), reads=['rs_sd'], writes=['rs_sd2'])
                    S.op('pool', lambda e, b=b, r=r: e.tensor_tensor(out=rs_r[0:r, b:b + 1], in0=rs_sd[0:r, 2 + b:3 + b], in1=mhalf[0:r, :], op=ALU.pow),
                         reads=['rs_sd2', 'mhalf'], writes=['rs_r'])
                    hbb = hb[b % 2]
                    S.op('dve', lambda e, b=b, r=r, hbb=hbb: e.scalar_tensor_tensor(out=hbb[0:r, :], in0=xbuf[0:r, b, :], scalar=rs_r[0:r, b:b + 1],
                                                                                    in1=gb[0:r, :], op0=ALU.mult, op1=ALU.mult),
                         reads=[xk, 'rs_r', 'gb'], writes=[f'hb{b % 2}'])

            def load_norm_transpose(ti, src_fn, n, xbuf, hTb, gb, keyp):
                nb = (n + 127) // 128
                for b in range(nb):
                    r = min(128, n - b * 128)
                    hbb = hb[b % 2]
                    if b > 0:
                        yield
                    pt, tk = tbank()

                    def tf(e, pt=pt, hbb=hbb, r=r):
                        last = None
                        for kc in range(8):
                            last = e.transpose(pt[:, kc * 128:kc * 128 + r], hbb[0:r, kc * 128:(kc + 1) * 128], ident[0:r, 0:r])
                        return last
                    S.op('pe', tf, reads=[f'hb{b % 2}', 'ident'], writes=[tk])
                    act_copy(hTb[:, :, 128 + b * 128:128 + b * 128 + r], pt[:].rearrange("p (k t) -> p k t", k=8)[:, :, 0:r], [tk], [f'hT{keyp}{ti % 2}'])

            S.op('pool', lambda e: e.memset(epsc[:], EPS), writes=['epsc'])

            def attn_A(qcols, nq, blocks, cat_cols, pidx):
                N = 4 * nq
                stiles = [sA, sBt]
                ptiles = [pA[pidx], pB[pidx]]
                pkeys = [f'pA{pidx}', f'pB{pidx}']
                skeys = ['sA', 'sBt']
                for bi, (KTap, M, VEap, bias_ap, rkeys) in enumerate(blocks):
                    ps, pk = bank()

                    def sf(e, ps=ps, KTap=KTap, M=M):
                        last = None
                        for kv in range(2):
                            last = e.matmul(ps[0:M, kv * 256:kv * 256 + N], lhsT=KTap,
                                            rhs=qTk[kv][:, :, qcols:qcols + nq], start=True, stop=True)
                        return last
                    S.op('pe', sf, reads=['qT'] + rkeys, writes=[pk])
                    st = stiles[bi]
                    S.op('dve', lambda e, ps=ps, st=st, M=M, bias_ap=bias_ap: e.tensor_tensor(
                        out=st[0:M, :, 0:N], in0=ps[0:M, :].rearrange("p (k n) -> p k n", k=2)[:, :, 0:N],
                        in1=bias_ap.rearrange("p k h q -> p k (h q)"), op=ALU.add), reads=[pk, 'bias'], writes=[skeys[bi]])
                    S.op('act', lambda e, st=st, M=M, pt_=ptiles[bi]: e.activation(out=pt_[0:M, :, 0:N], in_=st[0:M, :, 0:N], func=AF.Exp),
                         reads=[skeys[bi]], writes=[pkeys[bi]])

            def attn_B(qcols, nq, blocks, cat_cols, pidx):
                N = 4 * nq
                ptiles = [pA[pidx], pB[pidx]]
                pkeys = [f'pA{pidx}', f'pB{pidx}']
                po, pok = bank()

                def of(e, po=po):
                    last = None
                    for kv in range(2):
                        for bi, (KTap, M, VEap, bias_ap, rkeys) in enumerate(blocks):
                            e.matmul(po[:, kv * 256:kv * 256 + N], lhsT=VEap[0:M, kv, :], rhs=ptiles[bi][0:M, kv, 0:N], start=(bi == 0), stop=False)
                        for t3 in range(3):
                            last = e.matmul(po[:, kv * 256:kv * 256 + N], lhsT=msk[:, kv, :], rhs=sk3r[:, t3, kv * 4:kv * 4 + 4, 0:nq], start=False, stop=(t3 == 2))
                    return last
                vr = []
                for b_ in blocks:
                    vr += b_[4]
                S.op('pe', of, reads=[pkeys[i] for i in range(len(blocks))] + vr + ['msk', 'sk3r'], writes=[pok])
                S.op('dve', lambda e, po=po: e.reciprocal(rec[64:128, 0, 0:N], po[64:128, 0:N]), reads=[pok], writes=['rec0'])
                S.op('dve', lambda e, po=po: e.reciprocal(rec[0:64, 1, 0:N], po[0:64, 256:256 + N]), reads=[pok], writes=['rec1'])
                S.op('dve', lambda e, po=po: e.tensor_tensor(out=catT[0:64, 4:8, cat_cols:cat_cols + nq],
                                                             in0=po[0:64, 0:N].rearrange("p (h q) -> p h q", h=4),
                                                             in1=rec[64:128, 0, 0:N].rearrange("p (h q) -> p h q", h=4), op=ALU.mult),
                     reads=[pok, 'rec0'], writes=['catT'])
                S.op('dve', lambda e, po=po: e.tensor_tensor(out=catT[64:128, 4:8, cat_cols:cat_cols + nq],
                                                             in0=po[64:128, 256:256 + N].rearrange("p (h q) -> p h q", h=4),
                                                             in1=rec[0:64, 1, 0:N].rearrange("p (h q) -> p h q", h=4), op=ALU.mult),
                     reads=[pok, 'rec1'], writes=['catT'])

            def kv_window(hTb, hk, c0, M, extra_reads=()):
                ps, pk = bank()

                def f(e, ps=ps):
                    last = None
                    for kc in range(8):
                        last = e.matmul(ps[0:M, 0:256], lhsT=hTb[:, kc, c0:c0 + M], rhs=Win[:, kc, 1536:1792], start=(kc == 0), stop=(kc == 7))
                    return last
                S.op('pe', f, reads=[hk, 'WinK'] + list(extra_reads), writes=[pk])
                return ps, pk

            def v_to(ps, pk, M, dst, dkey, r0=0):
                S.op('act', lambda e: e.activation(out=dst[r0:r0 + M, 0, 0:64], in_=ps[r0:r0 + M, 128:192], func=AF.Copy), reads=[pk, 'Vinit'], writes=[dkey])
                S.op('act', lambda e: e.activation(out=dst[r0:r0 + M, 1, 64:128], in_=ps[r0:r0 + M, 192:256], func=AF.Copy), reads=[pk, dkey], writes=[dkey])

            stc = {'n': 0}

            def kv_out(ps, pk, r0, M, kdst, vdst):
                i = 0
                st = kvst[i]
                S.op('dve', lambda e: e.tensor_copy(st[r0:r0 + M, :], ps[r0:r0 + M, 0:256]), reads=[pk], writes=[f'kvst{i}'])
                if int(_os.environ.get("KDMA", "2")) >= 1:
                    S.dma('sp', f'o_kv{i}', [lambda e: e.dma_start(out=kdst, in_=st[r0:r0 + M, 0:128])], reads=[f'kvst{i}'])
                if int(_os.environ.get("KDMA", "2")) >= 2:
                    S.dma('sp', f'o_kv{i}', [lambda e: e.dma_start(out=vdst, in_=st[r0:r0 + M, 128:256])], reads=[f'kvst{i}'])

            def mixer_tile(ti, kind, n, src_fn, x2rows, tchunk0=None, nseg=1, pre=None, post_glu=None):
                xbuf = xt[ti % NXB]; hTb = hT[ti % 2]; hTp = hT[(ti + 1) % 2]
                gl = glb[ti % 2]; gln = glb[(ti + 1) % 2]
                glk = f'glb{ti % 2}'; glnk = f'glb{(ti + 1) % 2}'
                nb = (n + 127) // 128
                seglen = n // nseg
                if kind == 'frames' and ti > 0:
                    pass
                if pre is not None:
                    pre()
                xk, hk = f'xta{ti % NXB}', f'hTa{ti % 2}'
                if sub <= 1:
                    return
                if kind == 'frames':
                    S.op('pool', lambda e: e.tensor_copy(hTp[:, :, 0:128], hTb[:, :, NT:NT + 128]), reads=[hk], writes=[f'hTa{(ti + 1) % 2}t'])
                hreads = [hk, f'hTa{ti % 2}t'] if (kind == 'frames' and ti > 1) else [hk]

                def segview(ap3, off):
                    return ap3

                for j in range(4):
                    pa, pak = bank(); pb, pbk = bank()

                    def f(e, pa=pa, pb=pb, j=j):
                        last = None
                        for (ps, fc) in ((pa, j), (pb, 4 + j)):
                            for kc in range(8):
                                last = e.matmul(ps[:, 0:n], lhsT=Win[:, kc, fc * 128:(fc + 1) * 128], rhs=hTb[:, kc, 128:128 + n], start=(kc == 0), stop=(kc == 7))
                        return last
                    S.op('pe', f, reads=[hk, 'WinG'], writes=[pak, pbk])
                    sg = sig[j % 2]
                    S.op('act', lambda e, pb=pb, sg=sg: e.activation(out=sg[:, 0:n], in_=pb[:, 0:n], func=AF.Tanh, scale=0.5), reads=[pbk], writes=[f'sig{j % 2}'])
                    if nseg == 1:
                        gdst = gl[:, j, 30:30 + n]; gin0 = pa[:, 0:n]; gin1 = sg[:, 0:n]
                    else:
                        gdst = gl[:, j, 0:nseg * (30 + seglen)].rearrange("p (s c) -> p s c", s=nseg)[:, :, 30:30 + seglen]
                        gin0 = pa[:, 0:n].rearrange("p (s c) -> p s c", s=nseg); gin1 = sg[:, 0:n].rearrange("p (s c) -> p s c", s=nseg)
                    S.op('dve', lambda e, gdst=gdst, gin0=gin0, gin1=gin1: e.scalar_tensor_tensor(out=gdst, in0=gin1, scalar=1.0, in1=gin0, op0=ALU.add, op1=ALU.mult),
                         reads=[pak, f'sig{j % 2}'], writes=[glk])
                    if post_glu is not None:
                        c0_, w_ = (n - 30, 30) if nseg == 1 else (0, n)
                        S.op('dve', lambda e, pa=pa, sg=sg, j=j, c0_=c0_, w_=w_: e.scalar_tensor_tensor(out=glf[:, j, 0:w_], in0=sg[:, c0_:c0_ + w_], scalar=1.0, in1=pa[:, c0_:c0_ + w_], op0=ALU.add, op1=ALU.mult),
                             reads=[pak, f'sig{j % 2}'], writes=['glf'])
                    yield
                if sub <= 2:
                    return
                if post_glu is not None:
                    post_glu()
                yield
                def proj_fm(fc):
                    ps, pk = bank()

                    def f(e, ps=ps, fc=fc):
                        last = None
                        for kc in range(8):
                            last = e.matmul(ps[:, 0:n], lhsT=Win[:, kc, fc * 128:(fc + 1) * 128], rhs=hTb[:, kc, 128:128 + n], start=(kc == 0), stop=(kc == 7))
                        return last
                    S.op('pe', f, reads=[hk, 'WinK' if fc == 12 else 'WinQ'], writes=[pk])
                    return ps, pk
                for fc in (12,):
                    ps, pk = proj_fm(fc)
                    if kind == 'meta':
                        act_copy(KTm[:, :], ps[:, 0:n], [pk], ['KTm'])
                        S.op('pool', lambda e: e.tensor_copy(KB[:, :, 64:80], KTm[:].unsqueeze(1).broadcast_to([128, SEQ // CH, 16])), reads=['KTm'], writes=['KBm'])
                    elif kind == 'frames':
                        f0 = tchunk0 * CH
                        act_copy(KTl[:, f0:f0 + n], ps[:, 0:n], [pk], [f'KTl{ti}'])
                        S.op('dve', lambda e, ps=ps: e.tensor_copy(KB[:, tchunk0:tchunk0 + n // CH, 0:64], ps[:, 0:n].rearrange("p (c t) -> p c t", t=CH)),
                             reads=[pk], writes=[f'KBo{ti}'])
                    else:
                        S.op('dve', lambda e, ps=ps: e.tensor_copy(KBs[:, :, 0:SL], ps[:, 0:n].rearrange("p (s t) -> p s t", t=SL)), reads=[pk], writes=['KBs'])
                    yield
                if sub <= 3:
                    return
                yield
                if kind == 'meta':
                    kvar = int(_os.environ.get("KVAR", "9"))
                    ps, pk = kv_window(hTb, hk, 128, NMETA)
                    if kvar >= 1:
                        v_to(ps, pk, NMETA, VM, 'VM')
                    if kvar >= 2:
                        kv_out(ps, pk, 0, NMETA, kmp, vmp)
                    if kvar >= 3:
                        ps, pk = kv_window(hTb, hk, 64, 80)
                    if kvar >= 4:
                        for i in range(2):
                            for lc in range(4):
                                v_to(ps, pk, NMETA, VB[i][:, lc], f'VB{i}', r0=64)
                elif kind == 'frames':
                    tp = ti % 2
                    for lc in range(4):
                        c = tchunk0 + lc
                        ps, pk = kv_window(hTb, hk, 128 + lc * CH, CH)
                        v_to(ps, pk, CH, VB[tp][:, lc], f'VB{tp}')
                        if c == 1:
                            ps, pk = kv_window(hTb, hk, 128, CH)
                            v_to(ps, pk, CH, VA[tp][:, lc], f'VA{tp}')
                        elif c >= 2:
                            ps, pk = kv_window(hTb, hk, lc * CH, 128, extra_reads=[f'hTa{ti % 2}t'])
                            v_to(ps, pk, 128, VA[tp][:, lc], f'VA{tp}')
                        yield
                    if tchunk0 + 4 == SEQ // CH:
                        ps, pk = kv_window(hTb, hk, NT, 128)
                        kv_out(ps, pk, 0, 128, kwp, vwp)
                else:
                    for s in range(NSB):
                        ps, pk = kv_window(hTb, hk, 128 + s * SL, SL)
                        v_to(ps, pk, SL, VBs[:, s], 'VBs')
                        kv_out(ps, pk, 0, SL, kws[s, 96:128, :], vws[s, 96:128, :])
                        yield
                if sub <= 4:
                    return
                yield
                seg_stride = 30 + seglen
                for j in range(4):
                    ps, pk = bank()

                    def f(e, ps=ps, j=j):
                        last = None
                        for k in range(CW):
                            if nseg == 1:
                                rhs = gl[:, j, k:k + n]
                            else:
                                rhs = gl[:, j, 0:nseg * seg_stride].rearrange("p (s c) -> p s c", s=nseg)[:, :, k:k + seglen]
                            last = e.matmul(ps[:, 0:n], lhsT=diag[:, j * CW + k, :], rhs=rhs, start=(k == 0), stop=(k == CW - 1))
                        return last
                    S.op('pe', f, reads=[glk, 'diag'], writes=[pk])
                    S.op('act', lambda e, ps=ps, j=j: e.activation(out=cv[:, j, 0:n], in_=ps[:, 0:n], func=AF.Identity, bias=cparT[:, j, CW:CW + 1]),
                         reads=[pk, 'cparT'], writes=['cv'])
                    yield
                if kind in ('meta', 'frames'):
                    S.op('pool', lambda e: e.tensor_copy(gln[:, :, 0:30], gl[:, :, n:n + 30]), reads=[glk], writes=[glnk])
                if sub <= 5:
                    return
                yield
                S.op('dve', lambda e: e.tensor_copy(cvb[:, :, 0:n], cv[:, :, 0:n]), reads=['cv'], writes=['cvb'])
                S.op('dve', lambda e: e.tensor_tensor(out=sq[:, :, 0:n], in0=cv[:, :, 0:n], in1=cv[:, :, 0:n], op=ALU.mult), reads=['cv'], writes=['sq'])
                yield
                p1_, p1k = bank(); p2_, p2k = bank()

                def f(e):
                    last = None
                    for (ps, src) in ((p1_, cvb), (p2_, sq)):
                        for j in range(4):
                            last = e.matmul(ps[:, 0:n], lhsT=ones_bf[:], rhs=src[:, j, 0:n], start=(j == 0), stop=(j == 3))
                    return last
                S.op('pe', f, reads=['cvb', 'sq', 'ones_bf'], writes=[p1k, p2k])
                act_copy(lnm[:, 0:n], p1_[:, 0:n], [p1k], ['lnm'], scale=1.0 / DCONV)
                S.op('dve', lambda e: e.tensor_tensor(out=lnq[:, 0:n], in0=lnm[:, 0:n], in1=lnm[:, 0:n], op=ALU.mult), reads=['lnm'], writes=['lnq'])
                S.op('dve', lambda e: e.scalar_tensor_tensor(out=lnv[:, 0:n], in0=p2_[:, 0:n], scalar=1.0 / DCONV, in1=lnq[:, 0:n], op0=ALU.mult, op1=ALU.subtract),
                     reads=[p2k, 'lnq'], writes=['lnv'])
                S.op('act', lambda e: e.activation(out=lnq[:, 0:n], in_=lnv[:, 0:n], func=AF.Sqrt, bias=epsc[:, :]), reads=['lnv', 'epsc'], writes=['lnq'])
                S.op('dve', lambda e: e.reciprocal(lnr[:, 0:n], lnq[:, 0:n]), reads=['lnq'], writes=['lnr'])
                yield
                for j in range(4):
                    t_ = tn[j % 2]; tk_ = f'tn{j % 2}'
                    S.op('dve', lambda e, t_=t_, j=j: e.tensor_tensor(out=t_[:, 0:n], in0=cv[:, j, 0:n], in1=lnm[:, 0:n], op=ALU.subtract), reads=['cv', 'lnm'], writes=[tk_])
                    S.op('dve', lambda e, t_=t_: e.tensor_tensor(out=t_[:, 0:n], in0=t_[:, 0:n], in1=lnr[:, 0:n], op=ALU.mult), reads=[tk_, 'lnr'], writes=[tk_])
                    S.op('act', lambda e, t_=t_, j=j: e.activation(out=catT[:, j, 0:n], in_=t_[:, 0:n], func=AF.Silu, scale=cparT[:, j, CW + 1:CW + 2],
                                                                   bias=cparT[:, j, CW + 2:CW + 3]), reads=[tk_, 'cparT'], writes=['catT'])
                if sub <= 6:
                    return
                yield
                for fc in range(8, 12):
                    ps, pk = proj_fm(fc)
                    act_copy(qTk[0][0:64, fc - 8, 0:n], ps[0:64, 0:n], [pk], ['qT'], scale=0.125)
                    act_copy(qTk[1][64:128, fc - 8, 0:n], ps[64:128, 0:n], [pk], ['qT'], scale=0.125)
                    yield
                units = []
                if kind == 'meta':
                    units.append((0, NMETA, [(KTm[:, :], NMETA, VM, bM[:], ['KTm', 'VM'])], 0, 0))
                elif kind == 'frames':
                    tp = ti % 2
                    for lc in range(4):
                        c = tchunk0 + lc
                        blocks = []
                        if c == 1:
                            blocks.append((KTl[:, 0:CH], CH, VA[tp][:, lc], bA1[:], [f'KTl{ti}', f'VA{tp}']))
                        elif c >= 2:
                            blocks.append((KTl[:, (c - 2) * CH:c * CH], 128, VA[tp][:, lc], bA[:], [f'KTl{ti}', f'KTl{ti - 1}', f'VA{tp}']))
                        blocks.append((KB[:, c, :], 80, VB[tp][:, lc], bB[min(c, 3)][:], [f'KBo{ti}', 'KBm', f'VB{tp}']))
                        units.append((lc * CH, CH, blocks, lc * CH, lc % 2))
                else:
                    for s in range(NSB):
                        blocks = [(KTsA[:, s, :], 128, VAs[:, s], bSA[:], ['KTsA', 'VAs']),
                                  (KBs[:, s, :], 48, VBs[:, s], bSB[:], ['KBs', 'VBs'])]
                        units.append((s * SL, SL, blocks, s * SL, s % 2))
                pendB = None
                for u in units:
                    attn_A(*u)
                    yield
                    if pendB is not None:
                        attn_B(*pendB)
                        yield
                    pendB = u
                attn_B(*pendB)
                if sub <= 7:
                    return
                yield
                for b in range(nb):
                    r = min(128, n - b * 128)
                    pa, pak = bank(); pb, pbk = bank()

                    def f(e, pa=pa, pb=pb, b=b, r=r):
                        last = None
                        for (ps, half) in ((pa, 0), (pb, 1)):
                            for kc in range(8):
                                last = e.matmul(ps[0:r, :], lhsT=catT[:, kc, b * 128:b * 128 + r], rhs=Wout[:, kc, half * 512:(half + 1) * 512], start=(kc == 0), stop=(kc == 7))
                        return last
                    S.op('pe', f, reads=['catT', 'Wout'], writes=[pak, pbk])
                    S.op('dve', lambda e, pa=pa, b=b, r=r: e.tensor_tensor(out=xbuf[0:r, b, 0:512], in0=pa[0:r, :], in1=xbuf[0:r, b, 0:512], op=ALU.add), reads=[pak, xk], writes=[xk])
                    S.op('dve', lambda e, pb=pb, b=b, r=r: e.tensor_tensor(out=xbuf[0:r, b, 512:1024], in0=pb[0:r, :], in1=xbuf[0:r, b, 512:1024], op=ALU.add), reads=[pbk, xk], writes=[xk])
                    if b + 1 < nb:
                        yield
                S.dma('sp', f'o_x2{ti % NXB}', [lambda e, b=b: e.dma_start(out=x2[x2rows + b * 128:x2rows + b * 128 + min(128, n - b * 128), :],
                                                                       in_=xbuf[0:min(128, n - b * 128), b, :]) for b in range(nb)], reads=[xk], writes=['x2d'])

            def glu_out(n, off, dst):
                ps, pk = bank()

                def f(e, ps=ps):
                    last = None
                    for j in range(4):
                        last = e.transpose(ps[0:30, j * 128:(j + 1) * 128], glf[:, j, off:off + 30], identf[:])
                    return last
                S.op('pe', f, reads=['glf', 'identf'], writes=[pk])
                S.op('dve', lambda e, ps=ps: e.tensor_copy(ost[0:30, :], ps[0:30, :]), reads=[pk], writes=['ost'])
                S.dma('sp', 'o_glu', [lambda e: e.dma_start(out=dst, in_=ost[0:30, :])], reads=['ost'])

            for i in range(2):
                S.op('pool', lambda e, i=i: e.memset(qTk[i][:], 0.0), writes=['qT'])
            for i in range(2):
                S.op('pool', lambda e, i=i: e.memset(hT[i][:], 0.0), writes=[f'hTa{i}', f'hTa{i}t'])
            S.op('pool', lambda e: e.memset(glb[0][:], 0.0), writes=['glb0'])
            gens = []
            heads = []
            head_t = []
            def meta_with_bias():
                k = 0
                for _ in mixer_tile(0, 'meta', NMETA, lambda xb: [lambda e: e.dma_start(out=xb[0:NMETA, 0, :], in_=meta)], 0):
                    yield
                    k += 1
                    if bias_jobs and k >= 3:
                        bias_jobs.pop(0)()
                while bias_jobs:
                    bias_jobs.pop(0)()
            if lvl >= 2:
                gens.append(meta_with_bias)
                heads.append(lambda: head_a(0, lambda xb: [lambda e: e.dma_start(out=xb[0:NMETA, 0, :], in_=meta)], NMETA, gmix_b, 'a'))
                head_t.append(lambda: load_norm_transpose(0, None, NMETA, None, hT[0], gmix_b, 'a'))
            for t in range(SEQ // NT if lvl >= 4 else (1 if lvl >= 3 else 0)):
                gens.append(lambda t=t: mixer_tile(t + 1, 'frames', NT, lambda xb, t=t: [lambda e: e.dma_start(out=xb[:], in_=xp[t * NT:(t + 1) * NT, :].rearrange("(b p) d -> p b d", p=128))],
                                                   NMETA + t * NT, tchunk0=t * (NT // CH),
                                                   post_glu=(lambda: glu_out(NT, 0, cap)) if t == SEQ // NT - 1 else None))
                heads.append(lambda t=t: head_a(t + 1, lambda xb, t=t: [lambda e: e.dma_start(out=xb[:], in_=xp[t * NT:(t + 1) * NT, :].rearrange("(b p) d -> p b d", p=128))], NT, gmix_b, 'a'))
                head_t.append(lambda t=t: load_norm_transpose(t + 1, None, NT, None, hT[(t + 1) % 2], gmix_b, 'a'))
            ti = SEQ // NT + 1
            gls = glb[ti % 2]

            def sample_pre():
              for s in range(NSB):
                sca_s = ost[0:CW - 1, :]
                S.dma('sp', 'c_sca', [lambda e, s=s: e.dma_start(out=sca_s, in_=sca[s])], writes=['ost'])
                ps, pk = bank()

                def f(e, ps=ps, s=s, sca_s=sca_s):
                    last = None
                    for j in range(4):
                        last = e.transpose(ps[:, j * 32:j * 32 + 30], sca_s[:, j * 128:(j + 1) * 128], identf[0:30, 0:30])
                    return last
                S.op('pe', f, reads=['ost', 'identf'], writes=[pk])
                S.op('dve', lambda e, ps=ps, s=s: e.tensor_copy(gls[:, :, s * 62:s * 62 + 30], ps[:, 0:128].rearrange("p (j c) -> p j c", j=4)[:, :, 0:30]),
                     reads=[pk], writes=[f'glb{ti % 2}'])

            def sample_post_glu():
                for s in range(NSB):
                    glu_out(NSB * SL, s * SL + 2, cas[s])
            if lvl >= 5:
                gens.append(lambda: mixer_tile(ti, 'sample', NSB * SL, lambda xb: [lambda e: e.dma_start(out=xb[:, 0, :], in_=xs)], NMETA + SEQ, nseg=NSB,
                                               pre=sample_pre, post_glu=sample_post_glu))
                heads.append(lambda: head_a(ti, lambda xb: [lambda e: e.dma_start(out=xb[:, 0, :], in_=xs)], NSB * SL, gmix_b, 'a'))
                head_t.append(lambda: load_norm_transpose(ti, None, NSB * SL, None, hT[ti % 2], gmix_b, 'a'))
            PSKEW = int(_os.environ.get("KSKEW", "16"))
            active = []
            tick = 0
            nxt = 0
            last_start = -10 ** 9
            HLEAD = int(_os.environ.get("KHLEAD", "14"))
            heads_emitted = 0
            started = {}
            while active or nxt < len(gens):
                if nxt < len(gens) and len(active) < 2 and tick - last_start >= PSKEW and heads_emitted > nxt:
                    active.append((nxt, gens[nxt]())); started[nxt] = tick; nxt += 1
                    last_start = tick
                if heads_emitted < len(heads):
                    h = heads_emitted
                    if h == 0 or (h - 1 in started and tick - started[h - 1] >= HLEAD):
                        S.cur_tile = h
                        heads[h]()
                        heads_emitted += 1
                        for _ in head_t[h]():
                            pass
                for (tid, g) in (list(active)[::-1] if _os.environ.get("KORD", "0") == "1" else list(active)):
                    S.cur_tile = tid
                    try:
                        next(g)
                    except StopIteration:
                        active.remove((tid, g))
                tick += 1
            S.cur_tile = -1
            S.finish('sp')
            S.emit(block1)

        with ExitStack() as p2:
            pc = p2.enter_context

            def sb2(name, shape, dt=F32):
                return sb(name, shape, dt, ctx=pc)

            Wup = sb2("Wup", [128, 8, 2 * DFF], BF16); Wdn = sb2("Wdn", [128, NJ, D], BF16)
            xq = [sb2(f"xq{i}", [128, 2, D]) for i in range(2)]
            hb2 = [sb2(f"hc{i}", [128, D], BF16) for i in range(2)]
            h2T = [sb2(f"h2T{i}", [128, 8, NT], BF16) for i in range(2)]
            GSC = 2 + NT
            gs = [sb2(f"gs{i}", [128, GSC]) for i in range(2)]
            cacc = [sb2(f"cacc{i}", [128, NT]) for i in range(2)]
            sl = [sb2(f"sl{i}", [128, NT]) for i in range(2)]
            yT = sb2("yT", [128, NJ, NT], BF16)
            ghist = sb2("ghist", [128, NJ, 2])
            scfT = sb2("scfT", [128, NSB, NJ, 2])
            gst = sb2("gst", [128, 1 + NSB, 2, NJ])
            gso = sb2("gso", [64, 128])
            r_ss = sb2("r_ss", [128, 4]); r_sd = sb2("r_sd", [128, 4]); r_r = sb2("r_r", [128, 4])
            f_ss = sb2("f_ss", [128, 4]); f_sd = sb2("f_sd", [128, 4]); f_r = sb2("f_r", [128, 4])
            eps2 = sb2("eps2", [128, 1]); junk = sb2("junk", [128, 512])
            gffn_b = sb2("gffn_b", [128, D]); gfin_b = sb2("gfin_b", [128, D])
            fparS = yT[0:4].rearrange("p j t -> p (j t)").bitcast(F32)
            assert tuple(fparS.shape) == (4, DFF), fparS.shape
            block2 = pc(nc.Block())
            S.dma('sp', 'c_g2', [lambda e: e.dma_start(out=gffn_b[:], in_=gffn.partition_broadcast(128)),
                                 lambda e: e.dma_start(out=gfin_b[:], in_=gfin.partition_broadcast(128)),
                                 lambda e: e.dma_start(out=fparS, in_=fpar)], writes=['gb2', 'yT'])
            for j in range(NJ):
                ps, pk = bank()
                S.op('pe', lambda e, ps=ps, j=j: e.transpose(ps[:, 0:4], fparS[:, j * 128:(j + 1) * 128], identf[0:4, 0:4]),
                     reads=['gb2', 'yT', 'identf'], writes=[pk])
                act_copy(fparT[:, j, :], ps[:, 0:4], [pk], ['fparT'])
            S.op('pool', lambda e: e.memset(eps2[:], EPS), writes=['eps2'])
            S.op('pool', lambda e: e.memset(ghist[:], 0.0), writes=[f'ghist{j}' for j in range(NJ)])

            CB = 512
            ncb = (DFF + CB - 1) // CB
            wq = []
            for cbk in range(ncb if lvl >= 6 else 0):
                c0 = cbk * CB; c1 = min(DFF, c0 + CB)
                wq.append(lambda cbk=cbk, c0=c0, c1=c1: S.dma('pool', f'wg{cbk}', [lambda e, kc=kc: e.dma_start(out=Wup[:, kc, c0:c1], in_=wup[kc * 128:(kc + 1) * 128, c0:c1]) for kc in range(8)], writes=[f'Wg{cbk}']))
                wq.append(lambda cbk=cbk, c0=c0, c1=c1: S.dma('pool', f'wu{cbk}', [lambda e, kc=kc: e.dma_start(out=Wup[:, kc, DFF + c0:DFF + c1], in_=wup[kc * 128:(kc + 1) * 128, DFF + c0:DFF + c1]) for kc in range(8)], writes=[f'Wu{cbk}']))
            for j0 in range(0, NJ if lvl >= 6 else 0, 4):
                j1 = min(NJ, j0 + 4)
                wq.append(lambda j0=j0, j1=j1: S.dma('pool', f'wd{j0}', [lambda e, j=j: e.dma_start(out=Wdn[:, j, :], in_=wdn[j * 128:(j + 1) * 128, :]) for j in range(j0, j1)], writes=[f'Wd{j0 // 4}']))
            for _ in range(min(4, len(wq))):
                wq.pop(0)()
            scfS = gso[0:NJ, :]
            for s in range(NSB):
                for t in range(2):
                    S.dma('sp', 'c_scf', [lambda e, s=s, t=t: e.dma_start(out=scfS, in_=scf[s, t, :].rearrange("(j p) -> j p", p=128))], writes=['gso'])
                    ps, pk = bank()
                    S.op('pe', lambda e, ps=ps: e.transpose(ps[:, 0:NJ], scfS, identf[0:NJ, 0:NJ]), reads=['gso', 'identf'], writes=[pk])
                    act_copy(scfT[:, s, :, t], ps[:, 0:NJ], [pk], ['scfT'])

            def out_rows(r0, r):
                res = []
                a, bnd = r0, r0 + r
                for (lo, hi, dst, base) in ((NMETA, NMETA + SEQ, yp, NMETA), (NMETA + SEQ, NROWS, ys, NMETA + SEQ)):
                    s_, e_ = max(a, lo), min(bnd, hi)
                    if s_ < e_:
                        res.append((s_ - r0, e_ - s_, dst[s_ - base:e_ - base, :]))
                return res

            usb = [sb2(f"usb{i}", [128, NT], BF16) for i in range(2)]

            def ffn_head(ti, r0, n, groups):
                xbuf = xq[ti % 2]; xk = f'xq{ti % 2}'
                nb = (n + 127) // 128
                hTb = h2T[ti % 2]; hk = f'h2T{ti % 2}'
                S.dma('sp', f'xq{ti % 2}', [lambda e, b=b: e.dma_start(out=xbuf[0:min(128, n - b * 128), b, :], in_=x2[r0 + b * 128:r0 + b * 128 + min(128, n - b * 128), :]) for b in range(nb)],
                      reads=['x2d'], writes=[xk])
                for b in range(nb):
                    r = min(128, n - b * 128)
                    for hf in range(2):
                        S.op('act', lambda e, b=b, r=r, hf=hf: e.activation(out=junk[0:r, :], in_=xbuf[0:r, b, hf * 512:(hf + 1) * 512], func=AF.Square,
                                                                            accum_out=r_ss[0:r, 2 * b + hf:2 * b + hf + 1]), reads=[xk], writes=['r_ss'])
                    S.op('dve', lambda e, b=b, r=r: e.tensor_tensor(out=r_sd[0:r, b:b + 1], in0=r_ss[0:r, 2 * b:2 * b + 1], in1=r_ss[0:r, 2 * b + 1:2 * b + 2], op=ALU.add), reads=['r_ss'], writes=['r_sd'])
                    S.op('dve', lambda e, b=b, r=r: e.tensor_scalar(out=r_sd[0:r, 2 + b:3 + b], in0=r_sd[0:r, b:b + 1], scalar1=1.0 / D, scalar2=EPS, op0=ALU.mult, op1=ALU.add), reads=['r_sd'], writes=['r_sd2'])
                    S.op('pool', lambda e, b=b, r=r: e.tensor_tensor(out=r_r[0:r, b:b + 1], in0=r_sd[0:r, 2 + b:3 + b], in1=mhalf[0:r, :], op=ALU.pow), reads=['r_sd2', 'mhalf'], writes=['r_r'])
                    hbb = hb2[b % 2]
                    S.op('dve', lambda e, b=b, r=r, hbb=hbb: e.scalar_tensor_tensor(out=hbb[0:r, :], in0=xbuf[0:r, b, :], scalar=r_r[0:r, b:b + 1], in1=gffn_b[0:r, :], op0=ALU.mult, op1=ALU.mult),
                         reads=[xk, 'r_r', 'gb2'], writes=[f'hc{b % 2}'])

            def ffn_head_b(ti, r0, n, groups):
                nb = (n + 127) // 128
                hTb = h2T[ti % 2]; hk = f'h2T{ti % 2}'
                for b in range(nb):
                    r = min(128, n - b * 128)
                    hbb = hb2[b % 2]
                    pt, tk = tbank()

                    def tf(e, pt=pt, hbb=hbb, r=r):
                        last = None
                        for kc in range(8):
                            last = e.transpose(pt[:, kc * 128:kc * 128 + r], hbb[0:r, kc * 128:(kc + 1) * 128], ident[0:r, 0:r])
                        return last
                    S.op('pe', tf, reads=[f'hc{b % 2}', 'ident'], writes=[tk])
                    act_copy(hTb[:, :, b * 128:b * 128 + r], pt[:].rearrange("p (k t) -> p k t", k=8)[:, :, 0:r], [tk], [hk])

            def ffn_chunks(ti, r0, n, groups, j0, j1):
                hTb = h2T[ti % 2]; hk = f'h2T{ti % 2}'
                lay = []
                off = 0
                for (col0, nseg, seglen, hist) in groups:
                    lay.append(off); off += nseg * (2 + seglen)
                pend = []
                for j in range(j0, j1):
                    if wq:
                        wq.pop(0)()
                    pg, pgk = bank(); pu, puk = bank()
                    cbk = (j * 128) // CB

                    def f(e, pg=pg, pu=pu, j=j):
                        last = None
                        for (ps, c0) in ((pg, j * 128), (pu, DFF + j * 128)):
                            for kc in range(8):
                                last = e.matmul(ps[:, 0:n], lhsT=Wup[:, kc, c0:c0 + 128], rhs=hTb[:, kc, 0:n], start=(kc == 0), stop=(kc == 7))
                        return last
                    S.op('pe', f, reads=[hk, f'Wg{cbk}', f'Wu{cbk}'], writes=[pgk, puk])
                    g_ = gs[j % 2]; gk = f'gs{j % 2}'; ca = cacc[j % 2]; cak = f'cacc{j % 2}'; s_ = sl[j % 2]; sk = f'sl{j % 2}'
                    ub = usb[j % 2]; ubk = f'usb{j % 2}'
                    for gi, (col0, nseg, seglen, hist) in enumerate(groups):
                        base = lay[gi]
                        gv = g_[:, base:base + nseg * (2 + seglen)].rearrange("p (s c) -> p s c", s=nseg)
                        if hist == 'carry':
                            act_copy(gv[:, 0, 0:2], ghist[:, j, :], [f'ghist{j}'], [gk + 'h'])
                        else:
                            act_copy(gv[:, :, 0:2], scfT[:, :, j, :], ['scfT'], [gk + 'h'])
                        act_copy(gv[:, :, 2:2 + seglen], pg[:, col0:col0 + nseg * seglen].rearrange("p (s c) -> p s c", s=nseg), [pgk], [gk])
                        cav = ca[:, col0:col0 + nseg * seglen].rearrange("p (s c) -> p s c", s=nseg)
                        S.op('dve', lambda e, gv=gv, cav=cav, j=j, seglen=seglen: e.tensor_scalar(out=cav, in0=gv[:, :, 0:seglen], scalar1=fparT[:, j, 0:1], scalar2=fparT[:, j, 3:4], op0=ALU.mult, op1=ALU.add),
                             reads=[gk, gk + 'h', 'fparT'], writes=[cak])
                        for k in (1, 2):
                            S.op('dve', lambda e, gv=gv, cav=cav, j=j, k=k, seglen=seglen: e.scalar_tensor_tensor(out=cav, in0=gv[:, :, k:k + seglen], scalar=fparT[:, j, k:k + 1], in1=cav, op0=ALU.mult, op1=ALU.add),
                                 reads=[gk, gk + 'h', 'fparT', cak], writes=[cak])
                        if hist == 'carry' and ti < NTILES2 - 1:
                            S.op('pool', lambda e, gv=gv, j=j, seglen=seglen: e.tensor_copy(ghist[:, j, :], gv[:, 0, seglen:seglen + 2]), reads=[gk], writes=[f'ghist{j}'])
                    if ti == NTILES2 - 1:
                        gv0 = g_[:, lay[0]:lay[0] + 2 + groups[0][2]]
                        S.op('pool', lambda e, gv0=gv0, j=j: e.tensor_copy(gst[:, 0, :, j], gv0[:, groups[0][2]:groups[0][2] + 2]), reads=[gk], writes=['gst'])
                        gv1 = g_[:, lay[1]:lay[1] + NSB * (2 + SL)].rearrange("p (s c) -> p s c", s=NSB)
                        S.op('pool', lambda e, gv1=gv1, j=j: e.tensor_copy(gst[:, 1:1 + NSB, :, j], gv1[:, :, SL:SL + 2]), reads=[gk], writes=['gst'])
                    act_copy(ub[:, 0:n], pu[:, 0:n], [puk], [ubk])

                    def back(ca=ca, s_=s_, ub=ub, j=j, cak=cak, sk=sk, ubk=ubk):
                        S.op('act', lambda e: e.activation(out=s_[:, 0:n], in_=ca[:, 0:n], func=AF.Silu), reads=[cak], writes=[sk])
                        ydst = yTe[ti % 2][:, j, 0:n] if j < JE else yT[:, j, 0:n]
                        S.op(YENG, lambda e: e.tensor_tensor(out=ydst, in0=ub[:, 0:n], in1=s_[:, 0:n], op=ALU.mult), reads=[ubk, sk],
                             writes=[f'yTe{ti % 2}' if j < JE else 'yT'])
                    if pend:
                        pend.pop()()
                    pend.append(back)
                if pend:
                    pend.pop()()

            JE = int(_os.environ.get("KJE", "4"))
            yTe = [sb2(f"yTe{i}", [128, max(JE, 1), NT], BF16) for i in range(2)]
            wdb = {}
            YENG = _os.environ.get("KYENG", "pool")

            def ffn_wdown(ti, r0, n, groups, part):
                nb = (n + 127) // 128
                wdb[ti] = [(bank(), bank()) for b in range(nb)]
                ja, jb = 0, NJ
                for b in range(nb):
                    r = min(128, n - b * 128)
                    (pa, pak), (pb, pbk) = wdb[ti][b]

                    def f(e, pa=pa, pb=pb, b=b, r=r):
                        last = None
                        for (ps, half) in ((pa, 0), (pb, 1)):
                            for j in range(ja, jb):
                                ysrc = yTe[ti % 2] if j < JE else yT
                                last = e.matmul(ps[0:r, :], lhsT=ysrc[:, j, b * 128:b * 128 + r], rhs=Wdn[:, j, half * 512:(half + 1) * 512], start=(j == 0), stop=(j == NJ - 1))
                        return last
                    S.op('pe', f, reads=['yT', f'yTe{ti % 2}'] + [f'Wd{i}' for i in range((NJ + 3) // 4)], writes=[pak, pbk])

            def ffn_tail(ti, r0, n, groups):
                xbuf = xq[ti % 2]; xk = f'xq{ti % 2}'
                nb = (n + 127) // 128
                for b in range(nb):
                    r = min(128, n - b * 128)
                    (pa, pak), (pb, pbk) = wdb[ti][b]
                    S.op('dve', lambda e, pa=pa, b=b, r=r: e.tensor_tensor(out=xbuf[0:r, b, 0:512], in0=pa[0:r, :], in1=xbuf[0:r, b, 0:512], op=ALU.add), reads=[pak, xk], writes=[xk])
                    S.op('dve', lambda e, pb=pb, b=b, r=r: e.tensor_tensor(out=xbuf[0:r, b, 512:1024], in0=pb[0:r, :], in1=xbuf[0:r, b, 512:1024], op=ALU.add), reads=[pbk, xk], writes=[xk])
                    for hf in range(2):
                        S.op('act', lambda e, b=b, r=r, hf=hf: e.activation(out=junk[0:r, :], in_=xbuf[0:r, b, hf * 512:(hf + 1) * 512], func=AF.Square,
                                                                            accum_out=f_ss[0:r, 2 * b + hf:2 * b + hf + 1]), reads=[xk], writes=['f_ss'])
                    S.op('dve', lambda e, b=b, r=r: e.tensor_tensor(out=f_sd[0:r, b:b + 1], in0=f_ss[0:r, 2 * b:2 * b + 1], in1=f_ss[0:r, 2 * b + 1:2 * b + 2], op=ALU.add), reads=['f_ss'], writes=['f_sd'])
                    S.op('dve', lambda e, b=b, r=r: e.tensor_scalar(out=f_sd[0:r, 2 + b:3 + b], in0=f_sd[0:r, b:b + 1], scalar1=1.0 / D, scalar2=EPS, op0=ALU.mult, op1=ALU.add), reads=['f_sd'], writes=['f_sd2'])
                    S.op('pool', lambda e, b=b, r=r: e.tensor_tensor(out=f_r[0:r, b:b + 1], in0=f_sd[0:r, 2 + b:3 + b], in1=mhalf[0:r, :], op=ALU.pow), reads=['f_sd2', 'mhalf'], writes=['f_r'])
                    S.op('dve', lambda e, b=b, r=r: e.scalar_tensor_tensor(out=xbuf[0:r, b, :], in0=xbuf[0:r, b, :], scalar=f_r[0:r, b:b + 1], in1=gfin_b[0:r, :], op0=ALU.mult, op1=ALU.mult),
                         reads=[xk, 'f_r', 'gb2'], writes=[xk])
                    outs = out_rows(r0 + b * 128, r)
                    if outs:
                        S.dma('sp', f'o_y{ti % 2}', [lambda e, p0=p0, cnt=cnt, dst=dst, b=b: e.dma_start(out=dst, in_=xbuf[p0:p0 + cnt, b, :]) for (p0, cnt, dst) in outs], reads=[xk])

            NFULL = (NMETA + SEQ) // NT
            NTILES2 = NFULL + 1
            tail = NMETA + SEQ - NFULL * NT
            tiles2 = [(t, t * NT, NT, [(0, 1, NT, 'carry')]) for t in range(NFULL if lvl >= 9 else (1 if lvl >= 7 else 0))]
            if lvl >= 9:
                tiles2.append((NFULL, NFULL * NT, tail + NSB * SL, [(0, 1, tail, 'carry'), (tail, NSB, SL, 'state')]))
            JH = int(_os.environ.get("KJH", "8"))
            JHB = int(_os.environ.get("KJHB", "4"))
            if tiles2:
                ffn_head(*tiles2[0])
                ffn_head_b(*tiles2[0])
            for i_, tl in enumerate(tiles2):
                nx = tiles2[i_ + 1] if i_ + 1 < len(tiles2) else None
                ffn_chunks(*tl, JE if i_ > 0 else 0, JH)
                if nx:
                    ffn_head(*nx)
                ffn_chunks(*tl, JH, JH + JHB)
                if nx:
                    ffn_head_b(*nx)
                ffn_chunks(*tl, JH + JHB, NJ)
                if nx:
                    ffn_chunks(*nx, 0, JE)
                ffn_wdown(*tl, 0)
                ffn_tail(*tl)
            for g in range(1 + NSB if lvl >= 9 else 0):
                ps, pk = bank()
                S.op('pe', lambda e, ps=ps, g=g: e.transpose(ps[0:2 * NJ, 0:128], gst[:, g, :, :].rearrange("p t j -> p (t j)"), identf[:]), reads=['gst', 'identf'], writes=[pk])
                S.op('dve', lambda e, ps=ps: e.tensor_copy(gso[0:2 * NJ, :], ps[0:2 * NJ, 0:128]), reads=[pk], writes=['gso'])
                dst = cfp if g == 0 else cfs[g - 1]
                S.dma('sp', 'o_gst', [lambda e, t=t, dst=dst: e.dma_start(out=dst[t, :].rearrange("(j p) -> j p", p=128), in_=gso[t * NJ:(t + 1) * NJ, :]) for t in range(2)], reads=['gso'])
            S.finish('sp')
            S.emit(block2)
    return nc


def _rel_bucket_np(rel):
    nb = 16
    max_exact = 8
    ret = np.where(rel > 0, nb, 0)
    n = np.abs(rel)
    nf = np.maximum(n, 1).astype(np.float32)
    large = max_exact + (np.log(nf / np.float32(max_exact)) / np.float32(math.log(256 / max_exact)) * np.float32(nb - max_exact)).astype(np.int32)
    large = np.minimum(large, nb - 1)
    return ret + np.where(n < max_exact, n, large)


_CACHE = {}


def kernel(x_prompt, x_sample, cache_k_meta, cache_v_meta, cache_k_win, cache_v_win, state_conv_a, state_conv_ffn,
           meta_tokens, rel_bias_table, norm_mix, w_in, conv_dw_w, conv_dw_b, conv_ln_g, conv_ln_b, attn_sinks, w_out,
           norm_ffn, w_up, ffn_dw_w, ffn_dw_b, w_down, norm_final):
    f = lambda a: np.ascontiguousarray(np.asarray(a, dtype=np.float32))
    x_prompt = f(x_prompt); x_sample = f(x_sample)
    w_in0 = f(w_in)[0]; w_out0 = f(w_out)[0]
    qcols = []
    for j in range(4):
        qcols += list(range(1024 + 64 * j, 1024 + 64 * j + 64)) + list(range(1024 + 64 * (j + 4), 1024 + 64 * (j + 4) + 64))
    cols = list(range(1024)) + qcols + list(range(1536, 1792))
    win_p = np.ascontiguousarray(w_in0[:, cols])
    rows = list(range(512))
    for j in range(4):
        rows += list(range(512 + 64 * j, 512 + 64 * j + 64)) + list(range(512 + 64 * (j + 4), 512 + 64 * (j + 4) + 64))
    wout_p = np.ascontiguousarray(w_out0[rows, :])
    cpar = np.ascontiguousarray(np.concatenate([f(conv_dw_w)[0], f(conv_dw_b), f(conv_ln_g), f(conv_ln_b)], axis=0))
    fpar = np.ascontiguousarray(np.concatenate([f(ffn_dw_w)[0], f(ffn_dw_b)], axis=0))
    rel = np.arange(E_LEN, dtype=np.int32) - E_OFF
    bkt = _rel_bucket_np(rel)
    eoh = (bkt[None, :] == np.arange(32)[:, None]).astype(np.float32)
    shared = {
        "meta": f(meta_tokens), "win": win_p, "wout": wout_p, "wup": f(w_up)[0], "wdn": f(w_down)[0],
        "gmix": f(norm_mix)[0], "gffn": f(norm_ffn)[0], "gfin": f(norm_final), "cpar": cpar, "fpar": fpar,
        "sinks": f(attn_sinks)[0], "tab": f(rel_bias_table), "eoh": eoh,
    }
    ckm = f(cache_k_meta)[0].reshape(32, NMETA, 128); cvm = f(cache_v_meta)[0].reshape(32, NMETA, 128)
    ckw = f(cache_k_win)[0].reshape(32, 128, 128); cvw = f(cache_v_win)[0].reshape(32, 128, 128)
    sca = f(state_conv_a)[0]; scf = f(state_conv_ffn)[0]
    in_maps = []
    for c in range(8):
        m = dict(shared)
        sl_ = slice(NSB * c, NSB * (c + 1))
        m.update({"xp": x_prompt[c], "xs": np.ascontiguousarray(x_sample[sl_].reshape(NSB * SL, D)),
                  "ckm": ckm[sl_], "cvm": cvm[sl_], "ckw": ckw[sl_], "cvw": cvw[sl_], "sca": sca[sl_], "scf": scf[sl_]})
        in_maps.append({k: np.ascontiguousarray(v) for k, v in m.items()})
    if _CACHE.get("maps_only"):
        return in_maps
    if "nc" not in _CACHE:
        _CACHE["nc"] = build_program()
    res = run_bass_kernel_spmd(_CACHE["nc"], in_maps, core_ids=list(range(8)))
    R = res.results
    cat = lambda k: np.stack([np.asarray(R[c][k], dtype=np.float32) for c in range(8)], axis=0)
    y_prompt = cat("yp")
    y_sample = cat("ys").reshape(32, SL, D)
    k_meta_p = cat("kmp").reshape(1, 8, NMETA, 2, 64); v_meta_p = cat("vmp").reshape(1, 8, NMETA, 2, 64)
    k_win_p = cat("kwp").reshape(1, 8, 128, 2, 64); v_win_p = cat("vwp").reshape(1, 8, 128, 2, 64)
    conv_a_p = cat("cap").reshape(1, 8, CW - 1, DCONV); conv_ffn_p = cat("cfp").reshape(1, 8, 2, DFF)
    k_win_s = cat("kws").reshape(1, 32, 128, 2, 64); v_win_s = cat("vws").reshape(1, 32, 128, 2, 64)
    conv_a_s = cat("cas").reshape(1, 32, CW - 1, DCONV); conv_ffn_s = cat("cfs").reshape(1, 32, 2, DFF)
    return (y_prompt, y_sample, k_meta_p, v_meta_p, k_win_p, v_win_p, conv_a_p, conv_ffn_p,
            k_win_s, v_win_s, conv_a_s, conv_ffn_s)
```

```python
import math
from contextlib import ExitStack

import numpy as np
import concourse.bass as bass
import concourse.mybir as mybir
from concourse.bass_utils import run_bass_kernel_spmd

F32 = mybir.dt.float32
BF16 = mybir.dt.bfloat16
AF = mybir.ActivationFunctionType
ALU = mybir.AluOpType

D = 1024
SEQ = 2048
NMETA = 16
CH = 64
DCONV = 512
CW = 31
DFF = 2816
NJ = DFF // 128
EPS = 1e-6
NT = 256
NSB = 4
SL = 32
NROWS = NMETA + SEQ + NSB * SL
E_OFF = 271
E_LEN = 335


class Sched:
    def __init__(self, nc, sems):
        self.nc = nc
        self.free_sems = list(sems)
        self.prog = {k: [] for k in ('pe', 'act', 'dve', 'pool', 'sp')}
        self.esem = {k: self.free_sems.pop() for k in ('pe', 'act', 'dve', 'pool')}
        self.count = {k: 0 for k in self.esem}
        self.seen = {k: {} for k in self.prog}
        self.lastw = {}
        self.readers = {}
        self.slotsem = {}
        self.slotcnt = {}
        self.semid = {}
        self.cur_tile = -1
        self.wtile = {}

    def _deps(self, reads, writes, eng=None):
        deps = []
        mysem = self.esem.get(eng)
        for r in reads:
            if r in self.lastw:
                deps.append(self.lastw[r])
            if r[:2] in ('ps', 'pt'):
                deps.extend(t for t in self.readers.get(r, []) if t[0] is not mysem)
        for w in writes:
            if w in self.lastw:
                deps.append(self.lastw[w])
            deps.extend(self.readers.get(w, []))
        return deps

    def _emit_waits(self, eng, deps):
        need = {}
        for (sem, val) in deps:
            k = id(sem)
            self.semid[k] = sem
            if self.seen[eng].get(k, 0) >= val:
                continue
            need[k] = max(need.get(k, 0), val)
        for k, val in need.items():
            self.seen[eng][k] = val
            sem = self.semid[k]
            self.prog[eng].append(lambda e, sem=sem, val=val: e.wait_ge(sem, val))

    def _record(self, reads, writes, tok):
        for r in reads:
            assert self.wtile.get(r, -1) <= self.cur_tile or self.cur_tile < 0, (r, self.wtile.get(r), self.cur_tile)
        for w in writes:
            self.wtile[w] = self.cur_tile
        for w in writes:
            self.lastw[w] = tok
            self.readers[w] = []
        for r in reads:
            if r not in writes:
                self.readers.setdefault(r, []).append(tok)

    def op(self, eng, fn, reads=(), writes=()):
        self._emit_waits(eng, self._deps(reads, writes, eng))
        sem = self.esem[eng]
        self.count[eng] += 1
        val = self.count[eng]
        self.prog[eng].append(lambda e, fn=fn, sem=sem: fn(e).then_inc(sem, 1))
        self._record(reads, writes, (sem, val))

    def dma(self, q, slot, fn_list, reads=(), writes=()):
        if slot not in self.slotsem:
            self.slotsem[slot] = self.free_sems.pop()
            self.slotcnt[slot] = 0
        sem = self.slotsem[slot]
        self._emit_waits(q, self._deps(reads, writes))
        for fn in fn_list:
            self.slotcnt[slot] += 16
            self.prog[q].append(lambda e, fn=fn, sem=sem: fn(e).then_inc(sem, 16))
        self._record(reads, writes, (sem, self.slotcnt[slot]))

    def finish(self, eng='sp'):
        deps = [(s, self.slotcnt[k]) for k, s in self.slotsem.items()]
        deps += [(self.esem[k], self.count[k]) for k in self.esem if self.count[k] > 0]
        self._emit_waits(eng, deps)

    def emit(self, block):
        progs = self.prog
        self.prog = {k: [] for k in progs}

        @block.tensor
        def _(e):
            for f in progs['pe']:
                f(e)

        @block.scalar
        def _(e):
            for f in progs['act']:
                f(e)

        @block.vector
        def _(e):
            for f in progs['dve']:
                f(e)

        @block.gpsimd
        def _(e):
            for f in progs['pool']:
                f(e)

        @block.sync
        def _(e):
            for f in progs['sp']:
                f(e)


def build_program(stop=None):
    nc = bass.Bass("TRN2", target_bir_lowering=False)
    import os as _os
    sub = int(_os.environ.get("KSUB", "99"))
    lvl = {None: 9, 'setup0': 0, 'setup': 1, 'meta': 2, 'f1': 3, 'frames': 4, 'p1': 5, 'w2': 6, 't2': 7}[stop]

    def din(name, shape):
        return nc.dram_tensor(name, list(shape), F32, kind="ExternalInput").ap()

    def dout(name, shape):
        return nc.dram_tensor(name, list(shape), F32, kind="ExternalOutput").ap()

    xp = din("xp", [SEQ, D]); xs = din("xs", [NSB * SL, D]); meta = din("meta", [NMETA, D])
    ckm = din("ckm", [NSB, NMETA, 128]); cvm = din("cvm", [NSB, NMETA, 128])
    ckw = din("ckw", [NSB, 128, 128]); cvw = din("cvw", [NSB, 128, 128])
    sca = din("sca", [NSB, CW - 1, DCONV]); scf = din("scf", [NSB, 2, DFF])
    win = din("win", [D, 1792]); wout = din("wout", [D, D]); wup = din("wup", [D, 2 * DFF]); wdn = din("wdn", [DFF, D])
    gmix = din("gmix", [D]); gffn = din("gffn", [D]); gfin = din("gfin", [D])
    cpar = din("cpar", [CW + 3, DCONV])
    fpar = din("fpar", [4, DFF])
    sinks = din("sinks", [8]); tab = din("tab", [32, 8]); eoh = din("eoh", [32, E_LEN])
    yp = dout("yp", [SEQ, D]); ys = dout("ys", [NSB * SL, D])
    kmp = dout("kmp", [NMETA, 128]); vmp = dout("vmp", [NMETA, 128])
    kwp = dout("kwp", [128, 128]); vwp = dout("vwp", [128, 128])
    cap = dout("cap", [CW - 1, DCONV]); cfp = dout("cfp", [2, DFF])
    kws = dout("kws", [NSB, 128, 128]); vws = dout("vws", [NSB, 128, 128])
    cas = dout("cas", [NSB, CW - 1, DCONV]); cfs = dout("cfs", [NSB, 2, DFF])
    x2 = nc.dram_tensor("x2", [NROWS, D], F32).ap()

    with ExitStack() as outer:
        ec = outer.enter_context
        sems = [ec(nc.semaphore(f"s{i}")) for i in range(60)]
        S = Sched(nc, sems)

        def sb(name, shape, dt=F32, ctx=None):
            return (ctx or ec)(nc.sbuf_tensor(name, list(shape), dt))

        NBANK = 8
        psb = [ec(nc.psum_tensor(f"ps{i}", [128, 512], F32)) for i in range(NBANK)]
        rr = {'b': 0}

        def bank():
            i = rr['b']; rr['b'] = (i + 1) % NBANK
            return psb[i], f"ps{i}"

        def tbank():
            ps, pk = bank()
            return ps[:].bitcast(BF16), pk

        identf = sb("identf", [128, 128]); ident = sb("ident", [128, 128], BF16)
        ones_bf = sb("ones_bf", [128, 128], BF16)
        gmix_b = sb("gmix_b", [128, D])
        cparT = sb("cparT", [128, 4, CW + 3])
        fparT = sb("fparT", [128, NJ, 4])
        epsc = sb("epsc", [128, 1]); mhalf = sb("mhalf", [128, 1])
        S.op('pool', lambda e: e.memset(mhalf[:], -0.5), writes=['mhalf'])
        S.op('pool', lambda e: e.memset(identf[:], 0.0), writes=['identf'])
        S.op('pool', lambda e: e.affine_select(out=identf[:], in_=identf[:], pattern=[[-1, 128]], compare_op=ALU.not_equal,
                                               fill=1.0, base=0, channel_multiplier=1), reads=['identf'], writes=['identf'])
        S.op('pool', lambda e: e.tensor_copy(ident[:], identf[:]), reads=['identf'], writes=['ident'])
        S.op('pool', lambda e: e.memset(ones_bf[:], 1.0), writes=['ones_bf'])
        S.dma('sp', 'c_g', [lambda e: e.dma_start(out=gmix_b[:], in_=gmix.partition_broadcast(128))], writes=['gb'])

        def act_copy(out, in_, reads, writes, scale=None):
            if scale is None:
                S.op('act', lambda e: e.activation(out=out, in_=in_, func=AF.Copy), reads=reads, writes=writes)
            else:
                S.op('act', lambda e: e.activation(out=out, in_=in_, func=AF.Copy, scale=scale), reads=reads, writes=writes)

        with ExitStack() as p1:
            pc = p1.enter_context

            def sb1(name, shape, dt=F32):
                return sb(name, shape, dt, ctx=pc)

            Win = sb1("Win", [128, 8, 1792], BF16); Wout = sb1("Wout", [128, 8, D], BF16)
            diag = sb1("diag", [128, 4 * CW, 128], BF16)
            XLQ = _os.environ.get("KXLQ", "pool")
            NXB = 3
            xt = [sb1(f"xt{i}", [128, 2, D]) for i in range(NXB)]
            hb = [sb1(f"hb{i}", [128, D], BF16) for i in range(2)]
            HTC = 128 + NT
            hT = [sb1(f"hT{i}", [128, 8, HTC], BF16) for i in range(2)]
            GLC = 30 + NT
            glb = [sb1(f"glb{i}", [128, 4, GLC], BF16) for i in range(2)]
            sA = sb1("sA", [128, 2, 256]); sBt = sb1("sBt", [128, 2, 256])
            glf = sb1("glf", [128, 4, NSB * SL])
            junk1 = glf[:].rearrange("p j c -> p (j c)")
            sig = [sb1(f"sig{i}", [128, NT]) for i in range(2)]
            cv = sb1("cv", [128, 4, NT]); cvb = sb1("cvb", [128, 4, NT], BF16); sq = sb1("sq", [128, 4, NT], BF16)
            lnm = sb1("lnm", [128, NT]); lnq = sb1("lnq", [128, NT]); lnv = sb1("lnv", [128, NT]); lnr = sb1("lnr", [128, NT])
            tn = [sb1(f"tn{i}", [128, NT]) for i in range(2)]
            qTk = [sb1(f"qT{i}", [128, 4, NT], BF16) for i in range(2)]
            KTl = sb1("KTl", [128, SEQ], BF16)
            KB = sb1("KB", [128, SEQ // CH, 80], BF16)
            KTm = sb1("KTm", [128, NMETA], BF16)
            VA = [sb1(f"VA{i}", [128, 4, 2, 128], BF16) for i in range(2)]
            VB = [sb1(f"VB{i}", [128, 4, 2, 128], BF16) for i in range(2)]
            VM = sb1("VM", [128, 2, 128], BF16)
            catT = sb1("catT", [128, 8, NT], BF16)
            pA = [sb1(f"pA{i}", [128, 2, 256], BF16) for i in range(2)]
            pB = [sb1(f"pB{i}", [128, 2, 256], BF16) for i in range(2)]
            rec = sb1("rec", [128, 2, 256])
            kvst = [sb1("kvst0", [128, 256])] * 2
            ost = sb1("ost", [32, DCONV])
            Eb = ost[:, E_LEN + 1:DCONV].bitcast(BF16)[:, 0:E_LEN]
            rs_ss = sb1("rs_ss", [128, 4]); rs_sd = sb1("rs_sd", [128, 4]); rs_r = sb1("rs_r", [128, 4])
            bA = sb1("bA", [128, 2, 4, 64]); bA1 = sb1("bA1", [64, 2, 4, 64])
            bB = [sb1(f"bB{c}", [80, 2, 4, 64]) for c in range(4)]
            bM = sb1("bM", [16, 2, 4, 16])
            bSA = sb1("bSA", [128, 2, 4, SL]); bSB = sb1("bSB", [48, 2, 4, SL])
            tabf = sb1("tabf", [32, 8]); tabd = sb1("tabd", [32, 8]); tab3 = sb1("tab3", [32, 3, 8], BF16)
            skf = sb1("skf", [1, 8]); ske = sb1("ske", [1, 8]); skd = sb1("skd", [1, 8]); sk3 = sb1("sk3", [1, 3, 8], BF16)
            sk3r = sb1("sk3r", [1, 3, 8, 64], BF16)
            msk = sb1("msk", [1, 2, 128], BF16)
            cparS = rec[0:CW + 3].rearrange("p k n -> p (k n)")
            ckw_b = sb1("ckw_b", [128, NSB, 128], BF16); ckm_b = sb1("ckm_b", [16, NSB, 128], BF16)
            ckw_s = cv[:, 0:2, :].rearrange("p a (b f) -> p (a b) f", b=2)
            cvw_s = cv[:, 2:4, :].rearrange("p a (b f) -> p (a b) f", b=2)
            ckm_s = sA[0:16].rearrange("p a (b f) -> p (a b) f", b=2)
            cvm_s = sBt[:].rearrange("p a (b f) -> p (a b) f", b=2)
            KTsA = sb1("KTsA", [128, NSB, 128], BF16); KBs = sb1("KBs", [128, NSB, 48], BF16)
            VAs = sb1("VAs", [128, NSB, 2, 128], BF16); VBs = sb1("VBs", [128, NSB, 2, 128], BF16)
            block1 = pc(nc.Block())

            late_setup = []

            def win_group(wkey, c0_, c1_):
                S.dma('pool', 'w_in' + wkey, [lambda e, kc=kc: e.dma_start(out=Win[:, kc, c0_:c1_], in_=win[kc * 128:(kc + 1) * 128, c0_:c1_]) for kc in range(8)], writes=[wkey])
            win_group('WinG', 0, 2 * DCONV)
            win_group('WinK', 1536, 1792)
            late_setup.append(lambda: win_group('WinQ', 2 * DCONV, 1536))
            S.op('dve', lambda e: e.tensor_scalar(out=Win[:, :, 0:DCONV], in0=Win[:, :, 0:DCONV], scalar1=0.5, scalar2=None, op0=ALU.mult), reads=['WinG'], writes=['WinG'])
            late_setup.append(lambda: S.dma('pool', 'w_out', [lambda e, kc=kc: e.dma_start(out=Wout[:, kc, :], in_=wout[kc * 128:(kc + 1) * 128, :]) for kc in range(8)], writes=['Wout']))
            S.dma('sp', 'c_par2', [lambda e: e.dma_start(out=cparS, in_=cpar)], writes=['rec0', 'rec1'])
            S.dma('sp', 'c_par', [
                                  lambda e: e.dma_start(out=ost[0:32, 0:E_LEN], in_=eoh), lambda e: e.dma_start(out=tabf[:], in_=tab),
                                  lambda e: e.dma_start(out=skf[:], in_=sinks.rearrange("(o h) -> o h", o=1))], writes=['par', 'ost'])
            S.dma('sp', 'c_cache', [lambda e: e.dma_start(out=ckw_s, in_=ckw.rearrange("b t f -> t b f")),
                                    lambda e: e.dma_start(out=ckm_s, in_=ckm.rearrange("b t f -> t b f")),
                                    lambda e: e.dma_start(out=cvw_s, in_=cvw.rearrange("b t f -> t b f")),
                                    lambda e: e.dma_start(out=cvm_s[32:48, :, :], in_=cvm.rearrange("b t f -> t b f"))], writes=['cv', 'sA', 'sBt'])
            S.dma('sp', 'o_roll', [lambda e: e.dma_start(out=kws[:, 0:96, :], in_=ckw[:, 32:128, :]),
                                   lambda e: e.dma_start(out=vws[:, 0:96, :], in_=cvw[:, 32:128, :])])

            for j in range(4):
                ps, pk = bank()
                S.op('pe', lambda e, ps=ps, j=j: e.transpose(ps[:, 0:CW + 3], cparS[:, j * 128:(j + 1) * 128], identf[0:CW + 3, 0:CW + 3]),
                     reads=['rec0', 'rec1', 'identf'], writes=[pk])
                act_copy(cparT[:, j, :], ps[:, 0:CW + 3], [pk], ['cparT'])
            def build_diag():
                for j in range(4):
                    S.op('dve', lambda e, j=j: e.tensor_tensor(out=diag[:, j * CW:(j + 1) * CW, :],
                                                               in0=identf[:].unsqueeze(1).broadcast_to([128, CW, 128]),
                                                               in1=cparT[:, j, 0:CW].unsqueeze(2).broadcast_to([128, CW, 128]), op=ALU.mult),
                         reads=['identf', 'cparT'], writes=['diag'])
            def split3(dst3, src, tmp, rkey, wkey):
                S.op('dve', lambda e: e.tensor_copy(tmp, src), reads=[rkey], writes=[wkey + 't'])
                for i in range(3):
                    S.op('dve', lambda e, i=i: e.tensor_copy(dst3[:, i, :], tmp), reads=[wkey + 't'], writes=[wkey])
                    if i < 2:
                        S.op('dve', lambda e, i=i: e.tensor_tensor(out=tmp, in0=tmp, in1=dst3[:, i, :], op=ALU.subtract), reads=[wkey + 't', wkey], writes=[wkey + 't'])
            S.op('act', lambda e: e.activation(out=ske[:], in_=skf[:], func=AF.Exp), reads=['par'], writes=['ske'])
            split3(sk3, ske[:], skd[:], 'ske', 'sk3')
            S.op('dve', lambda e: e.tensor_copy(Eb, ost[0:32, 0:E_LEN]), reads=['par', 'ost'], writes=['Eb', 'ost'])
            split3(tab3, tabf[:], tabd[:], 'par', 'tab3')
            S.op('dve', lambda e: e.tensor_copy(sk3r[:], sk3[:].unsqueeze(3).broadcast_to([1, 3, 8, 64])), reads=['sk3'], writes=['sk3r'])
            S.op('pool', lambda e: e.memset(msk[:], 0.0), writes=['msk'])
            S.op('pool', lambda e: e.memset(msk[:, 0, 64:128], 1.0), reads=['msk'], writes=['msk'])
            S.op('pool', lambda e: e.memset(msk[:, 1, 0:64], 1.0), reads=['msk'], writes=['msk'])

            NSPLIT = 2

            def gen_bias(dst, parts, nq):
                ps, pk = bank()
                psv = ps[:, 0:nq * 8].rearrange("p (q h) -> p q h", h=8)

                def f(e):
                    last = None
                    for (r0, M, s0) in parts:
                        for i in range(nq):
                            for t3 in range(NSPLIT):
                                last = e.matmul(psv[r0:r0 + M, i, :], lhsT=Eb[:, s0 - i:s0 - i + M], rhs=tab3[:, t3, :], start=(t3 == 0), stop=(t3 == NSPLIT - 1))
                    return last
                S.op('pe', f, reads=['Eb', 'tab3', 'ost'], writes=[pk])
                rows = max(r0 + M for (r0, M, s0) in parts)
                S.op('dve', lambda e: e.tensor_copy(dst[0:rows].rearrange("p k h q -> p (k h) q"), psv[0:rows].rearrange("p q h -> p h q")),
                     reads=[pk], writes=['bias'])

            gen_bias(bM, [(0, 16, E_OFF)], 16)
            bias_jobs = [lambda: gen_bias(bB[0], [(0, 64, E_OFF), (64, 16, E_OFF - 16)], 64),
                         lambda: gen_bias(bA1, [(0, 64, E_OFF - 64)], 64),
                         lambda: gen_bias(bB[1], [(0, 64, E_OFF), (64, 16, E_OFF - 16 - 64)], 64),
                         lambda: gen_bias(bA, [(0, 128, E_OFF - 128)], 64),
                         lambda: gen_bias(bB[2], [(0, 64, E_OFF), (64, 16, E_OFF - 16 - 128)], 64),
                         lambda: gen_bias(bB[3], [(0, 64, E_OFF), (64, 16, E_OFF - 16 - 192)], 64),
                         lambda: gen_bias(bSA, [(0, 128, E_OFF - 128)], SL),
                         lambda: gen_bias(bSB, [(0, 32, E_OFF), (32, 16, SL)], SL)]

            for t_ in (VAs, VBs):
                S.op('pool', lambda e, t_=t_: e.memset(t_[:], 1.0), writes=['Vinit'])

            def late_vinit():
                for t_ in (VA[0], VA[1], VB[0], VB[1], VM):
                    S.op('pool', lambda e, t_=t_: e.memset(t_[:], 1.0), writes=['Vinit'])
            late_setup.append(late_vinit)

            S.op('dve', lambda e: e.tensor_copy(ckw_b[:], ckw_s), reads=['cv'], writes=['ckw_b'])
            S.op('dve', lambda e: e.tensor_copy(ckm_b[:], ckm_s), reads=['sA'], writes=['ckm_b'])
            for s in range(NSB):
                pt, tk = tbank()
                S.op('pe', lambda e, pt=pt, s=s: e.transpose(pt[:, 0:128], ckw_b[:, s, :], ident[:]), reads=['ckw_b', 'ident'], writes=[tk])
                act_copy(KTsA[:, s, :], pt[:, 0:128], [tk], ['KTsA'])
                pt, tk = tbank()
                S.op('pe', lambda e, pt=pt, s=s: e.transpose(pt[:, 0:16], ckm_b[:, s, :], ident[0:16, 0:16]), reads=['ckm_b', 'ident'], writes=[tk])
                act_copy(KBs[:, s, 32:48], pt[:, 0:16], [tk], ['KBs'])
            S.op('dve', lambda e: e.tensor_copy(VAs[:, :, 0, 0:64], cvw_s[:, :, 0:64]), reads=['cv', 'Vinit'], writes=['VAs'])
            S.op('dve', lambda e: e.tensor_copy(VAs[:, :, 1, 64:128], cvw_s[:, :, 64:128]), reads=['cv', 'VAs'], writes=['VAs'])
            S.op('dve', lambda e: e.tensor_copy(VBs[32:48, :, 0, 0:64], cvm_s[32:48, :, 0:64]), reads=['sBt', 'Vinit'], writes=['VBs'])
            S.op('dve', lambda e: e.tensor_copy(VBs[32:48, :, 1, 64:128], cvm_s[32:48, :, 64:128]), reads=['sBt', 'VBs'], writes=['VBs'])

            def head_a(ti, src_fn, n, gb, keyp):
                xbuf = xt[ti % NXB]
                nb = (n + 127) // 128
                S.dma(XLQ, f'x{keyp}{ti % NXB}', src_fn(xbuf), writes=[f'xt{keyp}{ti % NXB}'])
                xk = f'xt{keyp}{ti % NXB}'
                for b in range(nb):
                    r = min(128, n - b * 128)
                    S.op('act', lambda e, b=b, r=r: e.activation(out=junk1[0:r, :], in_=xbuf[0:r, b, 0:512], func=AF.Square,
                                                                 accum_out=rs_ss[0:r, 2 * b:2 * b + 1]), reads=[xk], writes=['rs_ss', 'glf'])
                    S.op('act', lambda e, b=b, r=r: e.activation(out=junk1[0:r, :], in_=xbuf[0:r, b, 512:1024], func=AF.Square,
                                                                 accum_out=rs_ss[0:r, 2 * b + 1:2 * b + 2]), reads=[xk], writes=['rs_ss', 'glf'])
                    S.op('dve', lambda e, b=b, r=r: e.tensor_tensor(out=rs_sd[0:r, b:b + 1], in0=rs_ss[0:r, 2 * b:2 * b + 1],
                                                                    in1=rs_ss[0:r, 2 * b + 1:2 * b + 2], op=ALU.add), reads=['rs_ss'], writes=['rs_sd'])
                    S.op('dve', lambda e, b=b, r=r: e.tensor_scalar(out=rs_sd[0:r, 2 + b:3 + b], in0=rs_sd[0:r, b:b + 1], scalar1=1.0 / D, scalar2=EPS,
                                                                    op0=ALU.mult, op1=ALU.add), reads=['rs_sd'], writes=['rs_sd2'])
                    S.op('pool', lambda e, b=b, r=r: e.tensor_tensor(out=rs_r[0:r, b:b + 1], in0=rs_sd[0:r, 2 + b:3 + b], in1=mhalf[0:r, :], op=ALU.pow),
                         reads=['rs_sd2', 'mhalf'], writes=['rs_r'])
                    hbb = hb[b % 2]
                    S.op('dve', lambda e, b=b, r=r, hbb=hbb: e.scalar_tensor_tensor(out=hbb[0:r, :], in0=xbuf[0:r, b, :], scalar=rs_r[0:r, b:b + 1],
                                                                                    in1=gb[0:r, :], op0=ALU.mult, op1=ALU.mult),
                         reads=[xk, 'rs_r', 'gb'], writes=[f'hb{b % 2}'])

            def load_norm_transpose(ti, src_fn, n, xbuf, hTb, gb, keyp):
                nb = (n + 127) // 128
                for b in range(nb):
                    r = min(128, n - b * 128)
                    hbb = hb[b % 2]
                    if b > 0:
                        yield
                    pt, tk = tbank()

                    def tf(e, pt=pt, hbb=hbb, r=r):
                        last = None
                        for kc in range(8):
                            last = e.transpose(pt[:, kc * 128:kc * 128 + r], hbb[0:r, kc * 128:(kc + 1) * 128], ident[0:r, 0:r])
                        return last
                    S.op('pe', tf, reads=[f'hb{b % 2}', 'ident'], writes=[tk])
                    act_copy(hTb[:, :, 128 + b * 128:128 + b * 128 + r], pt[:].rearrange("p (k t) -> p k t", k=8)[:, :, 0:r], [tk], [f'hT{keyp}{ti % 2}'])

            S.op('pool', lambda e: e.memset(epsc[:], EPS), writes=['epsc'])

            def attn_A(qcols, nq, blocks, cat_cols, pidx):
                N = 4 * nq
                stiles = [sA, sBt]
                ptiles = [pA[pidx], pB[pidx]]
                pkeys = [f'pA{pidx}', f'pB{pidx}']
                skeys = ['sA', 'sBt']
                for bi, (KTap, M, VEap, bias_ap, rkeys) in enumerate(blocks):
                    ps, pk = bank()

                    def sf(e, ps=ps, KTap=KTap, M=M):
                        last = None
                        for kv in range(2):
                            last = e.matmul(ps[0:M, kv * 256:kv * 256 + N], lhsT=KTap,
                                            rhs=qTk[kv][:, :, qcols:qcols + nq], start=True, stop=True)
                        return last
                    S.op('pe', sf, reads=['qT'] + rkeys, writes=[pk])
                    st = stiles[bi]
                    S.op('dve', lambda e, ps=ps, st=st, M=M, bias_ap=bias_ap: e.tensor_tensor(
                        out=st[0:M, :, 0:N], in0=ps[0:M, :].rearrange("p (k n) -> p k n", k=2)[:, :, 0:N],
                        in1=bias_ap.rearrange("p k h q -> p k (h q)"), op=ALU.add), reads=[pk, 'bias'], writes=[skeys[bi]])
                    S.op('act', lambda e, st=st, M=M, pt_=ptiles[bi]: e.activation(out=pt_[0:M, :, 0:N], in_=st[0:M, :, 0:N], func=AF.Exp),
                         reads=[skeys[bi]], writes=[pkeys[bi]])

            def attn_B(qcols, nq, blocks, cat_cols, pidx):
                N = 4 * nq
                ptiles = [pA[pidx], pB[pidx]]
                pkeys = [f'pA{pidx}', f'pB{pidx}']
                po, pok = bank()

                def of(e, po=po):
                    last = None
                    for kv in range(2):
                        for bi, (KTap, M, VEap, bias_ap, rkeys) in enumerate(blocks):
                            e.matmul(po[:, kv * 256:kv * 256 + N], lhsT=VEap[0:M, kv, :], rhs=ptiles[bi][0:M, kv, 0:N], start=(bi == 0), stop=False)
                        for t3 in range(3):
                            last = e.matmul(po[:, kv * 256:kv * 256 + N], lhsT=msk[:, kv, :], rhs=sk3r[:, t3, kv * 4:kv * 4 + 4, 0:nq], start=False, stop=(t3 == 2))
                    return last
                vr = []
                for b_ in blocks:
                    vr += b_[4]
                S.op('pe', of, reads=[pkeys[i] for i in range(len(blocks))] + vr + ['msk', 'sk3r'], writes=[pok])
                S.op('dve', lambda e, po=po: e.reciprocal(rec[64:128, 0, 0:N], po[64:128, 0:N]), reads=[pok], writes=['rec0'])
                S.op('dve', lambda e, po=po: e.reciprocal(rec[0:64, 1, 0:N], po[0:64, 256:256 + N]), reads=[pok], writes=['rec1'])
                S.op('dve', lambda e, po=po: e.tensor_tensor(out=catT[0:64, 4:8, cat_cols:cat_cols + nq],
                                                             in0=po[0:64, 0:N].rearrange("p (h q) -> p h q", h=4),
                                                             in1=rec[64:128, 0, 0:N].rearrange("p (h q) -> p h q", h=4), op=ALU.mult),
                     reads=[pok, 'rec0'], writes=['catT'])
                S.op('dve', lambda e, po=po: e.tensor_tensor(out=catT[64:128, 4:8, cat_cols:cat_cols + nq],
                                                             in0=po[64:128, 256:256 + N].rearrange("p (h q) -> p h q", h=4),
                                                             in1=rec[0:64, 1, 0:N].rearrange("p (h q) -> p h q", h=4), op=ALU.mult),
                     reads=[pok, 'rec1'], writes=['catT'])

            def kv_window(hTb, hk, c0, M, extra_reads=()):
                ps, pk = bank()

                def f(e, ps=ps):
                    last = None
                    for kc in range(8):
                        last = e.matmul(ps[0:M, 0:256], lhsT=hTb[:, kc, c0:c0 + M], rhs=Win[:, kc, 1536:1792], start=(kc == 0), stop=(kc == 7))
                    return last
                S.op('pe', f, reads=[hk, 'WinK'] + list(extra_reads), writes=[pk])
                return ps, pk

            def v_to(ps, pk, M, dst, dkey, r0=0):
                S.op('act', lambda e: e.activation(out=dst[r0:r0 + M, 0, 0:64], in_=ps[r0:r0 + M, 128:192], func=AF.Copy), reads=[pk, 'Vinit'], writes=[dkey])
                S.op('act', lambda e: e.activation(out=dst[r0:r0 + M, 1, 64:128], in_=ps[r0:r0 + M, 192:256], func=AF.Copy), reads=[pk, dkey], writes=[dkey])

            stc = {'n': 0}

            def kv_out(ps, pk, r0, M, kdst, vdst):
                i = 0
                st = kvst[i]
                S.op('dve', lambda e: e.tensor_copy(st[r0:r0 + M, :], ps[r0:r0 + M, 0:256]), reads=[pk], writes=[f'kvst{i}'])
                if int(_os.environ.get("KDMA", "2")) >= 1:
                    S.dma('sp', f'o_kv{i}', [lambda e: e.dma_start(out=kdst, in_=st[r0:r0 + M, 0:128])], reads=[f'kvst{i}'])
                if int(_os.environ.get("KDMA", "2")) >= 2:
                    S.dma('sp', f'o_kv{i}', [lambda e: e.dma_start(out=vdst, in_=st[r0:r0 + M, 128:256])], reads=[f'kvst{i}'])

            def mixer_tile(ti, kind, n, src_fn, x2rows, tchunk0=None, nseg=1, pre=None, post_glu=None):
                xbuf = xt[ti % NXB]; hTb = hT[ti % 2]; hTp = hT[(ti + 1) % 2]
                gl = glb[ti % 2]; gln = glb[(ti + 1) % 2]
                glk = f'glb{ti % 2}'; glnk = f'glb{(ti + 1) % 2}'
                nb = (n + 127) // 128
                seglen = n // nseg
                if kind == 'frames' and ti > 0:
                    pass
                if pre is not None:
                    pre()
                xk, hk = f'xta{ti % NXB}', f'hTa{ti % 2}'
                if sub <= 1:
                    return
                if kind == 'frames':
                    S.op('pool', lambda e: e.tensor_copy(hTp[:, :, 0:128], hTb[:, :, NT:NT + 128]), reads=[hk], writes=[f'hTa{(ti + 1) % 2}t'])
                hreads = [hk, f'hTa{ti % 2}t'] if (kind == 'frames' and ti > 1) else [hk]

                def segview(ap3, off):
                    return ap3

                for j in range(4):
                    pa, pak = bank(); pb, pbk = bank()

                    def f(e, pa=pa, pb=pb, j=j):
                        last = None
                        for (ps, fc) in ((pa, j), (pb, 4 + j)):
                            for kc in range(8):
                                last = e.matmul(ps[:, 0:n], lhsT=Win[:, kc, fc * 128:(fc + 1) * 128], rhs=hTb[:, kc, 128:128 + n], start=(kc == 0), stop=(kc == 7))
                        return last
                    S.op('pe', f, reads=[hk, 'WinG'], writes=[pak, pbk])
                    sg = sig[j % 2]
                    S.op('act', lambda e, pb=pb, sg=sg: e.activation(out=sg[:, 0:n], in_=pb[:, 0:n], func=AF.Tanh, scale=0.5), reads=[pbk], writes=[f'sig{j % 2}'])
                    if nseg == 1:
                        gdst = gl[:, j, 30:30 + n]; gin0 = pa[:, 0:n]; gin1 = sg[:, 0:n]
                    else:
                        gdst = gl[:, j, 0:nseg * (30 + seglen)].rearrange("p (s c) -> p s c", s=nseg)[:, :, 30:30 + seglen]
                        gin0 = pa[:, 0:n].rearrange("p (s c) -> p s c", s=nseg); gin1 = sg[:, 0:n].rearrange("p (s c) -> p s c", s=nseg)
                    S.op('dve', lambda e, gdst=gdst, gin0=gin0, gin1=gin1: e.scalar_tensor_tensor(out=gdst, in0=gin1, scalar=1.0, in1=gin0, op0=ALU.add, op1=ALU.mult),
                         reads=[pak, f'sig{j % 2}'], writes=[glk])
                    if post_glu is not None:
                        c0_, w_ = (n - 30, 30) if nseg == 1 else (0, n)
                        S.op('dve', lambda e, pa=pa, sg=sg, j=j, c0_=c0_, w_=w_: e.scalar_tensor_tensor(out=glf[:, j, 0:w_], in0=sg[:, c0_:c0_ + w_], scalar=1.0, in1=pa[:, c0_:c0_ + w_], op0=ALU.add, op1=ALU.mult),
                             reads=[pak, f'sig{j % 2}'], writes=['glf'])
                    yield
                if sub <= 2:
                    return
                if post_glu is not None:
                    post_glu()
                yield
                def proj_fm(fc):
                    ps, pk = bank()

                    def f(e, ps=ps, fc=fc):
                        last = None
                        for kc in range(8):
                            last = e.matmul(ps[:, 0:n], lhsT=Win[:, kc, fc * 128:(fc + 1) * 128], rhs=hTb[:, kc, 128:128 + n], start=(kc == 0), stop=(kc == 7))
                        return last
                    S.op('pe', f, reads=[hk, 'WinK' if fc == 12 else 'WinQ'], writes=[pk])
                    return ps, pk
                for fc in (12,):
                    ps, pk = proj_fm(fc)
                    if kind == 'meta':
                        act_copy(KTm[:, :], ps[:, 0:n], [pk], ['KTm'])
                        S.op('pool', lambda e: e.tensor_copy(KB[:, :, 64:80], KTm[:].unsqueeze(1).broadcast_to([128, SEQ // CH, 16])), reads=['KTm'], writes=['KBm'])
                    elif kind == 'frames':
                        f0 = tchunk0 * CH
                        act_copy(KTl[:, f0:f0 + n], ps[:, 0:n], [pk], [f'KTl{ti}'])
                        S.op('dve', lambda e, ps=ps: e.tensor_copy(KB[:, tchunk0:tchunk0 + n // CH, 0:64], ps[:, 0:n].rearrange("p (c t) -> p c t", t=CH)),
                             reads=[pk], writes=[f'KBo{ti}'])
                    else:
                        S.op('dve', lambda e, ps=ps: e.tensor_copy(KBs[:, :, 0:SL], ps[:, 0:n].rearrange("p (s t) -> p s t", t=SL)), reads=[pk], writes=['KBs'])
                    yield
                if sub <= 3:
                    return
                yield
                if kind == 'meta':
                    kvar = int(_os.environ.get("KVAR", "9"))
                    ps, pk = kv_window(hTb, hk, 128, NMETA)
                    if kvar >= 1:
                        v_to(ps, pk, NMETA, VM, 'VM')
                    if kvar >= 2:
                        kv_out(ps, pk, 0, NMETA, kmp, vmp)
                    if kvar >= 3:
                        ps, pk = kv_window(hTb, hk, 64, 80)
                    if kvar >= 4:
                        for i in range(2):
                            for lc in range(4):
                                v_to(ps, pk, NMETA, VB[i][:, lc], f'VB{i}', r0=64)
                elif kind == 'frames':
                    tp = ti % 2
                    for lc in range(4):
                        c = tchunk0 + lc
                        ps, pk = kv_window(hTb, hk, 128 + lc * CH, CH)
                        v_to(ps, pk, CH, VB[tp][:, lc], f'VB{tp}')
                        if c == 1:
                            ps, pk = kv_window(hTb, hk, 128, CH)
                            v_to(ps, pk, CH, VA[tp][:, lc], f'VA{tp}')
                        elif c >= 2:
                            ps, pk = kv_window(hTb, hk, lc * CH, 128, extra_reads=[f'hTa{ti % 2}t'])
                            v_to(ps, pk, 128, VA[tp][:, lc], f'VA{tp}')
                        yield
                    if tchunk0 + 4 == SEQ // CH:
                        ps, pk = kv_window(hTb, hk, NT, 128)
                        kv_out(ps, pk, 0, 128, kwp, vwp)
                else:
                    for s in range(NSB):
                        ps, pk = kv_window(hTb, hk, 128 + s * SL, SL)
                        v_to(ps, pk, SL, VBs[:, s], 'VBs')
                        kv_out(ps, pk, 0, SL, kws[s, 96:128, :], vws[s, 96:128, :])
                        yield
                if sub <= 4:
                    return
                yield
                seg_stride = 30 + seglen
                for j in range(4):
                    ps, pk = bank()

                    def f(e, ps=ps, j=j):
                        last = None
                        for k in range(CW):
                            if nseg == 1:
                                rhs = gl[:, j, k:k + n]
                            else:
                                rhs = gl[:, j, 0:nseg * seg_stride].rearrange("p (s c) -> p s c", s=nseg)[:, :, k:k + seglen]
                            last = e.matmul(ps[:, 0:n], lhsT=diag[:, j * CW + k, :], rhs=rhs, start=(k == 0), stop=(k == CW - 1))
                        return last
                    S.op('pe', f, reads=[glk, 'diag'], writes=[pk])
                    S.op('act', lambda e, ps=ps, j=j: e.activation(out=cv[:, j, 0:n], in_=ps[:, 0:n], func=AF.Identity, bias=cparT[:, j, CW:CW + 1]),
                         reads=[pk, 'cparT'], writes=['cv'])
                    yield
                if kind in ('meta', 'frames'):
                    S.op('pool', lambda e: e.tensor_copy(gln[:, :, 0:30], gl[:, :, n:n + 30]), reads=[glk], writes=[glnk])
                if sub <= 5:
                    return
                yield
                S.op('dve', lambda e: e.tensor_copy(cvb[:, :, 0:n], cv[:, :, 0:n]), reads=['cv'], writes=['cvb'])
                S.op('dve', lambda e: e.tensor_tensor(out=sq[:, :, 0:n], in0=cv[:, :, 0:n], in1=cv[:, :, 0:n], op=ALU.mult), reads=['cv'], writes=['sq'])
                yield
                p1_, p1k = bank(); p2_, p2k = bank()

                def f(e):
                    last = None
                    for (ps, src) in ((p1_, cvb), (p2_, sq)):
                        for j in range(4):
                            last = e.matmul(ps[:, 0:n], lhsT=ones_bf[:], rhs=src[:, j, 0:n], start=(j == 0), stop=(j == 3))
                    return last
                S.op('pe', f, reads=['cvb', 'sq', 'ones_bf'], writes=[p1k, p2k])
                act_copy(lnm[:, 0:n], p1_[:, 0:n], [p1k], ['lnm'], scale=1.0 / DCONV)
                S.op('dve', lambda e: e.tensor_tensor(out=lnq[:, 0:n], in0=lnm[:, 0:n], in1=lnm[:, 0:n], op=ALU.mult), reads=['lnm'], writes=['lnq'])
                S.op('dve', lambda e: e.scalar_tensor_tensor(out=lnv[:, 0:n], in0=p2_[:, 0:n], scalar=1.0 / DCONV, in1=lnq[:, 0:n], op0=ALU.mult, op1=ALU.subtract),
                     reads=[p2k, 'lnq'], writes=['lnv'])
                S.op('act', lambda e: e.activation(out=lnq[:, 0:n], in_=lnv[:, 0:n], func=AF.Sqrt, bias=epsc[:, :]), reads=['lnv', 'epsc'], writes=['lnq'])
                S.op('dve', lambda e: e.reciprocal(lnr[:, 0:n], lnq[:, 0:n]), reads=['lnq'], writes=['lnr'])
                yield
                for j in range(4):
                    t_ = tn[j % 2]; tk_ = f'tn{j % 2}'
                    S.op('dve', lambda e, t_=t_, j=j: e.tensor_tensor(out=t_[:, 0:n], in0=cv[:, j, 0:n], in1=lnm[:, 0:n], op=ALU.subtract), reads=['cv', 'lnm'], writes=[tk_])
                    S.op('dve', lambda e, t_=t_: e.tensor_tensor(out=t_[:, 0:n], in0=t_[:, 0:n], in1=lnr[:, 0:n], op=ALU.mult), reads=[tk_, 'lnr'], writes=[tk_])
                    S.op('act', lambda e, t_=t_, j=j: e.activation(out=catT[:, j, 0:n], in_=t_[:, 0:n], func=AF.Silu, scale=cparT[:, j, CW + 1:CW + 2],
                                                                   bias=cparT[:, j, CW + 2:CW + 3]), reads=[tk_, 'cparT'], writes=['catT'])
                if sub <= 6:
                    return
                yield
                for fc in range(8, 12):
                    ps, pk = proj_fm(fc)
                    act_copy(qTk[0][0:64, fc - 8, 0:n], ps[0:64, 0:n], [pk], ['qT'], scale=0.125)
                    act_copy(qTk[1][64:128, fc - 8, 0:n], ps[64:128, 0:n], [pk], ['qT'], scale=0.125)
                    yield
                units = []
                if kind == 'meta':
                    units.append((0, NMETA, [(KTm[:, :], NMETA, VM, bM[:], ['KTm', 'VM'])], 0, 0))
                elif kind == 'frames':
                    tp = ti % 2
                    for lc in range(4):
                        c = tchunk0 + lc
                        blocks = []
                        if c == 1:
                            blocks.append((KTl[:, 0:CH], CH, VA[tp][:, lc], bA1[:], [f'KTl{ti}', f'VA{tp}']))
                        elif c >= 2:
                            blocks.append((KTl[:, (c - 2) * CH:c * CH], 128, VA[tp][:, lc], bA[:], [f'KTl{ti}', f'KTl{ti - 1}', f'VA{tp}']))
                        blocks.append((KB[:, c, :], 80, VB[tp][:, lc], bB[min(c, 3)][:], [f'KBo{ti}', 'KBm', f'VB{tp}']))
                        units.append((lc * CH, CH, blocks, lc * CH, lc % 2))
                else:
                    for s in range(NSB):
                        blocks = [(KTsA[:, s, :], 128, VAs[:, s], bSA[:], ['KTsA', 'VAs']),
                                  (KBs[:, s, :], 48, VBs[:, s], bSB[:], ['KBs', 'VBs'])]
                        units.append((s * SL, SL, blocks, s * SL, s % 2))
                pendB = None
                for u in units:
                    attn_A(*u)
                    yield
                    if pendB is not None:
                        attn_B(*pendB)
                        yield
                    pendB = u
                attn_B(*pendB)
                if sub <= 7:
                    return
                yield
                for b in range(nb):
                    r = min(128, n - b * 128)
                    pa, pak = bank(); pb, pbk = bank()

                    def f(e, pa=pa, pb=pb, b=b, r=r):
                        last = None
                        for (ps, half) in ((pa, 0), (pb, 1)):
                            for kc in range(8):
                                last = e.matmul(ps[0:r, :], lhsT=catT[:, kc, b * 128:b * 128 + r], rhs=Wout[:, kc, half * 512:(half + 1) * 512], start=(kc == 0), stop=(kc == 7))
                        return last
                    S.op('pe', f, reads=['catT', 'Wout'], writes=[pak, pbk])
                    S.op('dve', lambda e, pa=pa, b=b, r=r: e.tensor_tensor(out=xbuf[0:r, b, 0:512], in0=pa[0:r, :], in1=xbuf[0:r, b, 0:512], op=ALU.add), reads=[pak, xk], writes=[xk])
                    S.op('dve', lambda e, pb=pb, b=b, r=r: e.tensor_tensor(out=xbuf[0:r, b, 512:1024], in0=pb[0:r, :], in1=xbuf[0:r, b, 512:1024], op=ALU.add), reads=[pbk, xk], writes=[xk])
                    if b + 1 < nb:
                        yield
                S.dma('sp', f'o_x2{ti % NXB}', [lambda e, b=b: e.dma_start(out=x2[x2rows + b * 128:x2rows + b * 128 + min(128, n - b * 128), :],
                                                                       in_=xbuf[0:min(128, n - b * 128), b, :]) for b in range(nb)], reads=[xk], writes=['x2d'])

            def glu_out(n, off, dst):
                ps, pk = bank()

                def f(e, ps=ps):
                    last = None
                    for j in range(4):
                        last = e.transpose(ps[0:30, j * 128:(j + 1) * 128], glf[:, j, off:off + 30], identf[:])
                    return last
                S.op('pe', f, reads=['glf', 'identf'], writes=[pk])
                S.op('dve', lambda e, ps=ps: e.tensor_copy(ost[0:30, :], ps[0:30, :]), reads=[pk], writes=['ost'])
                S.dma('sp', 'o_glu', [lambda e: e.dma_start(out=dst, in_=ost[0:30, :])], reads=['ost'])

            def late_qt():
                for i in range(2):
                    S.op('pool', lambda e, i=i: e.memset(qTk[i][:], 0.0), writes=['qT'])
            late_setup.append(late_qt)
            late_setup.append(build_diag)
            for i in range(2):
                S.op('pool', lambda e, i=i: e.memset(hT[i][:], 0.0), writes=[f'hTa{i}', f'hTa{i}t'])
            S.op('pool', lambda e: e.memset(glb[0][:], 0.0), writes=['glb0'])
            gens = []
            heads = []
            head_t = []
            def meta_with_bias():
                k = 0
                for _ in mixer_tile(0, 'meta', NMETA, lambda xb: [lambda e: e.dma_start(out=xb[0:NMETA, 0, :], in_=meta)], 0):
                    yield
                    k += 1
                    while late_setup:
                        late_setup.pop(0)()
                    if bias_jobs and k >= 3:
                        bias_jobs.pop(0)()
                while bias_jobs:
                    bias_jobs.pop(0)()
            if lvl >= 2:
                gens.append(meta_with_bias)
                heads.append(lambda: head_a(0, lambda xb: [lambda e: e.dma_start(out=xb[0:NMETA, 0, :], in_=meta)], NMETA, gmix_b, 'a'))
                head_t.append(lambda: load_norm_transpose(0, None, NMETA, None, hT[0], gmix_b, 'a'))
            for t in range(SEQ // NT if lvl >= 4 else (1 if lvl >= 3 else 0)):
                gens.append(lambda t=t: mixer_tile(t + 1, 'frames', NT, lambda xb, t=t: [lambda e: e.dma_start(out=xb[:], in_=xp[t * NT:(t + 1) * NT, :].rearrange("(b p) d -> p b d", p=128))],
                                                   NMETA + t * NT, tchunk0=t * (NT // CH),
                                                   post_glu=(lambda: glu_out(NT, 0, cap)) if t == SEQ // NT - 1 else None))
                heads.append(lambda t=t: head_a(t + 1, lambda xb, t=t: [lambda e: e.dma_start(out=xb[:], in_=xp[t * NT:(t + 1) * NT, :].rearrange("(b p) d -> p b d", p=128))], NT, gmix_b, 'a'))
                head_t.append(lambda t=t: load_norm_transpose(t + 1, None, NT, None, hT[(t + 1) % 2], gmix_b, 'a'))
            ti = SEQ // NT + 1
            gls = glb[ti % 2]

            def sample_pre():
              for s in range(NSB):
                sca_s = ost[0:CW - 1, :]
                S.dma('sp', 'c_sca', [lambda e, s=s: e.dma_start(out=sca_s, in_=sca[s])], writes=['ost'])
                ps, pk = bank()

                def f(e, ps=ps, s=s, sca_s=sca_s):
                    last = None
                    for j in range(4):
                        last = e.transpose(ps[:, j * 32:j * 32 + 30], sca_s[:, j * 128:(j + 1) * 128], identf[0:30, 0:30])
                    return last
                S.op('pe', f, reads=['ost', 'identf'], writes=[pk])
                S.op('dve', lambda e, ps=ps, s=s: e.tensor_copy(gls[:, :, s * 62:s * 62 + 30], ps[:, 0:128].rearrange("p (j c) -> p j c", j=4)[:, :, 0:30]),
                     reads=[pk], writes=[f'glb{ti % 2}'])

            def sample_post_glu():
                for s in range(NSB):
                    glu_out(NSB * SL, s * SL + 2, cas[s])
            if lvl >= 5:
                gens.append(lambda: mixer_tile(ti, 'sample', NSB * SL, lambda xb: [lambda e: e.dma_start(out=xb[:, 0, :], in_=xs)], NMETA + SEQ, nseg=NSB,
                                               pre=sample_pre, post_glu=sample_post_glu))
                heads.append(lambda: head_a(ti, lambda xb: [lambda e: e.dma_start(out=xb[:, 0, :], in_=xs)], NSB * SL, gmix_b, 'a'))
                head_t.append(lambda: load_norm_transpose(ti, None, NSB * SL, None, hT[ti % 2], gmix_b, 'a'))
            PSKEW = int(_os.environ.get("KSKEW", "16"))
            active = []
            tick = 0
            nxt = 0
            last_start = -10 ** 9
            HLEAD = int(_os.environ.get("KHLEAD", "14"))
            heads_emitted = 0
            started = {}
            while active or nxt < len(gens):
                if nxt < len(gens) and len(active) < 2 and tick - last_start >= PSKEW and heads_emitted > nxt:
                    active.append((nxt, gens[nxt]())); started[nxt] = tick; nxt += 1
                    last_start = tick
                if heads_emitted < len(heads):
                    h = heads_emitted
                    if h == 0 or (h - 1 in started and tick - started[h - 1] >= HLEAD):
                        S.cur_tile = h
                        heads[h]()
                        heads_emitted += 1
                        for _ in head_t[h]():
                            pass
                for (tid, g) in (list(active)[::-1] if _os.environ.get("KORD", "0") == "1" else list(active)):
                    S.cur_tile = tid
                    try:
                        next(g)
                    except StopIteration:
                        active.remove((tid, g))
                tick += 1
            S.cur_tile = -1
            S.finish('sp')
            S.emit(block1)

        with ExitStack() as p2:
            pc = p2.enter_context

            def sb2(name, shape, dt=F32):
                return sb(name, shape, dt, ctx=pc)

            Wup = sb2("Wup", [128, 8, 2 * DFF], BF16); Wdn = sb2("Wdn", [128, NJ, D], BF16)
            xq = [sb2(f"xq{i}", [128, 2, D]) for i in range(2)]
            hb2 = [sb2(f"hc{i}", [128, D], BF16) for i in range(2)]
            h2T = [sb2(f"h2T{i}", [128, 8, NT], BF16) for i in range(2)]
            GSC = 2 + NT
            gs = [sb2(f"gs{i}", [128, GSC]) for i in range(2)]
            cacc = [sb2(f"cacc{i}", [128, NT]) for i in range(2)]
            sl = [sb2(f"sl{i}", [128, NT]) for i in range(2)]
            yT = sb2("yT", [128, NJ, NT], BF16)
            ghist = sb2("ghist", [128, NJ, 2])
            scfT = sb2("scfT", [128, NSB, NJ, 2])
            gst = sb2("gst", [128, 1 + NSB, 2, NJ])
            gso = sb2("gso", [64, 128])
            r_ss = sb2("r_ss", [128, 4]); r_sd = sb2("r_sd", [128, 4]); r_r = sb2("r_r", [128, 4])
            f_ss = sb2("f_ss", [128, 4]); f_sd = sb2("f_sd", [128, 4]); f_r = sb2("f_r", [128, 4])
            eps2 = sb2("eps2", [128, 1]); junk = sb2("junk", [128, 512])
            gffn_b = sb2("gffn_b", [128, D]); gfin_b = sb2("gfin_b", [128, D])
            fparS = yT[0:4].rearrange("p j t -> p (j t)").bitcast(F32)
            assert tuple(fparS.shape) == (4, DFF), fparS.shape
            block2 = pc(nc.Block())
            S.dma('sp', 'c_g2', [lambda e: e.dma_start(out=gffn_b[:], in_=gffn.partition_broadcast(128)),
                                 lambda e: e.dma_start(out=gfin_b[:], in_=gfin.partition_broadcast(128)),
                                 lambda e: e.dma_start(out=fparS, in_=fpar)], writes=['gb2', 'yT'])
            for j in range(NJ):
                ps, pk = bank()
                S.op('pe', lambda e, ps=ps, j=j: e.transpose(ps[:, 0:4], fparS[:, j * 128:(j + 1) * 128], identf[0:4, 0:4]),
                     reads=['gb2', 'yT', 'identf'], writes=[pk])
                act_copy(fparT[:, j, :], ps[:, 0:4], [pk], ['fparT'])
            S.op('pool', lambda e: e.memset(eps2[:], EPS), writes=['eps2'])
            S.op('pool', lambda e: e.memset(ghist[:], 0.0), writes=[f'ghist{j}' for j in range(NJ)])

            CB = 512
            ncb = (DFF + CB - 1) // CB
            wq = []
            for cbk in range(ncb if lvl >= 6 else 0):
                c0 = cbk * CB; c1 = min(DFF, c0 + CB)
                wq.append(lambda cbk=cbk, c0=c0, c1=c1: S.dma('pool', f'wg{cbk}', [lambda e, kc=kc: e.dma_start(out=Wup[:, kc, c0:c1], in_=wup[kc * 128:(kc + 1) * 128, c0:c1]) for kc in range(8)], writes=[f'Wg{cbk}']))
                wq.append(lambda cbk=cbk, c0=c0, c1=c1: S.dma('pool', f'wu{cbk}', [lambda e, kc=kc: e.dma_start(out=Wup[:, kc, DFF + c0:DFF + c1], in_=wup[kc * 128:(kc + 1) * 128, DFF + c0:DFF + c1]) for kc in range(8)], writes=[f'Wu{cbk}']))
            for j0 in range(0, NJ if lvl >= 6 else 0, 4):
                j1 = min(NJ, j0 + 4)
                wq.append(lambda j0=j0, j1=j1: S.dma('pool', f'wd{j0}', [lambda e, j=j: e.dma_start(out=Wdn[:, j, :], in_=wdn[j * 128:(j + 1) * 128, :]) for j in range(j0, j1)], writes=[f'Wd{j0 // 4}']))
            for _ in range(min(4, len(wq))):
                wq.pop(0)()
            scfS = gso[0:NJ, :]
            for s in range(NSB):
                for t in range(2):
                    S.dma('sp', 'c_scf', [lambda e, s=s, t=t: e.dma_start(out=scfS, in_=scf[s, t, :].rearrange("(j p) -> j p", p=128))], writes=['gso'])
                    ps, pk = bank()
                    S.op('pe', lambda e, ps=ps: e.transpose(ps[:, 0:NJ], scfS, identf[0:NJ, 0:NJ]), reads=['gso', 'identf'], writes=[pk])
                    act_copy(scfT[:, s, :, t], ps[:, 0:NJ], [pk], ['scfT'])

            def out_rows(r0, r):
                res = []
                a, bnd = r0, r0 + r
                for (lo, hi, dst, base) in ((NMETA, NMETA + SEQ, yp, NMETA), (NMETA + SEQ, NROWS, ys, NMETA + SEQ)):
                    s_, e_ = max(a, lo), min(bnd, hi)
                    if s_ < e_:
                        res.append((s_ - r0, e_ - s_, dst[s_ - base:e_ - base, :]))
                return res

            usb = [sb2(f"usb{i}", [128, NT], BF16) for i in range(2)]

            def ffn_head(ti, r0, n, groups):
                xbuf = xq[ti % 2]; xk = f'xq{ti % 2}'
                nb = (n + 127) // 128
                hTb = h2T[ti % 2]; hk = f'h2T{ti % 2}'
                S.dma('sp', f'xq{ti % 2}', [lambda e, b=b: e.dma_start(out=xbuf[0:min(128, n - b * 128), b, :], in_=x2[r0 + b * 128:r0 + b * 128 + min(128, n - b * 128), :]) for b in range(nb)],
                      reads=['x2d'], writes=[xk])
                for b in range(nb):
                    r = min(128, n - b * 128)
                    for hf in range(2):
                        S.op('act', lambda e, b=b, r=r, hf=hf: e.activation(out=junk[0:r, :], in_=xbuf[0:r, b, hf * 512:(hf + 1) * 512], func=AF.Square,
                                                                            accum_out=r_ss[0:r, 2 * b + hf:2 * b + hf + 1]), reads=[xk], writes=['r_ss'])
                    S.op('dve', lambda e, b=b, r=r: e.tensor_tensor(out=r_sd[0:r, b:b + 1], in0=r_ss[0:r, 2 * b:2 * b + 1], in1=r_ss[0:r, 2 * b + 1:2 * b + 2], op=ALU.add), reads=['r_ss'], writes=['r_sd'])
                    S.op('dve', lambda e, b=b, r=r: e.tensor_scalar(out=r_sd[0:r, 2 + b:3 + b], in0=r_sd[0:r, b:b + 1], scalar1=1.0 / D, scalar2=EPS, op0=ALU.mult, op1=ALU.add), reads=['r_sd'], writes=['r_sd2'])
                    S.op('pool', lambda e, b=b, r=r: e.tensor_tensor(out=r_r[0:r, b:b + 1], in0=r_sd[0:r, 2 + b:3 + b], in1=mhalf[0:r, :], op=ALU.pow), reads=['r_sd2', 'mhalf'], writes=['r_r'])
                    hbb = hb2[b % 2]
                    S.op('dve', lambda e, b=b, r=r, hbb=hbb: e.scalar_tensor_tensor(out=hbb[0:r, :], in0=xbuf[0:r, b, :], scalar=r_r[0:r, b:b + 1], in1=gffn_b[0:r, :], op0=ALU.mult, op1=ALU.mult),
                         reads=[xk, 'r_r', 'gb2'], writes=[f'hc{b % 2}'])

            def ffn_head_b(ti, r0, n, groups):
                nb = (n + 127) // 128
                hTb = h2T[ti % 2]; hk = f'h2T{ti % 2}'
                for b in range(nb):
                    r = min(128, n - b * 128)
                    hbb = hb2[b % 2]
                    pt, tk = tbank()

                    def tf(e, pt=pt, hbb=hbb, r=r):
                        last = None
                        for kc in range(8):
                            last = e.transpose(pt[:, kc * 128:kc * 128 + r], hbb[0:r, kc * 128:(kc + 1) * 128], ident[0:r, 0:r])
                        return last
                    S.op('pe', tf, reads=[f'hc{b % 2}', 'ident'], writes=[tk])
                    act_copy(hTb[:, :, b * 128:b * 128 + r], pt[:].rearrange("p (k t) -> p k t", k=8)[:, :, 0:r], [tk], [hk])

            def ffn_chunks(ti, r0, n, groups, j0, j1):
                hTb = h2T[ti % 2]; hk = f'h2T{ti % 2}'
                lay = []
                off = 0
                for (col0, nseg, seglen, hist) in groups:
                    lay.append(off); off += nseg * (2 + seglen)
                pend = []
                for j in range(j0, j1):
                    if wq:
                        wq.pop(0)()
                    pg, pgk = bank(); pu, puk = bank()
                    cbk = (j * 128) // CB

                    def f(e, pg=pg, pu=pu, j=j):
                        last = None
                        for (ps, c0) in ((pg, j * 128), (pu, DFF + j * 128)):
                            for kc in range(8):
                                last = e.matmul(ps[:, 0:n], lhsT=Wup[:, kc, c0:c0 + 128], rhs=hTb[:, kc, 0:n], start=(kc == 0), stop=(kc == 7))
                        return last
                    S.op('pe', f, reads=[hk, f'Wg{cbk}', f'Wu{cbk}'], writes=[pgk, puk])
                    g_ = gs[j % 2]; gk = f'gs{j % 2}'; ca = cacc[j % 2]; cak = f'cacc{j % 2}'; s_ = sl[j % 2]; sk = f'sl{j % 2}'
                    ub = usb[j % 2]; ubk = f'usb{j % 2}'
                    for gi, (col0, nseg, seglen, hist) in enumerate(groups):
                        base = lay[gi]
                        gv = g_[:, base:base + nseg * (2 + seglen)].rearrange("p (s c) -> p s c", s=nseg)
                        if hist == 'carry':
                            act_copy(gv[:, 0, 0:2], ghist[:, j, :], [f'ghist{j}'], [gk + 'h'])
                        else:
                            act_copy(gv[:, :, 0:2], scfT[:, :, j, :], ['scfT'], [gk + 'h'])
                        act_copy(gv[:, :, 2:2 + seglen], pg[:, col0:col0 + nseg * seglen].rearrange("p (s c) -> p s c", s=nseg), [pgk], [gk])
                        cav = ca[:, col0:col0 + nseg * seglen].rearrange("p (s c) -> p s c", s=nseg)
                        S.op('dve', lambda e, gv=gv, cav=cav, j=j, seglen=seglen: e.tensor_scalar(out=cav, in0=gv[:, :, 0:seglen], scalar1=fparT[:, j, 0:1], scalar2=fparT[:, j, 3:4], op0=ALU.mult, op1=ALU.add),
                             reads=[gk, gk + 'h', 'fparT'], writes=[cak])
                        for k in (1, 2):
                            S.op('dve', lambda e, gv=gv, cav=cav, j=j, k=k, seglen=seglen: e.scalar_tensor_tensor(out=cav, in0=gv[:, :, k:k + seglen], scalar=fparT[:, j, k:k + 1], in1=cav, op0=ALU.mult, op1=ALU.add),
                                 reads=[gk, gk + 'h', 'fparT', cak], writes=[cak])
                        if hist == 'carry' and ti < NTILES2 - 1:
                            S.op('pool', lambda e, gv=gv, j=j, seglen=seglen: e.tensor_copy(ghist[:, j, :], gv[:, 0, seglen:seglen + 2]), reads=[gk], writes=[f'ghist{j}'])
                    if ti == NTILES2 - 1:
                        gv0 = g_[:, lay[0]:lay[0] + 2 + groups[0][2]]
                        S.op('pool', lambda e, gv0=gv0, j=j: e.tensor_copy(gst[:, 0, :, j], gv0[:, groups[0][2]:groups[0][2] + 2]), reads=[gk], writes=['gst'])
                        gv1 = g_[:, lay[1]:lay[1] + NSB * (2 + SL)].rearrange("p (s c) -> p s c", s=NSB)
                        S.op('pool', lambda e, gv1=gv1, j=j: e.tensor_copy(gst[:, 1:1 + NSB, :, j], gv1[:, :, SL:SL + 2]), reads=[gk], writes=['gst'])
                    act_copy(ub[:, 0:n], pu[:, 0:n], [puk], [ubk])

                    def back(ca=ca, s_=s_, ub=ub, j=j, cak=cak, sk=sk, ubk=ubk):
                        S.op('act', lambda e: e.activation(out=s_[:, 0:n], in_=ca[:, 0:n], func=AF.Silu), reads=[cak], writes=[sk])
                        ydst = yTe[ti % 2][:, j, 0:n] if j < JE else yT[:, j, 0:n]
                        S.op(YENG, lambda e: e.tensor_tensor(out=ydst, in0=ub[:, 0:n], in1=s_[:, 0:n], op=ALU.mult), reads=[ubk, sk],
                             writes=[f'yTe{ti % 2}' if j < JE else 'yT'])
                    if pend:
                        pend.pop()()
                    pend.append(back)
                if pend:
                    pend.pop()()

            JE = int(_os.environ.get("KJE", "4"))
            yTe = [sb2(f"yTe{i}", [128, max(JE, 1), NT], BF16) for i in range(2)]
            wdb = {}
            YENG = _os.environ.get("KYENG", "pool")

            def ffn_wdown(ti, r0, n, groups, part):
                nb = (n + 127) // 128
                wdb[ti] = [(bank(), bank()) for b in range(nb)]
                ja, jb = 0, NJ
                for b in range(nb):
                    r = min(128, n - b * 128)
                    (pa, pak), (pb, pbk) = wdb[ti][b]

                    def f(e, pa=pa, pb=pb, b=b, r=r):
                        last = None
                        for (ps, half) in ((pa, 0), (pb, 1)):
                            for j in range(ja, jb):
                                ysrc = yTe[ti % 2] if j < JE else yT
                                last = e.matmul(ps[0:r, :], lhsT=ysrc[:, j, b * 128:b * 128 + r], rhs=Wdn[:, j, half * 512:(half + 1) * 512], start=(j == 0), stop=(j == NJ - 1))
                        return last
                    S.op('pe', f, reads=['yT', f'yTe{ti % 2}'] + [f'Wd{i}' for i in range((NJ + 3) // 4)], writes=[pak, pbk])

            def ffn_tail(ti, r0, n, groups):
                xbuf = xq[ti % 2]; xk = f'xq{ti % 2}'
                nb = (n + 127) // 128
                for b in range(nb):
                    r = min(128, n - b * 128)
                    (pa, pak), (pb, pbk) = wdb[ti][b]
                    S.op('dve', lambda e, pa=pa, b=b, r=r: e.tensor_tensor(out=xbuf[0:r, b, 0:512], in0=pa[0:r, :], in1=xbuf[0:r, b, 0:512], op=ALU.add), reads=[pak, xk], writes=[xk])
                    S.op('dve', lambda e, pb=pb, b=b, r=r: e.tensor_tensor(out=xbuf[0:r, b, 512:1024], in0=pb[0:r, :], in1=xbuf[0:r, b, 512:1024], op=ALU.add), reads=[pbk, xk], writes=[xk])
                    for hf in range(2):
                        S.op('act', lambda e, b=b, r=r, hf=hf: e.activation(out=junk[0:r, :], in_=xbuf[0:r, b, hf * 512:(hf + 1) * 512], func=AF.Square,
                                                                            accum_out=f_ss[0:r, 2 * b + hf:2 * b + hf + 1]), reads=[xk], writes=['f_ss'])
                    S.op('dve', lambda e, b=b, r=r: e.tensor_tensor(out=f_sd[0:r, b:b + 1], in0=f_ss[0:r, 2 * b:2 * b + 1], in1=f_ss[0:r, 2 * b + 1:2 * b + 2], op=ALU.add), reads=['f_ss'], writes=['f_sd'])
                    S.op('dve', lambda e, b=b, r=r: e.tensor_scalar(out=f_sd[0:r, 2 + b:3 + b], in0=f_sd[0:r, b:b + 1], scalar1=1.0 / D, scalar2=EPS, op0=ALU.mult, op1=ALU.add), reads=['f_sd'], writes=['f_sd2'])
                    S.op('pool', lambda e, b=b, r=r: e.tensor_tensor(out=f_r[0:r, b:b + 1], in0=f_sd[0:r, 2 + b:3 + b], in1=mhalf[0:r, :], op=ALU.pow), reads=['f_sd2', 'mhalf'], writes=['f_r'])
                    S.op('dve', lambda e, b=b, r=r: e.scalar_tensor_tensor(out=xbuf[0:r, b, :], in0=xbuf[0:r, b, :], scalar=f_r[0:r, b:b + 1], in1=gfin_b[0:r, :], op0=ALU.mult, op1=ALU.mult),
                         reads=[xk, 'f_r', 'gb2'], writes=[xk])
                    outs = out_rows(r0 + b * 128, r)
                    if outs:
                        S.dma('sp', f'o_y{ti % 2}', [lambda e, p0=p0, cnt=cnt, dst=dst, b=b: e.dma_start(out=dst, in_=xbuf[p0:p0 + cnt, b, :]) for (p0, cnt, dst) in outs], reads=[xk])

            NFULL = (NMETA + SEQ) // NT
            NTILES2 = NFULL + 1
            tail = NMETA + SEQ - NFULL * NT
            tiles2 = [(t, t * NT, NT, [(0, 1, NT, 'carry')]) for t in range(NFULL if lvl >= 9 else (1 if lvl >= 7 else 0))]
            if lvl >= 9:
                tiles2.append((NFULL, NFULL * NT, tail + NSB * SL, [(0, 1, tail, 'carry'), (tail, NSB, SL, 'state')]))
            JH = int(_os.environ.get("KJH", "8"))
            JHB = int(_os.environ.get("KJHB", "4"))
            if tiles2:
                ffn_head(*tiles2[0])
                ffn_head_b(*tiles2[0])
            for i_, tl in enumerate(tiles2):
                nx = tiles2[i_ + 1] if i_ + 1 < len(tiles2) else None
                ffn_chunks(*tl, JE if i_ > 0 else 0, JH)
                if nx:
                    ffn_head(*nx)
                ffn_chunks(*tl, JH, JH + JHB)
                if nx:
                    ffn_head_b(*nx)
                ffn_chunks(*tl, JH + JHB, NJ)
                if nx:
                    ffn_chunks(*nx, 0, JE)
                ffn_wdown(*tl, 0)
                ffn_tail(*tl)
            for g in range(1 + NSB if lvl >= 9 else 0):
                ps, pk = bank()
                S.op('pe', lambda e, ps=ps, g=g: e.transpose(ps[0:2 * NJ, 0:128], gst[:, g, :, :].rearrange("p t j -> p (t j)"), identf[:]), reads=['gst', 'identf'], writes=[pk])
                S.op('dve', lambda e, ps=ps: e.tensor_copy(gso[0:2 * NJ, :], ps[0:2 * NJ, 0:128]), reads=[pk], writes=['gso'])
                dst = cfp if g == 0 else cfs[g - 1]
                S.dma('sp', 'o_gst', [lambda e, t=t, dst=dst: e.dma_start(out=dst[t, :].rearrange("(j p) -> j p", p=128), in_=gso[t * NJ:(t + 1) * NJ, :]) for t in range(2)], reads=['gso'])
            S.finish('sp')
            S.emit(block2)
    return nc


def _rel_bucket_np(rel):
    nb = 16
    max_exact = 8
    ret = np.where(rel > 0, nb, 0)
    n = np.abs(rel)
    nf = np.maximum(n, 1).astype(np.float32)
    large = max_exact + (np.log(nf / np.float32(max_exact)) / np.float32(math.log(256 / max_exact)) * np.float32(nb - max_exact)).astype(np.int32)
    large = np.minimum(large, nb - 1)
    return ret + np.where(n < max_exact, n, large)


_CACHE = {}


def kernel(x_prompt, x_sample, cache_k_meta, cache_v_meta, cache_k_win, cache_v_win, state_conv_a, state_conv_ffn,
           meta_tokens, rel_bias_table, norm_mix, w_in, conv_dw_w, conv_dw_b, conv_ln_g, conv_ln_b, attn_sinks, w_out,
           norm_ffn, w_up, ffn_dw_w, ffn_dw_b, w_down, norm_final):
    f = lambda a: np.ascontiguousarray(np.asarray(a, dtype=np.float32))
    x_prompt = f(x_prompt); x_sample = f(x_sample)
    w_in0 = f(w_in)[0]; w_out0 = f(w_out)[0]
    qcols = []
    for j in range(4):
        qcols += list(range(1024 + 64 * j, 1024 + 64 * j + 64)) + list(range(1024 + 64 * (j + 4), 1024 + 64 * (j + 4) + 64))
    cols = list(range(1024)) + qcols + list(range(1536, 1792))
    win_p = np.ascontiguousarray(w_in0[:, cols])
    rows = list(range(512))
    for j in range(4):
        rows += list(range(512 + 64 * j, 512 + 64 * j + 64)) + list(range(512 + 64 * (j + 4), 512 + 64 * (j + 4) + 64))
    wout_p = np.ascontiguousarray(w_out0[rows, :])
    cpar = np.ascontiguousarray(np.concatenate([f(conv_dw_w)[0], f(conv_dw_b), f(conv_ln_g), f(conv_ln_b)], axis=0))
    fpar = np.ascontiguousarray(np.concatenate([f(ffn_dw_w)[0], f(ffn_dw_b)], axis=0))
    rel = np.arange(E_LEN, dtype=np.int32) - E_OFF
    bkt = _rel_bucket_np(rel)
    eoh = (bkt[None, :] == np.arange(32)[:, None]).astype(np.float32)
    shared = {
        "meta": f(meta_tokens), "win": win_p, "wout": wout_p, "wup": f(w_up)[0], "wdn": f(w_down)[0],
        "gmix": f(norm_mix)[0], "gffn": f(norm_ffn)[0], "gfin": f(norm_final), "cpar": cpar, "fpar": fpar,
        "sinks": f(attn_sinks)[0], "tab": f(rel_bias_table), "eoh": eoh,
    }
    ckm = f(cache_k_meta)[0].reshape(32, NMETA, 128); cvm = f(cache_v_meta)[0].reshape(32, NMETA, 128)
    ckw = f(cache_k_win)[0].reshape(32, 128, 128); cvw = f(cache_v_win)[0].reshape(32, 128, 128)
    sca = f(state_conv_a)[0]; scf = f(state_conv_ffn)[0]
    in_maps = []
    for c in range(8):
        m = dict(shared)
        sl_ = slice(NSB * c, NSB * (c + 1))
        m.update({"xp": x_prompt[c], "xs": np.ascontiguousarray(x_sample[sl_].reshape(NSB * SL, D)),
                  "ckm": ckm[sl_], "cvm": cvm[sl_], "ckw": ckw[sl_], "cvw": cvw[sl_], "sca": sca[sl_], "scf": scf[sl_]})
        in_maps.append({k: np.ascontiguousarray(v) for k, v in m.items()})
    if _CACHE.get("maps_only"):
        return in_maps
    if "nc" not in _CACHE:
        _CACHE["nc"] = build_program()
    res = run_bass_kernel_spmd(_CACHE["nc"], in_maps, core_ids=list(range(8)))
    R = res.results
    cat = lambda k: np.stack([np.asarray(R[c][k], dtype=np.float32) for c in range(8)], axis=0)
    y_prompt = cat("yp")
    y_sample = cat("ys").reshape(32, SL, D)
    k_meta_p = cat("kmp").reshape(1, 8, NMETA, 2, 64); v_meta_p = cat("vmp").reshape(1, 8, NMETA, 2, 64)
    k_win_p = cat("kwp").reshape(1, 8, 128, 2, 64); v_win_p = cat("vwp").reshape(1, 8, 128, 2, 64)
    conv_a_p = cat("cap").reshape(1, 8, CW - 1, DCONV); conv_ffn_p = cat("cfp").reshape(1, 8, 2, DFF)
    k_win_s = cat("kws").reshape(1, 32, 128, 2, 64); v_win_s = cat("vws").reshape(1, 32, 128, 2, 64)
    conv_a_s = cat("cas").reshape(1, 32, CW - 1, DCONV); conv_ffn_s = cat("cfs").reshape(1, 32, 2, DFF)
    return (y_prompt, y_sample, k_meta_p, v_meta_p, k_win_p, v_win_p, conv_a_p, conv_ffn_p,
            k_win_s, v_win_s, conv_a_s, conv_ffn_s)
```

```python
import math
from contextlib import ExitStack

import numpy as np
import concourse.bass as bass
import concourse.mybir as mybir
from concourse.bass_utils import run_bass_kernel_spmd

F32 = mybir.dt.float32
BF16 = mybir.dt.bfloat16
AF = mybir.ActivationFunctionType
ALU = mybir.AluOpType

D = 1024
SEQ = 2048
NMETA = 16
CH = 64
DCONV = 512
CW = 31
DFF = 2816
NJ = DFF // 128
EPS = 1e-6
NT = 256
NSB = 4
SL = 32
NROWS = NMETA + SEQ + NSB * SL
E_OFF = 271
E_LEN = 335


class Sched:
    def __init__(self, nc, sems):
        self.nc = nc
        self.free_sems = list(sems)
        self.prog = {k: [] for k in ('pe', 'act', 'dve', 'pool', 'sp')}
        self.esem = {k: self.free_sems.pop() for k in ('pe', 'act', 'dve', 'pool')}
        self.count = {k: 0 for k in self.esem}
        self.seen = {k: {} for k in self.prog}
        self.lastw = {}
        self.readers = {}
        self.slotsem = {}
        self.slotcnt = {}
        self.semid = {}
        self.cur_tile = -1
        self.wtile = {}

    def _deps(self, reads, writes, eng=None):
        deps = []
        mysem = self.esem.get(eng)
        for r in reads:
            if r in self.lastw:
                deps.append(self.lastw[r])
            if r[:2] in ('ps', 'pt'):
                deps.extend(t for t in self.readers.get(r, []) if t[0] is not mysem)
        for w in writes:
            if w in self.lastw:
                deps.append(self.lastw[w])
            deps.extend(self.readers.get(w, []))
        return deps

    def _emit_waits(self, eng, deps):
        need = {}
        for (sem, val) in deps:
            k = id(sem)
            self.semid[k] = sem
            if self.seen[eng].get(k, 0) >= val:
                continue
            need[k] = max(need.get(k, 0), val)
        for k, val in need.items():
            self.seen[eng][k] = val
            sem = self.semid[k]
            self.prog[eng].append(lambda e, sem=sem, val=val: e.wait_ge(sem, val))

    def _record(self, reads, writes, tok):
        for r in reads:
            assert self.wtile.get(r, -1) <= self.cur_tile or self.cur_tile < 0, (r, self.wtile.get(r), self.cur_tile)
        for w in writes:
            self.wtile[w] = self.cur_tile
        for w in writes:
            self.lastw[w] = tok
            self.readers[w] = []
        for r in reads:
            if r not in writes:
                self.readers.setdefault(r, []).append(tok)

    def op(self, eng, fn, reads=(), writes=()):
        self._emit_waits(eng, self._deps(reads, writes, eng))
        sem = self.esem[eng]
        self.count[eng] += 1
        val = self.count[eng]
        self.prog[eng].append(lambda e, fn=fn, sem=sem: fn(e).then_inc(sem, 1))
        self._record(reads, writes, (sem, val))

    def dma(self, q, slot, fn_list, reads=(), writes=()):
        if slot not in self.slotsem:
            self.slotsem[slot] = self.free_sems.pop()
            self.slotcnt[slot] = 0
        sem = self.slotsem[slot]
        self._emit_waits(q, self._deps(reads, writes))
        for fn in fn_list:
            self.slotcnt[slot] += 16
            self.prog[q].append(lambda e, fn=fn, sem=sem: fn(e).then_inc(sem, 16))
        self._record(reads, writes, (sem, self.slotcnt[slot]))

    def finish(self, eng='sp'):
        deps = [(s, self.slotcnt[k]) for k, s in self.slotsem.items()]
        deps += [(self.esem[k], self.count[k]) for k in self.esem if self.count[k] > 0]
        self._emit_waits(eng, deps)

    def emit(self, block):
        progs = self.prog
        self.prog = {k: [] for k in progs}

        @block.tensor
        def _(e):
            for f in progs['pe']:
                f(e)

        @block.scalar
        def _(e):
            for f in progs['act']:
                f(e)

        @block.vector
        def _(e):
            for f in progs['dve']:
                f(e)

        @block.gpsimd
        def _(e):
            for f in progs['pool']:
                f(e)

        @block.sync
        def _(e):
            for f in progs['sp']:
                f(e)


def build_program(stop=None):
    nc = bass.Bass("TRN2", target_bir_lowering=False)
    import os as _os
    sub = int(_os.environ.get("KSUB", "99"))
    lvl = {None: 9, 'setup0': 0, 'setup': 1, 'meta': 2, 'f1': 3, 'frames': 4, 'p1': 5, 'w2': 6, 't2': 7}[stop]

    def din(name, shape):
        return nc.dram_tensor(name, list(shape), F32, kind="ExternalInput").ap()

    def dout(name, shape):
        return nc.dram_tensor(name, list(shape), F32, kind="ExternalOutput").ap()

    xp = din("xp", [SEQ, D]); xs = din("xs", [NSB * SL, D]); meta = din("meta", [NMETA, D])
    ckm = din("ckm", [NSB, NMETA, 128]); cvm = din("cvm", [NSB, NMETA, 128])
    ckw = din("ckw", [NSB, 128, 128]); cvw = din("cvw", [NSB, 128, 128])
    sca = din("sca", [NSB, CW - 1, DCONV]); scf = din("scf", [NSB, 2, DFF])
    win = din("win", [D, 1792]); wout = din("wout", [D, D]); wup = din("wup", [D, 2 * DFF]); wdn = din("wdn", [DFF, D])
    gmix = din("gmix", [D]); gffn = din("gffn", [D]); gfin = din("gfin", [D])
    cpar = din("cpar", [CW + 3, DCONV])
    fpar = din("fpar", [4, DFF])
    sinks = din("sinks", [8]); tab = din("tab", [32, 8]); eoh = din("eoh", [32, E_LEN])
    yp = dout("yp", [SEQ, D]); ys = dout("ys", [NSB * SL, D])
    kmp = dout("kmp", [NMETA, 128]); vmp = dout("vmp", [NMETA, 128])
    kwp = dout("kwp", [128, 128]); vwp = dout("vwp", [128, 128])
    cap = dout("cap", [CW - 1, DCONV]); cfp = dout("cfp", [2, DFF])
    kws = dout("kws", [NSB, 128, 128]); vws = dout("vws", [NSB, 128, 128])
    cas = dout("cas", [NSB, CW - 1, DCONV]); cfs = dout("cfs", [NSB, 2, DFF])
    x2 = nc.dram_tensor("x2", [NROWS, D], F32).ap()

    with ExitStack() as outer:
        ec = outer.enter_context
        sems = [ec(nc.semaphore(f"s{i}")) for i in range(60)]
        S = Sched(nc, sems)

        def sb(name, shape, dt=F32, ctx=None):
            return (ctx or ec)(nc.sbuf_tensor(name, list(shape), dt))

        NBANK = 8
        psb = [ec(nc.psum_tensor(f"ps{i}", [128, 512], F32)) for i in range(NBANK)]
        rr = {'b': 0}

        def bank():
            i = rr['b']; rr['b'] = (i + 1) % NBANK
            return psb[i], f"ps{i}"

        def tbank():
            ps, pk = bank()
            return ps[:].bitcast(BF16), pk

        identf = sb("identf", [128, 128]); ident = sb("ident", [128, 128], BF16)
        ones_bf = sb("ones_bf", [128, 128], BF16)
        gmix_b = sb("gmix_b", [128, D])
        cparT = sb("cparT", [128, 4, CW + 3])
        fparT = sb("fparT", [128, NJ, 4])
        epsc = sb("epsc", [128, 1]); mhalf = sb("mhalf", [128, 1])
        S.op('pool', lambda e: e.memset(mhalf[:], -0.5), writes=['mhalf'])
        S.op('pool', lambda e: e.memset(identf[:], 0.0), writes=['identf'])
        S.op('pool', lambda e: e.affine_select(out=identf[:], in_=identf[:], pattern=[[-1, 128]], compare_op=ALU.not_equal,
                                               fill=1.0, base=0, channel_multiplier=1), reads=['identf'], writes=['identf'])
        S.op('pool', lambda e: e.tensor_copy(ident[:], identf[:]), reads=['identf'], writes=['ident'])
        S.op('pool', lambda e: e.memset(ones_bf[:], 1.0), writes=['ones_bf'])
        S.dma('sp', 'c_g', [lambda e: e.dma_start(out=gmix_b[:], in_=gmix.partition_broadcast(128))], writes=['gb'])

        def act_copy(out, in_, reads, writes, scale=None):
            if scale is None:
                S.op('act', lambda e: e.activation(out=out, in_=in_, func=AF.Copy), reads=reads, writes=writes)
            else:
                S.op('act', lambda e: e.activation(out=out, in_=in_, func=AF.Copy, scale=scale), reads=reads, writes=writes)

        with ExitStack() as p1:
            pc = p1.enter_context

            def sb1(name, shape, dt=F32):
                return sb(name, shape, dt, ctx=pc)

            Win = sb1("Win", [128, 8, 1792], BF16); Wout = sb1("Wout", [128, 8, D], BF16)
            diag = sb1("diag", [128, 4 * CW, 128], BF16)
            XLQ = _os.environ.get("KXLQ", "pool")
            NXB = 3
            xt = [sb1(f"xt{i}", [128, 2, D]) for i in range(NXB)]
            hb = [sb1(f"hb{i}", [128, D], BF16) for i in range(2)]
            HTC = 128 + NT
            hT = [sb1(f"hT{i}", [128, 8, HTC], BF16) for i in range(2)]
            GLC = 30 + NT
            glb = [sb1(f"glb{i}", [128, 4, GLC], BF16) for i in range(2)]
            sA = sb1("sA", [128, 2, 256]); sBt = sb1("sBt", [128, 2, 256])
            glf = sb1("glf", [128, 4, NSB * SL])
            junk1 = glf[:].rearrange("p j c -> p (j c)")
            sig = [sb1(f"sig{i}", [128, NT]) for i in range(2)]
            cv = sb1("cv", [128, 4, NT]); cvb = sb1("cvb", [128, 4, NT], BF16); sq = sb1("sq", [128, 4, NT], BF16)
            lnm = sb1("lnm", [128, NT]); lnq = sb1("lnq", [128, NT]); lnv = sb1("lnv", [128, NT]); lnr = sb1("lnr", [128, NT])
            tn = [sb1(f"tn{i}", [128, NT]) for i in range(2)]
            qTk = [sb1(f"qT{i}", [128, 4, NT], BF16) for i in range(2)]
            KTl = sb1("KTl", [128, SEQ], BF16)
            KB = sb1("KB", [128, SEQ // CH, 80], BF16)
            KTm = sb1("KTm", [128, NMETA], BF16)
            VA = [sb1(f"VA{i}", [128, 4, 2, 128], BF16) for i in range(2)]
            VB = [sb1(f"VB{i}", [128, 4, 2, 128], BF16) for i in range(2)]
            VM = sb1("VM", [128, 2, 128], BF16)
            catT = sb1("catT", [128, 8, NT], BF16)
            pA = [sb1(f"pA{i}", [128, 2, 256], BF16) for i in range(2)]
            pB = [sb1(f"pB{i}", [128, 2, 256], BF16) for i in range(2)]
            rec = sb1("rec", [128, 2, 256])
            kvst = [sb1("kvst0", [128, 256])] * 2
            ost = sb1("ost", [32, DCONV])
            Eb = ost[:, E_LEN + 1:DCONV].bitcast(BF16)[:, 0:E_LEN]
            rs_ss = sb1("rs_ss", [128, 4]); rs_sd = sb1("rs_sd", [128, 4]); rs_r = sb1("rs_r", [128, 4])
            bA = sb1("bA", [128, 2, 4, 64]); bA1 = sb1("bA1", [64, 2, 4, 64])
            bB = [sb1(f"bB{c}", [80, 2, 4, 64]) for c in range(4)]
            bM = sb1("bM", [16, 2, 4, 16])
            bSA = sb1("bSA", [128, 2, 4, SL]); bSB = sb1("bSB", [48, 2, 4, SL])
            tabf = sb1("tabf", [32, 8]); tabd = sb1("tabd", [32, 8]); tab3 = sb1("tab3", [32, 3, 8], BF16)
            skf = sb1("skf", [1, 8]); ske = sb1("ske", [1, 8]); skd = sb1("skd", [1, 8]); sk3 = sb1("sk3", [1, 3, 8], BF16)
            sk3r = sb1("sk3r", [1, 3, 8, 64], BF16)
            msk = sb1("msk", [1, 2, 128], BF16)
            cparS = rec[0:CW + 3].rearrange("p k n -> p (k n)")
            ckw_b = sb1("ckw_b", [128, NSB, 128], BF16); ckm_b = sb1("ckm_b", [16, NSB, 128], BF16)
            ckw_s = cv[:, 0:2, :].rearrange("p a (b f) -> p (a b) f", b=2)
            cvw_s = cv[:, 2:4, :].rearrange("p a (b f) -> p (a b) f", b=2)
            ckm_s = sA[0:16].rearrange("p a (b f) -> p (a b) f", b=2)
            cvm_s = sBt[:].rearrange("p a (b f) -> p (a b) f", b=2)
            KTsA = sb1("KTsA", [128, NSB, 128], BF16); KBs = sb1("KBs", [128, NSB, 48], BF16)
            VAs = sb1("VAs", [128, NSB, 2, 128], BF16); VBs = sb1("VBs", [128, NSB, 2, 128], BF16)
            block1 = pc(nc.Block())

            late_setup = []

            def win_group(wkey, c0_, c1_):
                S.dma('pool', 'w_in' + wkey, [lambda e, kc=kc: e.dma_start(out=Win[:, kc, c0_:c1_], in_=win[kc * 128:(kc + 1) * 128, c0_:c1_]) for kc in range(8)], writes=[wkey])
            win_group('WinG', 0, 2 * DCONV)
            win_group('WinK', 1536, 1792)
            late_setup.append(lambda: win_group('WinQ', 2 * DCONV, 1536))
            S.op('dve', lambda e: e.tensor_scalar(out=Win[:, :, 0:DCONV], in0=Win[:, :, 0:DCONV], scalar1=0.5, scalar2=None, op0=ALU.mult), reads=['WinG'], writes=['WinG'])
            late_setup.append(lambda: S.dma('pool', 'w_out', [lambda e, kc=kc: e.dma_start(out=Wout[:, kc, :], in_=wout[kc * 128:(kc + 1) * 128, :]) for kc in range(8)], writes=['Wout']))
            S.dma('sp', 'c_par2', [lambda e: e.dma_start(out=cparS, in_=cpar)], writes=['rec0', 'rec1'])
            S.dma('sp', 'c_par', [
                                  lambda e: e.dma_start(out=ost[0:32, 0:E_LEN], in_=eoh), lambda e: e.dma_start(out=tabf[:], in_=tab),
                                  lambda e: e.dma_start(out=skf[:], in_=sinks.rearrange("(o h) -> o h", o=1))], writes=['par', 'ost'])
            S.dma('sp', 'c_cache', [lambda e: e.dma_start(out=ckw_s, in_=ckw.rearrange("b t f -> t b f")),
                                    lambda e: e.dma_start(out=ckm_s, in_=ckm.rearrange("b t f -> t b f")),
                                    lambda e: e.dma_start(out=cvw_s, in_=cvw.rearrange("b t f -> t b f")),
                                    lambda e: e.dma_start(out=cvm_s[32:48, :, :], in_=cvm.rearrange("b t f -> t b f"))], writes=['cv', 'sA', 'sBt'])
            S.dma('sp', 'o_roll', [lambda e: e.dma_start(out=kws[:, 0:96, :], in_=ckw[:, 32:128, :]),
                                   lambda e: e.dma_start(out=vws[:, 0:96, :], in_=cvw[:, 32:128, :])])

            for j in range(4):
                ps, pk = bank()
                S.op('pe', lambda e, ps=ps, j=j: e.transpose(ps[:, 0:CW + 3], cparS[:, j * 128:(j + 1) * 128], identf[0:CW + 3, 0:CW + 3]),
                     reads=['rec0', 'rec1', 'identf'], writes=[pk])
                act_copy(cparT[:, j, :], ps[:, 0:CW + 3], [pk], ['cparT'])
            def build_diag():
                for j in range(4):
                    S.op('dve', lambda e, j=j: e.tensor_tensor(out=diag[:, j * CW:(j + 1) * CW, :],
                                                               in0=identf[:].unsqueeze(1).broadcast_to([128, CW, 128]),
                                                               in1=cparT[:, j, 0:CW].unsqueeze(2).broadcast_to([128, CW, 128]), op=ALU.mult),
                         reads=['identf', 'cparT'], writes=['diag'])
            def split3(dst3, src, tmp, rkey, wkey):
                S.op('dve', lambda e: e.tensor_copy(tmp, src), reads=[rkey], writes=[wkey + 't'])
                for i in range(3):
                    S.op('dve', lambda e, i=i: e.tensor_copy(dst3[:, i, :], tmp), reads=[wkey + 't'], writes=[wkey])
                    if i < 2:
                        S.op('dve', lambda e, i=i: e.tensor_tensor(out=tmp, in0=tmp, in1=dst3[:, i, :], op=ALU.subtract), reads=[wkey + 't', wkey], writes=[wkey + 't'])
            S.op('act', lambda e: e.activation(out=ske[:], in_=skf[:], func=AF.Exp), reads=['par'], writes=['ske'])
            split3(sk3, ske[:], skd[:], 'ske', 'sk3')
            S.op('dve', lambda e: e.tensor_copy(Eb, ost[0:32, 0:E_LEN]), reads=['par', 'ost'], writes=['Eb', 'ost'])
            split3(tab3, tabf[:], tabd[:], 'par', 'tab3')
            S.op('dve', lambda e: e.tensor_copy(sk3r[:], sk3[:].unsqueeze(3).broadcast_to([1, 3, 8, 64])), reads=['sk3'], writes=['sk3r'])
            S.op('pool', lambda e: e.memset(msk[:], 0.0), writes=['msk'])
            S.op('pool', lambda e: e.memset(msk[:, 0, 64:128], 1.0), reads=['msk'], writes=['msk'])
            S.op('pool', lambda e: e.memset(msk[:, 1, 0:64], 1.0), reads=['msk'], writes=['msk'])

            NSPLIT = 2

            def gen_bias(dst, parts, nq):
                ps, pk = bank()
                psv = ps[:, 0:nq * 8].rearrange("p (q h) -> p q h", h=8)

                def f(e):
                    last = None
                    for (r0, M, s0) in parts:
                        for i in range(nq):
                            for t3 in range(NSPLIT):
                                last = e.matmul(psv[r0:r0 + M, i, :], lhsT=Eb[:, s0 - i:s0 - i + M], rhs=tab3[:, t3, :], start=(t3 == 0), stop=(t3 == NSPLIT - 1))
                    return last
                S.op('pe', f, reads=['Eb', 'tab3', 'ost'], writes=[pk])
                rows = max(r0 + M for (r0, M, s0) in parts)
                S.op('dve', lambda e: e.tensor_copy(dst[0:rows].rearrange("p k h q -> p (k h) q"), psv[0:rows].rearrange("p q h -> p h q")),
                     reads=[pk], writes=['bias'])

            gen_bias(bM, [(0, 16, E_OFF)], 16)
            bias_jobs = [lambda: gen_bias(bB[0], [(0, 64, E_OFF), (64, 16, E_OFF - 16)], 64),
                         lambda: gen_bias(bA1, [(0, 64, E_OFF - 64)], 64),
                         lambda: gen_bias(bB[1], [(0, 64, E_OFF), (64, 16, E_OFF - 16 - 64)], 64),
                         lambda: gen_bias(bA, [(0, 128, E_OFF - 128)], 64),
                         lambda: gen_bias(bB[2], [(0, 64, E_OFF), (64, 16, E_OFF - 16 - 128)], 64),
                         lambda: gen_bias(bB[3], [(0, 64, E_OFF), (64, 16, E_OFF - 16 - 192)], 64),
                         lambda: gen_bias(bSA, [(0, 128, E_OFF - 128)], SL),
                         lambda: gen_bias(bSB, [(0, 32, E_OFF), (32, 16, SL)], SL)]

            for t_ in (VAs, VBs):
                S.op('pool', lambda e, t_=t_: e.memset(t_[:], 1.0), writes=['Vinit'])

            def late_vinit():
                for t_ in (VA[0], VA[1], VB[0], VB[1], VM):
                    S.op('pool', lambda e, t_=t_: e.memset(t_[:], 1.0), writes=['Vinit'])
            late_setup.append(late_vinit)

            S.op('dve', lambda e: e.tensor_copy(ckw_b[:], ckw_s), reads=['cv'], writes=['ckw_b'])
            S.op('dve', lambda e: e.tensor_copy(ckm_b[:], ckm_s), reads=['sA'], writes=['ckm_b'])
            for s in range(NSB):
                pt, tk = tbank()
                S.op('pe', lambda e, pt=pt, s=s: e.transpose(pt[:, 0:128], ckw_b[:, s, :], ident[:]), reads=['ckw_b', 'ident'], writes=[tk])
                act_copy(KTsA[:, s, :], pt[:, 0:128], [tk], ['KTsA'])
                pt, tk = tbank()
                S.op('pe', lambda e, pt=pt, s=s: e.transpose(pt[:, 0:16], ckm_b[:, s, :], ident[0:16, 0:16]), reads=['ckm_b', 'ident'], writes=[tk])
                act_copy(KBs[:, s, 32:48], pt[:, 0:16], [tk], ['KBs'])
            S.op('dve', lambda e: e.tensor_copy(VAs[:, :, 0, 0:64], cvw_s[:, :, 0:64]), reads=['cv', 'Vinit'], writes=['VAs'])
            S.op('dve', lambda e: e.tensor_copy(VAs[:, :, 1, 64:128], cvw_s[:, :, 64:128]), reads=['cv', 'VAs'], writes=['VAs'])
            S.op('dve', lambda e: e.tensor_copy(VBs[32:48, :, 0, 0:64], cvm_s[32:48, :, 0:64]), reads=['sBt', 'Vinit'], writes=['VBs'])
            S.op('dve', lambda e: e.tensor_copy(VBs[32:48, :, 1, 64:128], cvm_s[32:48, :, 64:128]), reads=['sBt', 'VBs'], writes=['VBs'])

            def head_a(ti, src_fn, n, gb, keyp):
                xbuf = xt[ti % NXB]
                nb = (n + 127) // 128
                S.dma(XLQ, f'x{keyp}{ti % NXB}', src_fn(xbuf), writes=[f'xt{keyp}{ti % NXB}'])
                xk = f'xt{keyp}{ti % NXB}'
                for b in range(nb):
                    r = min(128, n - b * 128)
                    S.op('act', lambda e, b=b, r=r: e.activation(out=junk1[0:r, :], in_=xbuf[0:r, b, 0:512], func=AF.Square,
                                                                 accum_out=rs_ss[0:r, 2 * b:2 * b + 1]), reads=[xk], writes=['rs_ss', 'glf'])
                    S.op('act', lambda e, b=b, r=r: e.activation(out=junk1[0:r, :], in_=xbuf[0:r, b, 512:1024], func=AF.Square,
                                                                 accum_out=rs_ss[0:r, 2 * b + 1:2 * b + 2]), reads=[xk], writes=['rs_ss', 'glf'])
                    S.op('dve', lambda e, b=b, r=r: e.tensor_tensor(out=rs_sd[0:r, b:b + 1], in0=rs_ss[0:r, 2 * b:2 * b + 1],
                                                                    in1=rs_ss[0:r, 2 * b + 1:2 * b + 2], op=ALU.add), reads=['rs_ss'], writes=['rs_sd'])
                    S.op('dve', lambda e, b=b, r=r: e.tensor_scalar(out=rs_sd[0:r, 2 + b:3 + b], in0=rs_sd[0:r, b:b + 1], scalar1=1.0 / D, scalar2=EPS,
                                                                    op0=ALU.mult, op1=ALU.add), reads=['rs_sd'], writes=['rs_sd2'])
                    S.op('pool', lambda e, b=b, r=r: e.tensor_tensor(out=rs_r[0:r, b:b + 1], in0=rs_sd[0:r, 2 + b:3 + b], in1=mhalf[0:r, :], op=ALU.pow),
                         reads=['rs_sd2', 'mhalf'], writes=['rs_r'])
                    hbb = hb[b % 2]
                    S.op('dve', lambda e, b=b, r=r, hbb=hbb: e.scalar_tensor_tensor(out=hbb[0:r, :], in0=xbuf[0:r, b, :], scalar=rs_r[0:r, b:b + 1],
                                                                                    in1=gb[0:r, :], op0=ALU.mult, op1=ALU.mult),
                         reads=[xk, 'rs_r', 'gb'], writes=[f'hb{b % 2}'])

            def load_norm_transpose(ti, src_fn, n, xbuf, hTb, gb, keyp):
                nb = (n + 127) // 128
                for b in range(nb):
                    r = min(128, n - b * 128)
                    hbb = hb[b % 2]
                    if b > 0:
                        yield
                    pt, tk = tbank()

                    def tf(e, pt=pt, hbb=hbb, r=r):
                        last = None
                        for kc in range(8):
                            last = e.transpose(pt[:, kc * 128:kc * 128 + r], hbb[0:r, kc * 128:(kc + 1) * 128], ident[0:r, 0:r])
                        return last
                    S.op('pe', tf, reads=[f'hb{b % 2}', 'ident'], writes=[tk])
                    act_copy(hTb[:, :, 128 + b * 128:128 + b * 128 + r], pt[:].rearrange("p (k t) -> p k t", k=8)[:, :, 0:r], [tk], [f'hT{keyp}{ti % 2}'])

            S.op('pool', lambda e: e.memset(epsc[:], EPS), writes=['epsc'])

            def attn_A(qcols, nq, blocks, cat_cols, pidx):
                N = 4 * nq
                stiles = [sA, sBt]
                ptiles = [pA[pidx], pB[pidx]]
                pkeys = [f'pA{pidx}', f'pB{pidx}']
                skeys = ['sA', 'sBt']
                for bi, (KTap, M, VEap, bias_ap, rkeys) in enumerate(blocks):
                    ps, pk = bank()

                    def sf(e, ps=ps, KTap=KTap, M=M):
                        last = None
                        for kv in range(2):
                            last = e.matmul(ps[0:M, kv * 256:kv * 256 + N], lhsT=KTap,
                                            rhs=qTk[kv][:, :, qcols:qcols + nq], start=True, stop=True)
                        return last
                    S.op('pe', sf, reads=['qT'] + rkeys, writes=[pk])
                    st = stiles[bi]
                    S.op('dve', lambda e, ps=ps, st=st, M=M, bias_ap=bias_ap: e.tensor_tensor(
                        out=st[0:M, :, 0:N], in0=ps[0:M, :].rearrange("p (k n) -> p k n", k=2)[:, :, 0:N],
                        in1=bias_ap.rearrange("p k h q -> p k (h q)"), op=ALU.add), reads=[pk, 'bias'], writes=[skeys[bi]])
                    S.op('act', lambda e, st=st, M=M, pt_=ptiles[bi]: e.activation(out=pt_[0:M, :, 0:N], in_=st[0:M, :, 0:N], func=AF.Exp),
                         reads=[skeys[bi]], writes=[pkeys[bi]])

            def attn_B(qcols, nq, blocks, cat_cols, pidx):
                N = 4 * nq
                ptiles = [pA[pidx], pB[pidx]]
                pkeys = [f'pA{pidx}', f'pB{pidx}']
                po, pok = bank()

                def of(e, po=po):
                    last = None
                    for kv in range(2):
                        for bi, (KTap, M, VEap, bias_ap, rkeys) in enumerate(blocks):
                            e.matmul(po[:, kv * 256:kv * 256 + N], lhsT=VEap[0:M, kv, :], rhs=ptiles[bi][0:M, kv, 0:N], start=(bi == 0), stop=False)
                        for t3 in range(3):
                            last = e.matmul(po[:, kv * 256:kv * 256 + N], lhsT=msk[:, kv, :], rhs=sk3r[:, t3, kv * 4:kv * 4 + 4, 0:nq], start=False, stop=(t3 == 2))
                    return last
                vr = []
                for b_ in blocks:
                    vr += b_[4]
                S.op('pe', of, reads=[pkeys[i] for i in range(len(blocks))] + vr + ['msk', 'sk3r'], writes=[pok])
                S.op('dve', lambda e, po=po: e.reciprocal(rec[64:128, 0, 0:N], po[64:128, 0:N]), reads=[pok], writes=['rec0'])
                S.op('dve', lambda e, po=po: e.reciprocal(rec[0:64, 1, 0:N], po[0:64, 256:256 + N]), reads=[pok], writes=['rec1'])
                S.op('dve', lambda e, po=po: e.tensor_tensor(out=catT[0:64, 4:8, cat_cols:cat_cols + nq],
                                                             in0=po[0:64, 0:N].rearrange("p (h q) -> p h q", h=4),
                                                             in1=rec[64:128, 0, 0:N].rearrange("p (h q) -> p h q", h=4), op=ALU.mult),
                     reads=[pok, 'rec0'], writes=['catT'])
                S.op('dve', lambda e, po=po: e.tensor_tensor(out=catT[64:128, 4:8, cat_cols:cat_cols + nq],
                                                             in0=po[64:128, 256:256 + N].rearrange("p (h q) -> p h q", h=4),
                                                             in1=rec[0:64, 1, 0:N].rearrange("p (h q) -> p h q", h=4), op=ALU.mult),
                     reads=[pok, 'rec1'], writes=['catT'])

            def kv_window(hTb, hk, c0, M, extra_reads=()):
                ps, pk = bank()

                def f(e, ps=ps):
                    last = None
                    for kc in range(8):
                        last = e.matmul(ps[0:M, 0:256], lhsT=hTb[:, kc, c0:c0 + M], rhs=Win[:, kc, 1536:1792], start=(kc == 0), stop=(kc == 7))
                    return last
                S.op('pe', f, reads=[hk, 'WinK'] + list(extra_reads), writes=[pk])
                return ps, pk

            def v_to(ps, pk, M, dst, dkey, r0=0):
                S.op('act', lambda e: e.activation(out=dst[r0:r0 + M, 0, 0:64], in_=ps[r0:r0 + M, 128:192], func=AF.Copy), reads=[pk, 'Vinit'], writes=[dkey])
                S.op('act', lambda e: e.activation(out=dst[r0:r0 + M, 1, 64:128], in_=ps[r0:r0 + M, 192:256], func=AF.Copy), reads=[pk, dkey], writes=[dkey])

            stc = {'n': 0}

            def kv_out(ps, pk, r0, M, kdst, vdst):
                i = 0
                st = kvst[i]
                S.op('dve', lambda e: e.tensor_copy(st[r0:r0 + M, :], ps[r0:r0 + M, 0:256]), reads=[pk], writes=[f'kvst{i}'])
                if int(_os.environ.get("KDMA", "2")) >= 1:
                    S.dma('sp', f'o_kv{i}', [lambda e: e.dma_start(out=kdst, in_=st[r0:r0 + M, 0:128])], reads=[f'kvst{i}'])
                if int(_os.environ.get("KDMA", "2")) >= 2:
                    S.dma('sp', f'o_kv{i}', [lambda e: e.dma_start(out=vdst, in_=st[r0:r0 + M, 128:256])], reads=[f'kvst{i}'])

            def mixer_tile(ti, kind, n, src_fn, x2rows, tchunk0=None, nseg=1, pre=None, post_glu=None):
                xbuf = xt[ti % NXB]; hTb = hT[ti % 2]; hTp = hT[(ti + 1) % 2]
                gl = glb[ti % 2]; gln = glb[(ti + 1) % 2]
                glk = f'glb{ti % 2}'; glnk = f'glb{(ti + 1) % 2}'
                nb = (n + 127) // 128
                seglen = n // nseg
                if kind == 'frames' and ti > 0:
                    pass
                if pre is not None:
                    pre()
                xk, hk = f'xta{ti % NXB}', f'hTa{ti % 2}'
                if sub <= 1:
                    return
                if kind == 'frames':
                    S.op('pool', lambda e: e.tensor_copy(hTp[:, :, 0:128], hTb[:, :, NT:NT + 128]), reads=[hk], writes=[f'hTa{(ti + 1) % 2}t'])
                hreads = [hk, f'hTa{ti % 2}t'] if (kind == 'frames' and ti > 1) else [hk]

                def segview(ap3, off):
                    return ap3

                for j in range(4):
                    pa, pak = bank(); pb, pbk = bank()

                    def f(e, pa=pa, pb=pb, j=j):
                        last = None
                        for (ps, fc) in ((pa, j), (pb, 4 + j)):
                            for kc in range(8):
                                last = e.matmul(ps[:, 0:n], lhsT=Win[:, kc, fc * 128:(fc + 1) * 128], rhs=hTb[:, kc, 128:128 + n], start=(kc == 0), stop=(kc == 7))
                        return last
                    S.op('pe', f, reads=[hk, 'WinG'], writes=[pak, pbk])
                    sg = sig[j % 2]
                    S.op('act', lambda e, pb=pb, sg=sg: e.activation(out=sg[:, 0:n], in_=pb[:, 0:n], func=AF.Tanh, scale=0.5), reads=[pbk], writes=[f'sig{j % 2}'])
                    if nseg == 1:
                        gdst = gl[:, j, 30:30 + n]; gin0 = pa[:, 0:n]; gin1 = sg[:, 0:n]
                    else:
                        gdst = gl[:, j, 0:nseg * (30 + seglen)].rearrange("p (s c) -> p s c", s=nseg)[:, :, 30:30 + seglen]
                        gin0 = pa[:, 0:n].rearrange("p (s c) -> p s c", s=nseg); gin1 = sg[:, 0:n].rearrange("p (s c) -> p s c", s=nseg)
                    S.op('dve', lambda e, gdst=gdst, gin0=gin0, gin1=gin1: e.scalar_tensor_tensor(out=gdst, in0=gin1, scalar=1.0, in1=gin0, op0=ALU.add, op1=ALU.mult),
                         reads=[pak, f'sig{j % 2}'], writes=[glk])
                    if post_glu is not None:
                        c0_, w_ = (n - 30, 30) if nseg == 1 else (0, n)
                        S.op('dve', lambda e, pa=pa, sg=sg, j=j, c0_=c0_, w_=w_: e.scalar_tensor_tensor(out=glf[:, j, 0:w_], in0=sg[:, c0_:c0_ + w_], scalar=1.0, in1=pa[:, c0_:c0_ + w_], op0=ALU.add, op1=ALU.mult),
                             reads=[pak, f'sig{j % 2}'], writes=['glf'])
                    yield
                if sub <= 2:
                    return
                if post_glu is not None:
                    post_glu()
                yield
                def proj_fm(fc):
                    ps, pk = bank()

                    def f(e, ps=ps, fc=fc):
                        last = None
                        for kc in range(8):
                            last = e.matmul(ps[:, 0:n], lhsT=Win[:, kc, fc * 128:(fc + 1) * 128], rhs=hTb[:, kc, 128:128 + n], start=(kc == 0), stop=(kc == 7))
                        return last
                    S.op('pe', f, reads=[hk, 'WinK' if fc == 12 else 'WinQ'], writes=[pk])
                    return ps, pk
                for fc in (12,):
                    ps, pk = proj_fm(fc)
                    if kind == 'meta':
                        act_copy(KTm[:, :], ps[:, 0:n], [pk], ['KTm'])
                        S.op('pool', lambda e: e.tensor_copy(KB[:, :, 64:80], KTm[:].unsqueeze(1).broadcast_to([128, SEQ // CH, 16])), reads=['KTm'], writes=['KBm'])
                    elif kind == 'frames':
                        f0 = tchunk0 * CH
                        act_copy(KTl[:, f0:f0 + n], ps[:, 0:n], [pk], [f'KTl{ti}'])
                        S.op('dve', lambda e, ps=ps: e.tensor_copy(KB[:, tchunk0:tchunk0 + n // CH, 0:64], ps[:, 0:n].rearrange("p (c t) -> p c t", t=CH)),
                             reads=[pk], writes=[f'KBo{ti}'])
                    else:
                        S.op('dve', lambda e, ps=ps: e.tensor_copy(KBs[:, :, 0:SL], ps[:, 0:n].rearrange("p (s t) -> p s t", t=SL)), reads=[pk], writes=['KBs'])
                    yield
                if sub <= 3:
                    return
                yield
                if kind == 'meta':
                    kvar = int(_os.environ.get("KVAR", "9"))
                    ps, pk = kv_window(hTb, hk, 128, NMETA)
                    if kvar >= 1:
                        v_to(ps, pk, NMETA, VM, 'VM')
                    if kvar >= 2:
                        kv_out(ps, pk, 0, NMETA, kmp, vmp)
                    if kvar >= 3:
                        ps, pk = kv_window(hTb, hk, 64, 80)
                    if kvar >= 4:
                        for i in range(2):
                            for lc in range(4):
                                v_to(ps, pk, NMETA, VB[i][:, lc], f'VB{i}', r0=64)
                elif kind == 'frames':
                    tp = ti % 2
                    for lc in range(4):
                        c = tchunk0 + lc
                        ps, pk = kv_window(hTb, hk, 128 + lc * CH, CH)
                        v_to(ps, pk, CH, VB[tp][:, lc], f'VB{tp}')
                        if c == 1:
                            ps, pk = kv_window(hTb, hk, 128, CH)
                            v_to(ps, pk, CH, VA[tp][:, lc], f'VA{tp}')
                        elif c >= 2:
                            ps, pk = kv_window(hTb, hk, lc * CH, 128, extra_reads=[f'hTa{ti % 2}t'])
                            v_to(ps, pk, 128, VA[tp][:, lc], f'VA{tp}')
                        yield
                    if tchunk0 + 4 == SEQ // CH:
                        ps, pk = kv_window(hTb, hk, NT, 128)
                        kv_out(ps, pk, 0, 128, kwp, vwp)
                else:
                    for s in range(NSB):
                        ps, pk = kv_window(hTb, hk, 128 + s * SL, SL)
                        v_to(ps, pk, SL, VBs[:, s], 'VBs')
                        kv_out(ps, pk, 0, SL, kws[s, 96:128, :], vws[s, 96:128, :])
                        yield
                if sub <= 4:
                    return
                yield
                seg_stride = 30 + seglen
                for j in range(4):
                    ps, pk = bank()

                    def f(e, ps=ps, j=j):
                        last = None
                        for k in range(CW):
                            if nseg == 1:
                                rhs = gl[:, j, k:k + n]
                            else:
                                rhs = gl[:, j, 0:nseg * seg_stride].rearrange("p (s c) -> p s c", s=nseg)[:, :, k:k + seglen]
                            last = e.matmul(ps[:, 0:n], lhsT=diag[:, j * CW + k, :], rhs=rhs, start=(k == 0), stop=(k == CW - 1))
                        return last
                    S.op('pe', f, reads=[glk, 'diag'], writes=[pk])
                    S.op('act', lambda e, ps=ps, j=j: e.activation(out=cv[:, j, 0:n], in_=ps[:, 0:n], func=AF.Identity, bias=cparT[:, j, CW:CW + 1]),
                         reads=[pk, 'cparT'], writes=['cv'])
                    yield
                if kind in ('meta', 'frames'):
                    S.op('pool', lambda e: e.tensor_copy(gln[:, :, 0:30], gl[:, :, n:n + 30]), reads=[glk], writes=[glnk])
                if sub <= 5:
                    return
                yield
                S.op('dve', lambda e: e.tensor_copy(cvb[:, :, 0:n], cv[:, :, 0:n]), reads=['cv'], writes=['cvb'])
                S.op('dve', lambda e: e.tensor_tensor(out=sq[:, :, 0:n], in0=cv[:, :, 0:n], in1=cv[:, :, 0:n], op=ALU.mult), reads=['cv'], writes=['sq'])
                yield
                p1_, p1k = bank(); p2_, p2k = bank()

                def f(e):
                    last = None
                    for (ps, src) in ((p1_, cvb), (p2_, sq)):
                        for j in range(4):
                            last = e.matmul(ps[:, 0:n], lhsT=ones_bf[:], rhs=src[:, j, 0:n], start=(j == 0), stop=(j == 3))
                    return last
                S.op('pe', f, reads=['cvb', 'sq', 'ones_bf'], writes=[p1k, p2k])
                act_copy(lnm[:, 0:n], p1_[:, 0:n], [p1k], ['lnm'], scale=1.0 / DCONV)
                S.op('dve', lambda e: e.tensor_tensor(out=lnq[:, 0:n], in0=lnm[:, 0:n], in1=lnm[:, 0:n], op=ALU.mult), reads=['lnm'], writes=['lnq'])
                S.op('dve', lambda e: e.scalar_tensor_tensor(out=lnv[:, 0:n], in0=p2_[:, 0:n], scalar=1.0 / DCONV, in1=lnq[:, 0:n], op0=ALU.mult, op1=ALU.subtract),
                     reads=[p2k, 'lnq'], writes=['lnv'])
                S.op('act', lambda e: e.activation(out=lnq[:, 0:n], in_=lnv[:, 0:n], func=AF.Sqrt, bias=epsc[:, :]), reads=['lnv', 'epsc'], writes=['lnq'])
                S.op('dve', lambda e: e.reciprocal(lnr[:, 0:n], lnq[:, 0:n]), reads=['lnq'], writes=['lnr'])
                yield
                for j in range(4):
                    t_ = tn[j % 2]; tk_ = f'tn{j % 2}'
                    S.op('dve', lambda e, t_=t_, j=j: e.tensor_tensor(out=t_[:, 0:n], in0=cv[:, j, 0:n], in1=lnm[:, 0:n], op=ALU.subtract), reads=['cv', 'lnm'], writes=[tk_])
                    S.op('dve', lambda e, t_=t_: e.tensor_tensor(out=t_[:, 0:n], in0=t_[:, 0:n], in1=lnr[:, 0:n], op=ALU.mult), reads=[tk_, 'lnr'], writes=[tk_])
                    S.op('act', lambda e, t_=t_, j=j: e.activation(out=catT[:, j, 0:n], in_=t_[:, 0:n], func=AF.Silu, scale=cparT[:, j, CW + 1:CW + 2],
                                                                   bias=cparT[:, j, CW + 2:CW + 3]), reads=[tk_, 'cparT'], writes=['catT'])
                if sub <= 6:
                    return
                yield
                for fc in range(8, 12):
                    ps, pk = proj_fm(fc)
                    act_copy(qTk[0][0:64, fc - 8, 0:n], ps[0:64, 0:n], [pk], ['qT'], scale=0.125)
                    act_copy(qTk[1][64:128, fc - 8, 0:n], ps[64:128, 0:n], [pk], ['qT'], scale=0.125)
                    yield
                units = []
                if kind == 'meta':
                    units.append((0, NMETA, [(KTm[:, :], NMETA, VM, bM[:], ['KTm', 'VM'])], 0, 0))
                elif kind == 'frames':
                    tp = ti % 2
                    for lc in range(4):
                        c = tchunk0 + lc
                        blocks = []
                        if c == 1:
                            blocks.append((KTl[:, 0:CH], CH, VA[tp][:, lc], bA1[:], [f'KTl{ti}', f'VA{tp}']))
                        elif c >= 2:
                            blocks.append((KTl[:, (c - 2) * CH:c * CH], 128, VA[tp][:, lc], bA[:], [f'KTl{ti}', f'KTl{ti - 1}', f'VA{tp}']))
                        blocks.append((KB[:, c, :], 80, VB[tp][:, lc], bB[min(c, 3)][:], [f'KBo{ti}', 'KBm', f'VB{tp}']))
                        units.append((lc * CH, CH, blocks, lc * CH, lc % 2))
                else:
                    for s in range(NSB):
                        blocks = [(KTsA[:, s, :], 128, VAs[:, s], bSA[:], ['KTsA', 'VAs']),
                                  (KBs[:, s, :], 48, VBs[:, s], bSB[:], ['KBs', 'VBs'])]
                        units.append((s * SL, SL, blocks, s * SL, s % 2))
                pendB = None
                for u in units:
                    attn_A(*u)
                    yield
                    if pendB is not None:
                        attn_B(*pendB)
                        yield
                    pendB = u
                attn_B(*pendB)
                if sub <= 7:
                    return
                yield
                for b in range(nb):
                    r = min(128, n - b * 128)
                    pa, pak = bank(); pb, pbk = bank()

                    def f(e, pa=pa, pb=pb, b=b, r=r):
                        last = None
                        for (ps, half) in ((pa, 0), (pb, 1)):
                            for kc in range(8):
                                last = e.matmul(ps[0:r, :], lhsT=catT[:, kc, b * 128:b * 128 + r], rhs=Wout[:, kc, half * 512:(half + 1) * 512], start=(kc == 0), stop=(kc == 7))
                        return last
                    S.op('pe', f, reads=['catT', 'Wout'], writes=[pak, pbk])
                    S.op('dve', lambda e, pa=pa, b=b, r=r: e.tensor_tensor(out=xbuf[0:r, b, 0:512], in0=pa[0:r, :], in1=xbuf[0:r, b, 0:512], op=ALU.add), reads=[pak, xk], writes=[xk])
                    S.op('dve', lambda e, pb=pb, b=b, r=r: e.tensor_tensor(out=xbuf[0:r, b, 512:1024], in0=pb[0:r, :], in1=xbuf[0:r, b, 512:1024], op=ALU.add), reads=[pbk, xk], writes=[xk])
                    if b + 1 < nb:
                        yield
                S.dma('sp', f'o_x2{ti % NXB}', [lambda e, b=b: e.dma_start(out=x2[x2rows + b * 128:x2rows + b * 128 + min(128, n - b * 128), :],
                                                                       in_=xbuf[0:min(128, n - b * 128), b, :]) for b in range(nb)], reads=[xk], writes=['x2d'])

            def glu_out(n, off, dst):
                ps, pk = bank()

                def f(e, ps=ps):
                    last = None
                    for j in range(4):
                        last = e.transpose(ps[0:30, j * 128:(j + 1) * 128], glf[:, j, off:off + 30], identf[:])
                    return last
                S.op('pe', f, reads=['glf', 'identf'], writes=[pk])
                S.op('dve', lambda e, ps=ps: e.tensor_copy(ost[0:30, :], ps[0:30, :]), reads=[pk], writes=['ost'])
                S.dma('sp', 'o_glu', [lambda e: e.dma_start(out=dst, in_=ost[0:30, :])], reads=['ost'])

            def late_qt():
                for i in range(2):
                    S.op('pool', lambda e, i=i: e.memset(qTk[i][:], 0.0), writes=['qT'])
            late_setup.append(late_qt)
            late_setup.append(build_diag)
            for i in range(2):
                S.op('pool', lambda e, i=i: e.memset(hT[i][:], 0.0), writes=[f'hTa{i}', f'hTa{i}t'])
            S.op('pool', lambda e: e.memset(glb[0][:], 0.0), writes=['glb0'])
            gens = []
            heads = []
            head_t = []
            def meta_with_bias():
                k = 0
                for _ in mixer_tile(0, 'meta', NMETA, lambda xb: [lambda e: e.dma_start(out=xb[0:NMETA, 0, :], in_=meta)], 0):
                    yield
                    k += 1
                    while late_setup:
                        late_setup.pop(0)()
                    if bias_jobs and k >= 3:
                        bias_jobs.pop(0)()
                while bias_jobs:
                    bias_jobs.pop(0)()
            if lvl >= 2:
                gens.append(meta_with_bias)
                heads.append(lambda: head_a(0, lambda xb: [lambda e: e.dma_start(out=xb[0:NMETA, 0, :], in_=meta)], NMETA, gmix_b, 'a'))
                head_t.append(lambda: load_norm_transpose(0, None, NMETA, None, hT[0], gmix_b, 'a'))
            for t in range(SEQ // NT if lvl >= 4 else (1 if lvl >= 3 else 0)):
                gens.append(lambda t=t: mixer_tile(t + 1, 'frames', NT, lambda xb, t=t: [lambda e: e.dma_start(out=xb[:], in_=xp[t * NT:(t + 1) * NT, :].rearrange("(b p) d -> p b d", p=128))],
                                                   NMETA + t * NT, tchunk0=t * (NT // CH),
                                                   post_glu=(lambda: glu_out(NT, 0, cap)) if t == SEQ // NT - 1 else None))
                heads.append(lambda t=t: head_a(t + 1, lambda xb, t=t: [lambda e: e.dma_start(out=xb[:], in_=xp[t * NT:(t + 1) * NT, :].rearrange("(b p) d -> p b d", p=128))], NT, gmix_b, 'a'))
                head_t.append(lambda t=t: load_norm_transpose(t + 1, None, NT, None, hT[(t + 1) % 2], gmix_b, 'a'))
            ti = SEQ // NT + 1
            gls = glb[ti % 2]

            def sample_pre():
              for s in range(NSB):
                sca_s = ost[0:CW - 1, :]
                S.dma('sp', 'c_sca', [lambda e, s=s: e.dma_start(out=sca_s, in_=sca[s])], writes=['ost'])
                ps, pk = bank()

                def f(e, ps=ps, s=s, sca_s=sca_s):
                    last = None
                    for j in range(4):
                        last = e.transpose(ps[:, j * 32:j * 32 + 30], sca_s[:, j * 128:(j + 1) * 128], identf[0:30, 0:30])
                    return last
                S.op('pe', f, reads=['ost', 'identf'], writes=[pk])
                S.op('dve', lambda e, ps=ps, s=s: e.tensor_copy(gls[:, :, s * 62:s * 62 + 30], ps[:, 0:128].rearrange("p (j c) -> p j c", j=4)[:, :, 0:30]),
                     reads=[pk], writes=[f'glb{ti % 2}'])

            def sample_post_glu():
                for s in range(NSB):
                    glu_out(NSB * SL, s * SL + 2, cas[s])
            if lvl >= 5:
                gens.append(lambda: mixer_tile(ti, 'sample', NSB * SL, lambda xb: [lambda e: e.dma_start(out=xb[:, 0, :], in_=xs)], NMETA + SEQ, nseg=NSB,
                                               pre=sample_pre, post_glu=sample_post_glu))
                heads.append(lambda: head_a(ti, lambda xb: [lambda e: e.dma_start(out=xb[:, 0, :], in_=xs)], NSB * SL, gmix_b, 'a'))
                head_t.append(lambda: load_norm_transpose(ti, None, NSB * SL, None, hT[ti % 2], gmix_b, 'a'))
            PSKEW = int(_os.environ.get("KSKEW", "16"))
            active = []
            tick = 0
            nxt = 0
            last_start = -10 ** 9
            HLEAD = int(_os.environ.get("KHLEAD", "14"))
            heads_emitted = 0
            started = {}
            while active or nxt < len(gens):
                if nxt < len(gens) and len(active) < 2 and tick - last_start >= PSKEW and heads_emitted > nxt:
                    active.append((nxt, gens[nxt]())); started[nxt] = tick; nxt += 1
                    last_start = tick
                if heads_emitted < len(heads):
                    h = heads_emitted
                    if h == 0 or (h - 1 in started and tick - started[h - 1] >= HLEAD):
                        S.cur_tile = h
                        heads[h]()
                        heads_emitted += 1
                        for _ in head_t[h]():
                            pass
                for (tid, g) in (list(active)[::-1] if _os.environ.get("KORD", "0") == "1" else list(active)):
                    S.cur_tile = tid
                    try:
                        next(g)
                    except StopIteration:
                        active.remove((tid, g))
                tick += 1
            S.cur_tile = -1
            S.finish('sp')
            S.emit(block1)

        with ExitStack() as p2:
            pc = p2.enter_context

            def sb2(name, shape, dt=F32):
                return sb(name, shape, dt, ctx=pc)

            Wup = sb2("Wup", [128, 8, 2 * DFF], BF16); Wdn = sb2("Wdn", [128, NJ, D], BF16)
            xq = [sb2(f"xq{i}", [128, 2, D]) for i in range(2)]
            hb2 = [sb2(f"hc{i}", [128, D], BF16) for i in range(2)]
            h2T = [sb2(f"h2T{i}", [128, 8, NT], BF16) for i in range(2)]
            GSC = 2 + NT
            gs = [sb2(f"gs{i}", [128, GSC]) for i in range(2)]
            cacc = [sb2(f"cacc{i}", [128, NT]) for i in range(2)]
            sl = [sb2(f"sl{i}", [128, NT]) for i in range(2)]
            yT = sb2("yT", [128, NJ, NT], BF16)
            ghist = sb2("ghist", [128, NJ, 2])
            scfT = sb2("scfT", [128, NSB, NJ, 2])
            gst = sb2("gst", [128, 1 + NSB, 2, NJ])
            gso = sb2("gso", [64, 128])
            r_ss = sb2("r_ss", [128, 4]); r_sd = sb2("r_sd", [128, 4]); r_r = sb2("r_r", [128, 4])
            f_ss = sb2("f_ss", [128, 4]); f_sd = sb2("f_sd", [128, 4]); f_r = sb2("f_r", [128, 4])
            eps2 = sb2("eps2", [128, 1]); junk = sb2("junk", [128, 512])
            gffn_b = sb2("gffn_b", [128, D]); gfin_b = sb2("gfin_b", [128, D])
            fparS = yT[0:4].rearrange("p j t -> p (j t)").bitcast(F32)
            assert tuple(fparS.shape) == (4, DFF), fparS.shape
            block2 = pc(nc.Block())
            S.dma('sp', 'c_g2', [lambda e: e.dma_start(out=gffn_b[:], in_=gffn.partition_broadcast(128)),
                                 lambda e: e.dma_start(out=gfin_b[:], in_=gfin.partition_broadcast(128)),
                                 lambda e: e.dma_start(out=fparS, in_=fpar)], writes=['gb2', 'yT'])
            for j in range(NJ):
                ps, pk = bank()
                S.op('pe', lambda e, ps=ps, j=j: e.transpose(ps[:, 0:4], fparS[:, j * 128:(j + 1) * 128], identf[0:4, 0:4]),
                     reads=['gb2', 'yT', 'identf'], writes=[pk])
                act_copy(fparT[:, j, :], ps[:, 0:4], [pk], ['fparT'])
            S.op('pool', lambda e: e.memset(eps2[:], EPS), writes=['eps2'])
            S.op('pool', lambda e: e.memset(ghist[:], 0.0), writes=[f'ghist{j}' for j in range(NJ)])

            CB = 512
            ncb = (DFF + CB - 1) // CB
            wq = []
            for cbk in range(ncb if lvl >= 6 else 0):
                c0 = cbk * CB; c1 = min(DFF, c0 + CB)
                wq.append(lambda cbk=cbk, c0=c0, c1=c1: S.dma('pool', f'wg{cbk}', [lambda e, kc=kc: e.dma_start(out=Wup[:, kc, c0:c1], in_=wup[kc * 128:(kc + 1) * 128, c0:c1]) for kc in range(8)], writes=[f'Wg{cbk}']))
                wq.append(lambda cbk=cbk, c0=c0, c1=c1: S.dma('pool', f'wu{cbk}', [lambda e, kc=kc: e.dma_start(out=Wup[:, kc, DFF + c0:DFF + c1], in_=wup[kc * 128:(kc + 1) * 128, DFF + c0:DFF + c1]) for kc in range(8)], writes=[f'Wu{cbk}']))
            for j0 in range(0, NJ if lvl >= 6 else 0, 4):
                j1 = min(NJ, j0 + 4)
                wq.append(lambda j0=j0, j1=j1: S.dma('pool', f'wd{j0}', [lambda e, j=j: e.dma_start(out=Wdn[:, j, :], in_=wdn[j * 128:(j + 1) * 128, :]) for j in range(j0, j1)], writes=[f'Wd{j0 // 4}']))
            for _ in range(min(4, len(wq))):
                wq.pop(0)()
            scfS = gso[0:NJ, :]

            def load_scf():
                for s in range(NSB):
                    for t in range(2):
                        S.dma('sp', 'c_scf', [lambda e, s=s, t=t: e.dma_start(out=scfS, in_=scf[s, t, :].rearrange("(j p) -> j p", p=128))], writes=['gso'])
                        ps, pk = bank()
                        S.op('pe', lambda e, ps=ps: e.transpose(ps[:, 0:NJ], scfS, identf[0:NJ, 0:NJ]), reads=['gso', 'identf'], writes=[pk])
                        act_copy(scfT[:, s, :, t], ps[:, 0:NJ], [pk], ['scfT'])

            def out_rows(r0, r):
                res = []
                a, bnd = r0, r0 + r
                for (lo, hi, dst, base) in ((NMETA, NMETA + SEQ, yp, NMETA), (NMETA + SEQ, NROWS, ys, NMETA + SEQ)):
                    s_, e_ = max(a, lo), min(bnd, hi)
                    if s_ < e_:
                        res.append((s_ - r0, e_ - s_, dst[s_ - base:e_ - base, :]))
                return res

            usb = [sb2(f"usb{i}", [128, NT], BF16) for i in range(2)]

            def ffn_head(ti, r0, n, groups):
                xbuf = xq[ti % 2]; xk = f'xq{ti % 2}'
                nb = (n + 127) // 128
                hTb = h2T[ti % 2]; hk = f'h2T{ti % 2}'
                S.dma('sp', f'xq{ti % 2}', [lambda e, b=b: e.dma_start(out=xbuf[0:min(128, n - b * 128), b, :], in_=x2[r0 + b * 128:r0 + b * 128 + min(128, n - b * 128), :]) for b in range(nb)],
                      reads=['x2d'], writes=[xk])
                for b in range(nb):
                    r = min(128, n - b * 128)
                    for hf in range(2):
                        S.op('act', lambda e, b=b, r=r, hf=hf: e.activation(out=junk[0:r, :], in_=xbuf[0:r, b, hf * 512:(hf + 1) * 512], func=AF.Square,
                                                                            accum_out=r_ss[0:r, 2 * b + hf:2 * b + hf + 1]), reads=[xk], writes=['r_ss'])
                    S.op('dve', lambda e, b=b, r=r: e.tensor_tensor(out=r_sd[0:r, b:b + 1], in0=r_ss[0:r, 2 * b:2 * b + 1], in1=r_ss[0:r, 2 * b + 1:2 * b + 2], op=ALU.add), reads=['r_ss'], writes=['r_sd'])
                    S.op('dve', lambda e, b=b, r=r: e.tensor_scalar(out=r_sd[0:r, 2 + b:3 + b], in0=r_sd[0:r, b:b + 1], scalar1=1.0 / D, scalar2=EPS, op0=ALU.mult, op1=ALU.add), reads=['r_sd'], writes=['r_sd2'])
                    S.op('pool', lambda e, b=b, r=r: e.tensor_tensor(out=r_r[0:r, b:b + 1], in0=r_sd[0:r, 2 + b:3 + b], in1=mhalf[0:r, :], op=ALU.pow), reads=['r_sd2', 'mhalf'], writes=['r_r'])
                    hbb = hb2[b % 2]
                    S.op('dve', lambda e, b=b, r=r, hbb=hbb: e.scalar_tensor_tensor(out=hbb[0:r, :], in0=xbuf[0:r, b, :], scalar=r_r[0:r, b:b + 1], in1=gffn_b[0:r, :], op0=ALU.mult, op1=ALU.mult),
                         reads=[xk, 'r_r', 'gb2'], writes=[f'hc{b % 2}'])

            def ffn_head_b(ti, r0, n, groups):
                nb = (n + 127) // 128
                hTb = h2T[ti % 2]; hk = f'h2T{ti % 2}'
                for b in range(nb):
                    r = min(128, n - b * 128)
                    hbb = hb2[b % 2]
                    pt, tk = tbank()

                    def tf(e, pt=pt, hbb=hbb, r=r):
                        last = None
                        for kc in range(8):
                            last = e.transpose(pt[:, kc * 128:kc * 128 + r], hbb[0:r, kc * 128:(kc + 1) * 128], ident[0:r, 0:r])
                        return last
                    S.op('pe', tf, reads=[f'hc{b % 2}', 'ident'], writes=[tk])
                    act_copy(hTb[:, :, b * 128:b * 128 + r], pt[:].rearrange("p (k t) -> p k t", k=8)[:, :, 0:r], [tk], [hk])

            def ffn_chunks(ti, r0, n, groups, j0, j1):
                hTb = h2T[ti % 2]; hk = f'h2T{ti % 2}'
                lay = []
                off = 0
                for (col0, nseg, seglen, hist) in groups:
                    lay.append(off); off += nseg * (2 + seglen)
                pend = []
                for j in range(j0, j1):
                    if wq:
                        wq.pop(0)()
                    pg, pgk = bank(); pu, puk = bank()
                    cbk = (j * 128) // CB

                    def f(e, pg=pg, pu=pu, j=j):
                        last = None
                        for (ps, c0) in ((pg, j * 128), (pu, DFF + j * 128)):
                            for kc in range(8):
                                last = e.matmul(ps[:, 0:n], lhsT=Wup[:, kc, c0:c0 + 128], rhs=hTb[:, kc, 0:n], start=(kc == 0), stop=(kc == 7))
                        return last
                    S.op('pe', f, reads=[hk, f'Wg{cbk}', f'Wu{cbk}'], writes=[pgk, puk])
                    g_ = gs[j % 2]; gk = f'gs{j % 2}'; ca = cacc[j % 2]; cak = f'cacc{j % 2}'; s_ = sl[j % 2]; sk = f'sl{j % 2}'
                    ub = usb[j % 2]; ubk = f'usb{j % 2}'
                    for gi, (col0, nseg, seglen, hist) in enumerate(groups):
                        base = lay[gi]
                        gv = g_[:, base:base + nseg * (2 + seglen)].rearrange("p (s c) -> p s c", s=nseg)
                        if hist == 'carry':
                            act_copy(gv[:, 0, 0:2], ghist[:, j, :], [f'ghist{j}'], [gk + 'h'])
                        else:
                            act_copy(gv[:, :, 0:2], scfT[:, :, j, :], ['scfT'], [gk + 'h'])
                        act_copy(gv[:, :, 2:2 + seglen], pg[:, col0:col0 + nseg * seglen].rearrange("p (s c) -> p s c", s=nseg), [pgk], [gk])
                        cav = ca[:, col0:col0 + nseg * seglen].rearrange("p (s c) -> p s c", s=nseg)
                        S.op('dve', lambda e, gv=gv, cav=cav, j=j, seglen=seglen: e.tensor_scalar(out=cav, in0=gv[:, :, 0:seglen], scalar1=fparT[:, j, 0:1], scalar2=fparT[:, j, 3:4], op0=ALU.mult, op1=ALU.add),
                             reads=[gk, gk + 'h', 'fparT'], writes=[cak])
                        for k in (1, 2):
                            S.op('dve', lambda e, gv=gv, cav=cav, j=j, k=k, seglen=seglen: e.scalar_tensor_tensor(out=cav, in0=gv[:, :, k:k + seglen], scalar=fparT[:, j, k:k + 1], in1=cav, op0=ALU.mult, op1=ALU.add),
                                 reads=[gk, gk + 'h', 'fparT', cak], writes=[cak])
                        if hist == 'carry' and ti < NTILES2 - 1:
                            S.op('pool', lambda e, gv=gv, j=j, seglen=seglen: e.tensor_copy(ghist[:, j, :], gv[:, 0, seglen:seglen + 2]), reads=[gk], writes=[f'ghist{j}'])
                    if ti == NTILES2 - 1:
                        gv0 = g_[:, lay[0]:lay[0] + 2 + groups[0][2]]
                        S.op('pool', lambda e, gv0=gv0, j=j: e.tensor_copy(gst[:, 0, :, j], gv0[:, groups[0][2]:groups[0][2] + 2]), reads=[gk], writes=['gst'])
                        gv1 = g_[:, lay[1]:lay[1] + NSB * (2 + SL)].rearrange("p (s c) -> p s c", s=NSB)
                        S.op('pool', lambda e, gv1=gv1, j=j: e.tensor_copy(gst[:, 1:1 + NSB, :, j], gv1[:, :, SL:SL + 2]), reads=[gk], writes=['gst'])
                    act_copy(ub[:, 0:n], pu[:, 0:n], [puk], [ubk])

                    def back(ca=ca, s_=s_, ub=ub, j=j, cak=cak, sk=sk, ubk=ubk):
                        S.op('act', lambda e: e.activation(out=s_[:, 0:n], in_=ca[:, 0:n], func=AF.Silu), reads=[cak], writes=[sk])
                        ydst = yTe[ti % 2][:, j, 0:n] if j < JE else yT[:, j, 0:n]
                        S.op(YENG, lambda e: e.tensor_tensor(out=ydst, in0=ub[:, 0:n], in1=s_[:, 0:n], op=ALU.mult), reads=[ubk, sk],
                             writes=[f'yTe{ti % 2}' if j < JE else 'yT'])
                    if pend:
                        pend.pop()()
                    pend.append(back)
                if pend:
                    pend.pop()()

            JE = int(_os.environ.get("KJE", "4"))
            yTe = [sb2(f"yTe{i}", [128, max(JE, 1), NT], BF16) for i in range(2)]
            wdb = {}
            YENG = _os.environ.get("KYENG", "pool")

            def ffn_wdown(ti, r0, n, groups, part):
                nb = (n + 127) // 128
                wdb[ti] = [(bank(), bank()) for b in range(nb)]
                ja, jb = 0, NJ
                for b in range(nb):
                    r = min(128, n - b * 128)
                    (pa, pak), (pb, pbk) = wdb[ti][b]

                    def f(e, pa=pa, pb=pb, b=b, r=r):
                        last = None
                        for (ps, half) in ((pa, 0), (pb, 1)):
                            for j in range(ja, jb):
                                ysrc = yTe[ti % 2] if j < JE else yT
                                last = e.matmul(ps[0:r, :], lhsT=ysrc[:, j, b * 128:b * 128 + r], rhs=Wdn[:, j, half * 512:(half + 1) * 512], start=(j == 0), stop=(j == NJ - 1))
                        return last
                    S.op('pe', f, reads=['yT', f'yTe{ti % 2}'] + [f'Wd{i}' for i in range((NJ + 3) // 4)], writes=[pak, pbk])

            def ffn_tail(ti, r0, n, groups):
                xbuf = xq[ti % 2]; xk = f'xq{ti % 2}'
                nb = (n + 127) // 128
                for b in range(nb):
                    r = min(128, n - b * 128)
                    (pa, pak), (pb, pbk) = wdb[ti][b]
                    S.op('dve', lambda e, pa=pa, b=b, r=r: e.tensor_tensor(out=xbuf[0:r, b, 0:512], in0=pa[0:r, :], in1=xbuf[0:r, b, 0:512], op=ALU.add), reads=[pak, xk], writes=[xk])
                    S.op('dve', lambda e, pb=pb, b=b, r=r: e.tensor_tensor(out=xbuf[0:r, b, 512:1024], in0=pb[0:r, :], in1=xbuf[0:r, b, 512:1024], op=ALU.add), reads=[pbk, xk], writes=[xk])
                    for hf in range(2):
                        S.op('act', lambda e, b=b, r=r, hf=hf: e.activation(out=junk[0:r, :], in_=xbuf[0:r, b, hf * 512:(hf + 1) * 512], func=AF.Square,
                                                                            accum_out=f_ss[0:r, 2 * b + hf:2 * b + hf + 1]), reads=[xk], writes=['f_ss'])
                    S.op('dve', lambda e, b=b, r=r: e.tensor_tensor(out=f_sd[0:r, b:b + 1], in0=f_ss[0:r, 2 * b:2 * b + 1], in1=f_ss[0:r, 2 * b + 1:2 * b + 2], op=ALU.add), reads=['f_ss'], writes=['f_sd'])
                    S.op('dve', lambda e, b=b, r=r: e.tensor_scalar(out=f_sd[0:r, 2 + b:3 + b], in0=f_sd[0:r, b:b + 1], scalar1=1.0 / D, scalar2=EPS, op0=ALU.mult, op1=ALU.add), reads=['f_sd'], writes=['f_sd2'])
                    S.op('pool', lambda e, b=b, r=r: e.tensor_tensor(out=f_r[0:r, b:b + 1], in0=f_sd[0:r, 2 + b:3 + b], in1=mhalf[0:r, :], op=ALU.pow), reads=['f_sd2', 'mhalf'], writes=['f_r'])
                    S.op('dve', lambda e, b=b, r=r: e.scalar_tensor_tensor(out=xbuf[0:r, b, :], in0=xbuf[0:r, b, :], scalar=f_r[0:r, b:b + 1], in1=gfin_b[0:r, :], op0=ALU.mult, op1=ALU.mult),
                         reads=[xk, 'f_r', 'gb2'], writes=[xk])
                    outs = out_rows(r0 + b * 128, r)
                    if outs:
                        S.dma('sp', f'o_y{ti % 2}', [lambda e, p0=p0, cnt=cnt, dst=dst, b=b: e.dma_start(out=dst, in_=xbuf[p0:p0 + cnt, b, :]) for (p0, cnt, dst) in outs], reads=[xk])

            NFULL = (NMETA + SEQ) // NT
            NTILES2 = NFULL + 1
            tail = NMETA + SEQ - NFULL * NT
            tiles2 = [(t, t * NT, NT, [(0, 1, NT, 'carry')]) for t in range(NFULL if lvl >= 9 else (1 if lvl >= 7 else 0))]
            if lvl >= 9:
                tiles2.append((NFULL, NFULL * NT, tail + NSB * SL, [(0, 1, tail, 'carry'), (tail, NSB, SL, 'state')]))
            JH = int(_os.environ.get("KJH", "8"))
            JHB = int(_os.environ.get("KJHB", "4"))
            if tiles2:
                ffn_head(*tiles2[0])
                ffn_head_b(*tiles2[0])
            for i_, tl in enumerate(tiles2):
                nx = tiles2[i_ + 1] if i_ + 1 < len(tiles2) else None
                ffn_chunks(*tl, JE if i_ > 0 else 0, JH)
                if i_ == 0:
                    load_scf()
                if nx:
                    ffn_head(*nx)
                ffn_chunks(*tl, JH, JH + JHB)
                if nx:
                    ffn_head_b(*nx)
                ffn_chunks(*tl, JH + JHB, NJ)
                if nx:
                    ffn_chunks(*nx, 0, JE)
                ffn_wdown(*tl, 0)
                ffn_tail(*tl)
            for g in range(1 + NSB if lvl >= 9 else 0):
                ps, pk = bank()
                S.op('pe', lambda e, ps=ps, g=g: e.transpose(ps[0:2 * NJ, 0:128], gst[:, g, :, :].rearrange("p t j -> p (t j)"), identf[:]), reads=['gst', 'identf'], writes=[pk])
                S.op('dve', lambda e, ps=ps: e.tensor_copy(gso[0:2 * NJ, :], ps[0:2 * NJ, 0:128]), reads=[pk], writes=['gso'])
                dst = cfp if g == 0 else cfs[g - 1]
                S.dma('sp', 'o_gst', [lambda e, t=t, dst=dst: e.dma_start(out=dst[t, :].rearrange("(j p) -> j p", p=128), in_=gso[t * NJ:(t + 1) * NJ, :]) for t in range(2)], reads=['gso'])
            S.finish('sp')
            S.emit(block2)
    return nc


def _rel_bucket_np(rel):
    nb = 16
    max_exact = 8
    ret = np.where(rel > 0, nb, 0)
    n = np.abs(rel)
    nf = np.maximum(n, 1).astype(np.float32)
    large = max_exact + (np.log(nf / np.float32(max_exact)) / np.float32(math.log(256 / max_exact)) * np.float32(nb - max_exact)).astype(np.int32)
    large = np.minimum(large, nb - 1)
    return ret + np.where(n < max_exact, n, large)


_CACHE = {}


def kernel(x_prompt, x_sample, cache_k_meta, cache_v_meta, cache_k_win, cache_v_win, state_conv_a, state_conv_ffn,
           meta_tokens, rel_bias_table, norm_mix, w_in, conv_dw_w, conv_dw_b, conv_ln_g, conv_ln_b, attn_sinks, w_out,
           norm_ffn, w_up, ffn_dw_w, ffn_dw_b, w_down, norm_final):
    f = lambda a: np.ascontiguousarray(np.asarray(a, dtype=np.float32))
    x_prompt = f(x_prompt); x_sample = f(x_sample)
    w_in0 = f(w_in)[0]; w_out0 = f(w_out)[0]
    qcols = []
    for j in range(4):
        qcols += list(range(1024 + 64 * j, 1024 + 64 * j + 64)) + list(range(1024 + 64 * (j + 4), 1024 + 64 * (j + 4) + 64))
    cols = list(range(1024)) + qcols + list(range(1536, 1792))
    win_p = np.ascontiguousarray(w_in0[:, cols])
    rows = list(range(512))
    for j in range(4):
        rows += list(range(512 + 64 * j, 512 + 64 * j + 64)) + list(range(512 + 64 * (j + 4), 512 + 64 * (j + 4) + 64))
    wout_p = np.ascontiguousarray(w_out0[rows, :])
    cpar = np.ascontiguousarray(np.concatenate([f(conv_dw_w)[0], f(conv_dw_b), f(conv_ln_g), f(conv_ln_b)], axis=0))
    fpar = np.ascontiguousarray(np.concatenate([f(ffn_dw_w)[0], f(ffn_dw_b)], axis=0))
    rel = np.arange(E_LEN, dtype=np.int32) - E_OFF
    bkt = _rel_bucket_np(rel)
    eoh = (bkt[None, :] == np.arange(32)[:, None]).astype(np.float32)
    shared = {
        "meta": f(meta_tokens), "win": win_p, "wout": wout_p, "wup": f(w_up)[0], "wdn": f(w_down)[0],
        "gmix": f(norm_mix)[0], "gffn": f(norm_ffn)[0], "gfin": f(norm_final), "cpar": cpar, "fpar": fpar,
        "sinks": f(attn_sinks)[0], "tab": f(rel_bias_table), "eoh": eoh,
    }
    ckm = f(cache_k_meta)[0].reshape(32, NMETA, 128); cvm = f(cache_v_meta)[0].reshape(32, NMETA, 128)
    ckw = f(cache_k_win)[0].reshape(32, 128, 128); cvw = f(cache_v_win)[0].reshape(32, 128, 128)
    sca = f(state_conv_a)[0]; scf = f(state_conv_ffn)[0]
    in_maps = []
    for c in range(8):
        m = dict(shared)
        sl_ = slice(NSB * c, NSB * (c + 1))
        m.update({"xp": x_prompt[c], "xs": np.ascontiguousarray(x_sample[sl_].reshape(NSB * SL, D)),
                  "ckm": ckm[sl_], "cvm": cvm[sl_], "ckw": ckw[sl_], "cvw": cvw[sl_], "sca": sca[sl_], "scf": scf[sl_]})
        in_maps.append({k: np.ascontiguousarray(v) for k, v in m.items()})
    if _CACHE.get("maps_only"):
        return in_maps
    if "nc" not in _CACHE:
        _CACHE["nc"] = build_program()
    res = run_bass_kernel_spmd(_CACHE["nc"], in_maps, core_ids=list(range(8)))
    R = res.results
    cat = lambda k: np.stack([np.asarray(R[c][k], dtype=np.float32) for c in range(8)], axis=0)
    y_prompt = cat("yp")
    y_sample = cat("ys").reshape(32, SL, D)
    k_meta_p = cat("kmp").reshape(1, 8, NMETA, 2, 64); v_meta_p = cat("vmp").reshape(1, 8, NMETA, 2, 64)
    k_win_p = cat("kwp").reshape(1, 8, 128, 2, 64); v_win_p = cat("vwp").reshape(1, 8, 128, 2, 64)
    conv_a_p = cat("cap").reshape(1, 8, CW - 1, DCONV); conv_ffn_p = cat("cfp").reshape(1, 8, 2, DFF)
    k_win_s = cat("kws").reshape(1, 32, 128, 2, 64); v_win_s = cat("vws").reshape(1, 32, 128, 2, 64)
    conv_a_s = cat("cas").reshape(1, 32, CW - 1, DCONV); conv_ffn_s = cat("cfs").reshape(1, 32, 2, DFF)
    return (y_prompt, y_sample, k_meta_p, v_meta_p, k_win_p, v_win_p, conv_a_p, conv_ffn_p,
            k_win_s, v_win_s, conv_a_s, conv_ffn_s)
```
